# Optimizing a Trainium2 kernel written in Bass

```python
import math
import jax, jax.numpy as jnp
from jax import lax
import numpy as np

D_MODEL = 1024
BATCH = 8
SEQ = 2048
DEPTH = 2

GRID_W = 64
CTX_LEN = 256
HEAD_DIM = 64
ROT_PER_AXIS = HEAD_DIM // 2
ROPE_THETA = 10000.0
GDN_HEADS = 8
GDN_DK = 64
GDN_DV = 64
GDN_CONV = 5
GDN_CHUNK = 64
GA_HEADS = 4
GA_KV = 2
WA_HEADS = 4
WA_KV = 2
WINDOW = 128
Q_BLOCK = 128
D_FF = 4 * D_MODEL
N_MOD = 6
RMS_EPS = 1e-6
A_QKV = GDN_HEADS * (2 * GDN_DK + GDN_DV)
A_Z = GDN_HEADS * GDN_DV
A_GATES = 2 * GDN_HEADS
B_QKV = (GA_HEADS + 2 * GA_KV) * HEAD_DIM
C_QKV = (WA_HEADS + 2 * WA_KV) * HEAD_DIM
D_IN = A_QKV + A_Z + 2 * A_GATES + B_QKV + C_QKV
D_MIX = GDN_HEADS * GDN_DV + GA_HEADS * HEAD_DIM + WA_HEADS * HEAD_DIM

kernel_name = "hybrid_gdn_gqa_swa_flow_block"


def rms_norm(x, g):
    xf = x.astype(jnp.float32)
    y = xf * lax.rsqrt(jnp.mean(jnp.square(xf), axis=-1, keepdims=True) + RMS_EPS)
    return (y * g.astype(jnp.float32)).astype(x.dtype)


def modulate(x, g, shift, scale):
    return rms_norm(x, g) * (1 + scale) + shift


def ada_mod(cond, w_mod, b_mod):
    return jnp.split(jax.nn.silu(cond) @ w_mod + b_mod, N_MOD, axis=-1)


def split_cols(p):
    sizes = (A_QKV, A_Z, A_GATES, A_GATES, B_QKV, C_QKV)
    offsets = [int(o) for o in np.cumsum(sizes)[:-1]]
    return jnp.split(p, offsets, axis=-1)


def l2_normalize(x):
    return x * lax.rsqrt(jnp.sum(jnp.square(x), axis=-1, keepdims=True) + 1e-6)


def short_conv(x, w):
    pad = GDN_CONV // 2
    t = x.shape[1]
    xp = jnp.pad(x, ((0, 0), (pad, pad), (0, 0)))
    return sum(xp[:, j:j + t] * w[j] for j in range(GDN_CONV))


def axial_rope_tables(rows, dtype):
    row = jnp.repeat(jnp.arange(rows, dtype=jnp.float32), GRID_W)
    col = jnp.tile(jnp.arange(GRID_W, dtype=jnp.float32), rows)
    half = ROT_PER_AXIS // 2
    inv_freq = ROPE_THETA ** (-jnp.arange(half, dtype=jnp.float32) / half)
    ang_r = row[:, None] * inv_freq
    ang_c = col[:, None] * inv_freq
    return tuple(a[:, None, :].astype(dtype) for a in
                 (jnp.cos(ang_r), jnp.sin(ang_r), jnp.cos(ang_c), jnp.sin(ang_c)))


def _rotate(x, cos, sin):
    x1, x2 = jnp.split(x, 2, axis=-1)
    return jnp.concatenate([x1 * cos - x2 * sin, x2 * cos + x1 * sin], axis=-1)


def apply_axial_rope(x, rope):
    cos_r, sin_r, cos_c, sin_c = rope
    return jnp.concatenate([_rotate(x[..., :ROT_PER_AXIS], cos_r, sin_r),
                            _rotate(x[..., ROT_PER_AXIS:], cos_c, sin_c)], axis=-1)


def gated_delta_chunked(q, k, v, g, beta, s0):
    b, t, h, dk = k.shape
    dv = v.shape[-1]
    n = t // GDN_CHUNK

    def chunks(a):
        a = a.reshape((b, n, GDN_CHUNK, h) + a.shape[3:])
        return jnp.moveaxis(a, (1, 3), (0, 2))

    kc, vc, gc, bc = chunks(k), chunks(v), chunks(g), chunks(beta)
    gcum = jnp.cumsum(gc, axis=-1)
    idx = jnp.arange(GDN_CHUNK)
    incl = idx[:, None] >= idx[None, :]
    decay = jnp.exp(jnp.where(incl, gcum[..., :, None] - gcum[..., None, :], -jnp.inf))
    kb = kc * bc[..., None]
    low = jnp.where(idx[:, None] > idx[None, :], jnp.einsum("nbhid,nbhjd->nbhij", kb, kc) * decay, 0.0)
    eye = jnp.eye(GDN_CHUNK, dtype=low.dtype)
    rhs = jnp.concatenate([vc * bc[..., None], kb * jnp.exp(gcum)[..., None]], axis=-1)
    sol = lax.linalg.triangular_solve(eye + low, rhs, left_side=True, lower=True, unit_diagonal=True)
    u, w = sol[..., :dv], sol[..., dv:]
    g_last = gcum[..., -1]
    k_end = kc * jnp.exp(g_last[..., None] - gcum)[..., None]

    def update(s, u_i, w_i, ke_i, gl_i):
        v_new = u_i - jnp.einsum("bhck,bhkv->bhcv", w_i, s)
        s_new = s * jnp.exp(gl_i)[..., None, None] + jnp.einsum("bhck,bhcv->bhkv", ke_i, v_new)
        return s_new, v_new

    if q is None:
        def step_state(s, xs):
            s_new, _ = update(s, *xs)
            return s_new, None
        s_fin, _ = lax.scan(step_state, s0, (u, w, k_end, g_last))
        return None, s_fin

    qc = chunks(q)
    intra = jnp.einsum("nbhid,nbhjd->nbhij", qc, kc) * decay
    q_dec = qc * jnp.exp(gcum)[..., None]

    def step(s, xs):
        u_i, w_i, ke_i, gl_i, q_i, a_i = xs
        s_new, v_new = update(s, u_i, w_i, ke_i, gl_i)
        o_i = jnp.einsum("bhck,bhkv->bhcv", q_i, s) + jnp.einsum("bhij,bhjv->bhiv", a_i, v_new)
        return s_new, o_i

    s_fin, o = lax.scan(step, s0, (u, w, k_end, g_last, q_dec, intra))
    return jnp.moveaxis(o, (0, 2), (1, 3)).reshape(b, t, h, dv), s_fin


def gdn_heads(qkv, beta_raw, alpha_raw, conv_w, a_log, dt_bias, with_q):
    b, t, _ = qkv.shape
    f32 = jnp.float32
    qkv = jax.nn.silu(short_conv(qkv, conv_w)).astype(f32)
    q, k, v = jnp.split(qkv, [GDN_HEADS * GDN_DK, 2 * GDN_HEADS * GDN_DK], axis=-1)
    k = l2_normalize(k.reshape(b, t, GDN_HEADS, GDN_DK))
    v = v.reshape(b, t, GDN_HEADS, GDN_DV)
    q = l2_normalize(q.reshape(b, t, GDN_HEADS, GDN_DK)) * GDN_DK ** -0.5 if with_q else None
    beta = jax.nn.sigmoid(beta_raw.astype(f32)).reshape(b, t, 2, GDN_HEADS)
    g = -jnp.exp(a_log.astype(f32)) * jax.nn.softplus(
        alpha_raw.astype(f32).reshape(b, t, 2, GDN_HEADS) + dt_bias.astype(f32))
    return q, k, v, g, beta


def direction_inputs(heads, d):
    q, k, v, g, beta = heads
    f = lambda a: None if a is None else (jnp.flip(a, axis=1) if d == 1 else a)
    return f(q), f(k), f(v), f(g[:, :, d]), f(beta[:, :, d])


def gated_out_norm(o, z, gain):
    b, t, h, dv = o.shape
    y = rms_norm(o, gain) * jax.nn.silu(z.reshape(b, t, h, dv).astype(jnp.float32))
    return y.reshape(b, t, h * dv).astype(z.dtype)


def gdn_mixer(a_qkv, a_z, a_beta, a_alpha, ca_qkv, ca_z, ca_beta, ca_alpha,
              conv_w, a_log, dt_bias, norm_g, need_ctx_out):
    lat = gdn_heads(a_qkv, a_beta, a_alpha, conv_w, a_log, dt_bias, True)
    ctxh = gdn_heads(ca_qkv, ca_beta, ca_alpha, conv_w, a_log, dt_bias, need_ctx_out)
    s0 = jnp.zeros((a_qkv.shape[0], GDN_HEADS, GDN_DK, GDN_DV), jnp.float32)
    o_lat, o_ctx = None, None
    for d in range(2):
        oc, sc = gated_delta_chunked(*direction_inputs(ctxh, d), s0)
        ol, _ = gated_delta_chunked(*direction_inputs(lat, d), sc)
        ol = jnp.flip(ol, axis=1) if d == 1 else ol
        o_lat = ol if o_lat is None else o_lat + ol
        if need_ctx_out:
            oc = jnp.flip(oc, axis=1) if d == 1 else oc
            o_ctx = oc if o_ctx is None else o_ctx + oc
    out = gated_out_norm(o_lat, a_z, norm_g)
    out_c = gated_out_norm(o_ctx, ca_z, norm_g) if need_ctx_out else None
    return out, out_c


def attn_heads(p, n_q, n_kv, qg, kg, rope, with_q):
    b, t, _ = p.shape
    q, k, v = jnp.split(p, [n_q * HEAD_DIM, (n_q + n_kv) * HEAD_DIM], axis=-1)
    k = rms_norm(k.reshape(b, t, n_kv, HEAD_DIM), kg)
    v = v.reshape(b, t, n_kv, HEAD_DIM)
    q = rms_norm(q.reshape(b, t, n_q, HEAD_DIM), qg) if with_q else None
    if rope is not None:
        k = apply_axial_rope(k, rope)
        q = apply_axial_rope(q, rope)
    return q, k, v


def global_gqa(q, k, v, qc, kc, vc):
    b, t, hq, dh = q.shape
    hkv = k.shape[2]
    grp = hq // hkv
    scale = dh ** -0.5
    k_all = jnp.concatenate([k, kc], axis=1)
    v_all = jnp.concatenate([v, vc], axis=1)

    def block(qi):
        s = jnp.einsum("bqhgd,bkhd->bhgqk", qi, k_all).astype(jnp.float32) * scale
        p = jax.nn.softmax(s, axis=-1).astype(v_all.dtype)
        return jnp.einsum("bhgqk,bkhd->bqhgd", p, v_all)

    nb = t // Q_BLOCK
    qb = jnp.moveaxis(q.reshape(b, nb, Q_BLOCK, hkv, grp, dh), 1, 0)
    o = jnp.moveaxis(lax.map(block, qb), 0, 1).reshape(b, t, hq * dh)
    o_ctx = None
    if qc is not None:
        lc = qc.shape[1]
        s = jnp.einsum("bqhgd,bkhd->bhgqk", qc.reshape(b, lc, hkv, grp, dh), kc).astype(jnp.float32) * scale
        p = jax.nn.softmax(s, axis=-1).astype(vc.dtype)
        o_ctx = jnp.einsum("bhgqk,bkhd->bqhgd", p, vc).reshape(b, lc, hq * dh)
    return o, o_ctx


def window_gqa(q, k, v, qc, kc, vc, sink):
    b, t, hq, dh = q.shape
    hkv = k.shape[2]
    grp = hq // hkv
    scale = dh ** -0.5
    f32 = jnp.float32
    nb = t // Q_BLOCK
    wb = WINDOW // Q_BLOCK
    nkb = 2 * wb + 1
    pad = ((0, 0), (wb * Q_BLOCK, wb * Q_BLOCK), (0, 0), (0, 0))

    def band(a):
        ap = jnp.pad(a, pad).reshape(b, nb + 2 * wb, Q_BLOCK, hkv, dh)
        return jnp.concatenate([ap[:, j:j + nb] for j in range(nkb)], axis=2)

    kw, vw = band(k), band(v)
    qi = jnp.arange(Q_BLOCK)
    kj = jnp.arange(nkb * Q_BLOCK) - wb * Q_BLOCK
    kpos = jnp.arange(nb)[:, None] * Q_BLOCK + kj[None, :]
    valid = (jnp.abs(kj[None, :] - qi[:, None]) <= WINDOW)[None] & ((kpos >= 0) & (kpos < t))[:, None, :]
    qb = q.reshape(b, nb, Q_BLOCK, hkv, grp, dh)
    s_win = jnp.einsum("bnqhgd,bnkhd->bnhgqk", qb, kw).astype(f32) * scale
    s_win = jnp.where(valid[None, :, None, None], s_win, -jnp.inf)
    s_ctx = jnp.einsum("bnqhgd,bchd->bnhgqc", qb, kc).astype(f32) * scale
    sink_col = jnp.broadcast_to(sink.astype(f32).reshape(hkv, grp, 1), s_win.shape[:-1])[..., None]
    p = jax.nn.softmax(jnp.concatenate([s_win, s_ctx, sink_col], axis=-1), axis=-1).astype(v.dtype)
    kwn = kw.shape[2]
    lc = kc.shape[1]
    o = (jnp.einsum("bnhgqk,bnkhd->bnqhgd", p[..., :kwn], vw)
         + jnp.einsum("bnhgqc,bchd->bnqhgd", p[..., kwn:kwn + lc], vc)).reshape(b, t, hq * dh)
    o_ctx = None
    if qc is not None:
        s = jnp.einsum("bqhgd,bkhd->bhgqk", qc.reshape(b, lc, hkv, grp, dh), kc).astype(f32) * scale
        sc = jnp.broadcast_to(sink.astype(f32).reshape(hkv, grp, 1), s.shape[:-1])[..., None]
        pc = jax.nn.softmax(jnp.concatenate([s, sc], axis=-1), axis=-1).astype(vc.dtype)
        o_ctx = jnp.einsum("bhgqk,bkhd->bqhgd", pc[..., :lc], vc).reshape(b, lc, hq * dh)
    return o, o_ctx


def sq_relu_mlp(h, w1, w2):
    return jnp.square(jax.nn.relu(h @ w1)) @ w2


def hybrid_layer(x, cx, cond, cond_ctx, rope, w_mod, b_mod, g_attn, w_in, gdn_conv_w, gdn_a_log,
                 gdn_dt_bias, gdn_norm_g, ga_q_norm_g, ga_k_norm_g, wa_q_norm_g, wa_k_norm_g, wa_sink,
                 w_out, g_mlp, w_mlp_in, w_mlp_out, need_ctx_out):
    sh_a, sc_a, gt_a, sh_m, sc_m, gt_m = ada_mod(cond, w_mod, b_mod)
    csh_a, csc_a, cgt_a, csh_m, csc_m, cgt_m = ada_mod(cond_ctx, w_mod, b_mod)
    a_qkv, a_z, a_beta, a_alpha, b_qkv, c_qkv = split_cols(modulate(x, g_attn, sh_a, sc_a) @ w_in)
    ca_qkv, ca_z, ca_beta, ca_alpha, cb_qkv, cc_qkv = split_cols(modulate(cx, g_attn, csh_a, csc_a) @ w_in)
    o_a, oc_a = gdn_mixer(a_qkv, a_z, a_beta, a_alpha, ca_qkv, ca_z, ca_beta, ca_alpha,
                          gdn_conv_w, gdn_a_log, gdn_dt_bias, gdn_norm_g, need_ctx_out)
    q, k, v = attn_heads(b_qkv, GA_HEADS, GA_KV, ga_q_norm_g, ga_k_norm_g, rope, True)
    qc, kc, vc = attn_heads(cb_qkv, GA_HEADS, GA_KV, ga_q_norm_g, ga_k_norm_g, None, need_ctx_out)
    o_b, oc_b = global_gqa(q, k, v, qc, kc, vc)
    q, k, v = attn_heads(c_qkv, WA_HEADS, WA_KV, wa_q_norm_g, wa_k_norm_g, rope, True)
    qc, kc, vc = attn_heads(cc_qkv, WA_HEADS, WA_KV, wa_q_norm_g, wa_k_norm_g, None, need_ctx_out)
    o_c, oc_c = window_gqa(q, k, v, qc, kc, vc, wa_sink)
    x = x + gt_a * (jnp.concatenate([o_a, o_b, o_c], axis=-1) @ w_out)
    x = x + gt_m * sq_relu_mlp(modulate(x, g_mlp, sh_m, sc_m), w_mlp_in, w_mlp_out)
    if need_ctx_out:
        cx = cx + cgt_a * (jnp.concatenate([oc_a, oc_b, oc_c], axis=-1) @ w_out)
        cx = cx + cgt_m * sq_relu_mlp(modulate(cx, g_mlp, csh_m, csc_m), w_mlp_in, w_mlp_out)
    return x, cx


def setup_inputs(seed: int = 0) -> dict:
    key = jax.random.key(seed)
    ks = jax.random.split(key, 24)
    f32 = jnp.float32
    L = DEPTH

    def nrm(k, shape, scale):
        return jax.random.normal(k, shape, f32) * scale

    def gain(k, shape):
        return 1.0 + 0.02 * jax.random.normal(k, shape, f32)

    dt = jnp.exp(jax.random.uniform(ks[10], (L, 2, GDN_HEADS), f32, math.log(1e-3), math.log(1e-1)))
    return {
        "x": nrm(ks[0], (BATCH, SEQ, D_MODEL), 1.0),
        "c": nrm(ks[1], (BATCH, D_MODEL), 1.0),
        "ctx": nrm(ks[2], (BATCH, CTX_LEN, D_MODEL), 1.0),
        "c_ctx": nrm(ks[3], (D_MODEL,), 1.0),
        "w_mod": nrm(ks[4], (L, D_MODEL, N_MOD * D_MODEL), 0.5 * D_MODEL ** -0.5),
        "b_mod": nrm(ks[5], (L, N_MOD * D_MODEL), 0.01),
        "g_attn": gain(ks[6], (L, D_MODEL)),
        "w_in": nrm(ks[7], (L, D_MODEL, D_IN), D_MODEL ** -0.5),
        "gdn_conv_w": nrm(ks[8], (L, GDN_CONV, A_QKV), GDN_CONV ** -0.5),
        "gdn_a_log": jnp.log(jax.random.uniform(ks[9], (L, 2, GDN_HEADS), f32, 1.0, 16.0)),
        "gdn_dt_bias": dt + jnp.log(-jnp.expm1(-dt)),
        "gdn_norm_g": gain(ks[11], (L, GDN_DV)),
        "ga_q_norm_g": gain(ks[12], (L, HEAD_DIM)),
        "ga_k_norm_g": gain(ks[13], (L, HEAD_DIM)),
        "wa_q_norm_g": gain(ks[14], (L, HEAD_DIM)),
        "wa_k_norm_g": gain(ks[15], (L, HEAD_DIM)),
        "wa_sink": nrm(ks[16], (L, WA_HEADS), 1.0),
        "w_out": nrm(ks[17], (L, D_MIX, D_MODEL), D_MIX ** -0.5),
        "g_mlp": gain(ks[18], (L, D_MODEL)),
        "w_mlp_in": nrm(ks[19], (L, D_MODEL, D_FF), D_MODEL ** -0.5),
        "w_mlp_out": nrm(ks[20], (L, D_FF, D_MODEL), D_FF ** -0.5),
    }


def reference(x, c, ctx, c_ctx, w_mod, b_mod, g_attn, w_in, gdn_conv_w, gdn_a_log, gdn_dt_bias,
              gdn_norm_g, ga_q_norm_g, ga_k_norm_g, wa_q_norm_g, wa_k_norm_g, wa_sink, w_out, g_mlp,
              w_mlp_in, w_mlp_out):
    t = x.shape[1]
    rows = t // GRID_W
    rope = axial_rope_tables(rows, x.dtype)
    cond = c[:, None, :]
    cond_ctx = c_ctx[None, None, :]
    cx = ctx
    for l in range(DEPTH):
        x, cx = hybrid_layer(x, cx, cond, cond_ctx, rope, w_mod[l], b_mod[l], g_attn[l], w_in[l],
                             gdn_conv_w[l], gdn_a_log[l], gdn_dt_bias[l], gdn_norm_g[l],
                             ga_q_norm_g[l], ga_k_norm_g[l], wa_q_norm_g[l], wa_k_norm_g[l], wa_sink[l],
                             w_out[l], g_mlp[l], w_mlp_in[l], w_mlp_out[l], l < DEPTH - 1)
    return x
```

```python
from concourse.bass_utils import run_bass_kernel_spmd
import numpy as np
import concourse.bass as bass
import concourse.mybir as mybir

F32 = mybir.dt.float32
BF16 = mybir.dt.bfloat16
F32R = mybir.dt.float32r
ALU = mybir.AluOpType
AF = mybir.ActivationFunctionType
AX = mybir.AxisListType

_ESZ = {}


def _esize(dt):
    if dt not in _ESZ:
        _ESZ[dt] = mybir.dt.size(dt) if hasattr(mybir.dt, "size") else None
    return _ESZ[dt]


def esize(dt):
    s = str(dt)
    if "float32" in s or "int32" in s:
        return 4
    if "bfloat16" in s or "float16" in s or "int16" in s:
        return 2
    if "int8" in s or "float8" in s:
        return 1
    raise ValueError(s)


def region(ap):
    pat = ap.ap
    es = esize(ap.dtype)
    pstep, pcnt = pat[0]
    off = ap.offset
    if pstep == 0:
        p0 = 0
        f0 = off
    else:
        p0 = off // pstep
        f0 = off % pstep
    ext = 1
    for st, cn in pat[1:]:
        ext += (cn - 1) * abs(st)
    b0, b1 = f0 * es, (f0 + ext) * es
    if ap.tensor.name == "ps":
        b0 = (b0 // 2048) * 2048
        b1 = ((b1 + 2047) // 2048) * 2048
        return (ap.tensor.name, 0, 128, b0, b1)
    return (ap.tensor.name, p0, p0 + pcnt, b0, b1)


def _overlap(a, b):
    return a[1] < b[2] and b[1] < a[2] and a[3] < b[4] and b[3] < a[4]


def _covers(a, b):
    return a[1] <= b[1] and a[2] >= b[2] and a[3] <= b[3] and a[4] >= b[4]


STRICT_SAME_ENGINE = False


class Prog:
    ENGS = ("pe", "act", "dve", "pool", "sp")

    def __init__(self, nc):
        self.nc = nc
        self.ops = []
        self.eng_ops = {e: [] for e in self.ENGS}
        self.track = {}
        self.lane_last = {}
        self.lane_cnt = {}

    def add(self, eng, fn, reads=(), writes=(), lane=None, extra=()):
        idx = len(self.ops)
        deps = set((d, 'raw') for d in extra)
        rregs = [region(a) for a in reads]
        wregs = [region(a) for a in writes]
        for r in rregs:
            st = self.track.setdefault(r[0], {"w": [], "r": []})
            for (wr, wop) in st["w"]:
                if _overlap(r, wr):
                    deps.add((wop, "raw"))
        for w in wregs:
            st = self.track.setdefault(w[0], {"w": [], "r": []})
            for (wr, wop) in st["w"]:
                if _overlap(w, wr):
                    deps.add((wop, "waw"))
            for (rr, rop) in st["r"]:
                if _overlap(w, rr):
                    deps.add((rop, "war"))
        for w in wregs:
            st = self.track[w[0]]
            st["w"] = [(wr, wop) for (wr, wop) in st["w"] if not _covers(w, wr)]
            st["r"] = [(rr, rop) for (rr, rop) in st["r"] if not _covers(w, rr)]
            st["w"].append((w, idx))
        for r in rregs:
            st = self.track[r[0]]
            st["r"] = [(rr, rop) for (rr, rop) in st["r"]
                       if rop == idx or not (self.ops[rop]["eng"] == eng and self.ops[rop]["lane"] is None
                               and lane is None and _covers(r, rr))]
            st["r"].append((r, idx))
        if lane is not None:
            if lane in self.lane_last:
                deps.add((self.lane_last[lane], "lane"))
            self.lane_last[lane] = idx
            self.lane_cnt[lane] = self.lane_cnt.get(lane, 0) + 1
        op = dict(eng=eng, fn=fn, deps=deps, lane=lane, idx=idx,
                  pos=len(self.eng_ops[eng]) + 1,
                  lpos=self.lane_cnt.get(lane, 0) if lane is not None else 0)
        self.ops.append(op)
        self.eng_ops[eng].append(idx)
        return idx

    def finalize(self, stack):
        nc = self.nc
        ops = self.ops
        def dom(o):
            return ("L", o["lane"]) if o["lane"] is not None else ("E", o["eng"])

        def dpos(o):
            return o["lpos"] if o["lane"] is not None else o["pos"]

        clk = {e: {} for e in self.ENGS}
        vcs = [None] * len(ops)
        waits = [None] * len(ops)
        signal = [False] * len(ops)
        for o in ops:
            e = o["eng"]
            c = clk[e]
            ws = []
            for (d, kind) in sorted(o["deps"], key=lambda t: -t[0]):
                od = ops[d]
                if od["lane"] is None and od["eng"] == e:
                    if e == "pe" or e == "sp" or (kind != "raw" and not STRICT_SAME_ENGINE):
                        continue
                dd, dp = dom(od), dpos(od)
                if c.get(dd, 0) >= dp:
                    continue
                ws.append(d)
                for k, v in vcs[d].items():
                    if c.get(k, 0) < v:
                        c[k] = v
            waits[o["idx"]] = ws
            for d in ws:
                signal[d] = True
            vc = dict(c)
            vc[dom(o)] = dpos(o)
            vcs[o["idx"]] = vc
        import os as _os
        LIM = int(_os.environ.get("SEMLIM", 1000))
        LIML = 60
        sem = {}
        cnt = {}
        skey = [None] * len(ops)
        sval = [0] * len(ops)

        def getsem(key):
            if key not in sem:
                sem[key] = stack.enter_context(nc.semaphore("s%d" % len(sem)))
            return sem[key]

        for o in ops:
            d = dom(o)
            if o["lane"] is not None:
                cnt[d] = cnt.get(d, 0) + 1
                ep = (cnt[d] - 1) // LIML
                skey[o["idx"]] = (d, ep)
                sval[o["idx"]] = ((cnt[d] - 1) % LIML + 1) * 16
                getsem((d, ep))
            elif signal[o["idx"]]:
                cnt[d] = cnt.get(d, 0) + 1
                ep = (cnt[d] - 1) // LIM
                skey[o["idx"]] = (d, ep)
                sval[o["idx"]] = (cnt[d] - 1) % LIM + 1
                getsem((d, ep))
        self.n_sem = len(sem)
        self.n_wait = sum(len(w) for w in waits)
        block = stack.enter_context(nc.Block())
        engs = {"pe": block.tensor, "act": block.scalar, "dve": block.vector,
                "pool": block.gpsimd, "sp": block.sync}

        def make(e):
            def body(eng):
                for i in self.eng_ops[e]:
                    o = ops[i]
                    for d in waits[i]:
                        eng.wait_ge(sem[skey[d]], sval[d])
                    if o["fn"] is None:
                        continue
                    ins = o["fn"](eng)
                    if o["lane"] is not None:
                        ins.then_inc(sem[skey[i]], 16)
                    elif signal[i]:
                        ins.then_inc(sem[skey[i]], 1)
            return body

        for e in self.ENGS:
            if self.eng_ops[e]:
                engs[e](make(e))
from contextlib import ExitStack

D = 1024; T = 2048; CL = 256; NT = 2304; NTILE = 18; DFF = 4096; DIN = 3104; L = 2
BLOCKS = [(0, 256)] + [(256 + 512 * i, 512) for i in range(4)]
BASE_B = 2080; BASE_C = 2592
ORDER = [list(range(18)), [1, 0] + list(range(17, 1, -1))]


def make_consts():
    k = np.arange(128)[:, None]; j = np.arange(128)[None, :]
    ident = (k == j)
    A_f = (k > j); A_b = (k < j); B_f = (k <= j); B_b = (k >= j)
    Minc_f = (j >= k); Minc_b = (j <= k); Mstr_f = (j > k); Mstr_b = (j < k)
    blk = (k // 64 == j // 64)
    ones = np.ones((128, 128))
    R = np.zeros((128, 128))
    for d in range(128):
        loc = d % 32
        if loc < 16:
            R[d + 16, d] = -1.0
        else:
            R[d - 16, d] = 1.0
    b32 = (k // 32 == j // 32); b64 = (k // 64 == j // 64)
    cols = [ident, A_f, A_b, B_f, B_b, Minc_f, Minc_f, Minc_b, Minc_b, Mstr_f, Mstr_f, Mstr_b, Mstr_b,
            blk, ones, R, ident, ident, ident, ident, b32, b64 & ~b32, ~b64]
    return np.concatenate([np.asarray(c, np.float32) for c in cols], axis=1)


C_ID = 0; C_AF = 128; C_AB = 256; C_BF = 384; C_BB = 512; C_MINC = 640; C_MSTR = 1152
C_BLK = 1664; C_ONES = 1792; C_R = 1920; C_ID4 = 2048; C_B32 = 2560; C_B64 = 2688; C_BOFF = 2816; NCONST = 2944


def make_rope():
    half = 16
    inv = 10000.0 ** (-np.arange(half, dtype=np.float64) / half)
    t = np.arange(T)
    row = (t // 64).astype(np.float64); col = (t % 64).astype(np.float64)
    cosT = np.ones((128, NT), np.float32); sinT = np.zeros((128, NT), np.float32)
    for p in range(128):
        d = p % 64; grp = d // 32; f = d % 16
        pos = row if grp == 0 else col
        ang = (pos.astype(np.float32) * np.float32(inv[f]).astype(np.float32)).astype(np.float32)
        cosT[p, CL:] = np.cos(ang); sinT[p, CL:] = np.sin(ang)
    return cosT, sinT


def make_wmask():
    m = np.zeros((6, 128, 512), np.float32)
    jj = np.arange(128)[:, None]; ii = np.arange(512)[None, :]
    for r in range(6):
        rel = r - 1
        m[r] = (np.abs(ii - jj - 128 * rel) <= 128)
    return m


class KB:
    def __init__(self, nc, stack, dbg=None, stop=None):
        self.nc = nc; self.st = stack; self.P = Prog(nc); self.dbg = dbg or {}
        self.stop = stop
        self.lane_rr = 0

    def mm(self, out, lhsT, rhs, start=True, stop=True):
        self.P.add("pe", lambda e: e.matmul(out, lhsT, rhs, start=start, stop=stop), reads=[lhsT, rhs], writes=[out])

    def tr(self, out, in_, ident):
        self.P.add("pe", lambda e: e.transpose(out, in_, ident), reads=[in_, ident], writes=[out])

    def act(self, out, in_, func, bias=None, scale=None):
        kw = {}
        rd = [in_]
        if bias is not None:
            kw["bias"] = bias
            if not isinstance(bias, (int, float)):
                rd.append(bias)
        if scale is not None:
            kw["scale"] = scale
            if not isinstance(scale, (int, float)):
                rd.append(scale)
        self.P.add("act", lambda e: e.activation(out, in_, func, **kw), reads=rd, writes=[out])

    def tt(self, eng, out, in0, in1, op):
        self.P.add(eng, lambda e: e.tensor_tensor(out, in0, in1, op), reads=[in0, in1], writes=[out])

    def ts(self, eng, out, in0, s1, op0, s2=None, op1=None):
        rd = [in0] + [s for s in (s1, s2) if s is not None and not isinstance(s, (int, float))]
        if op1 is None:
            self.P.add(eng, lambda e: e.tensor_scalar(out, in0, s1, None, op0), reads=rd, writes=[out])
        else:
            self.P.add(eng, lambda e: e.tensor_scalar(out, in0, s1, s2, op0, op1), reads=rd, writes=[out])

    def stt(self, eng, out, in0, scalar, in1, op0, op1):
        rd = [in0, in1] + ([] if isinstance(scalar, (int, float)) else [scalar])
        self.P.add(eng, lambda e: e.scalar_tensor_tensor(out, in0, scalar, in1, op0, op1), reads=rd, writes=[out])

    def copy(self, eng, out, in_):
        if eng == "act":
            self.P.add("act", lambda e: e.activation(out, in_, AF.Copy), reads=[in_], writes=[out])
        else:
            self.P.add(eng, lambda e: e.tensor_copy(out, in_), reads=[in_], writes=[out])

    def memset(self, eng, out, val):
        self.P.add(eng, lambda e: e.memset(out, val), writes=[out])

    def recip(self, out, in_):
        self.P.add("dve", lambda e: e.reciprocal(out, in_), reads=[in_], writes=[out])

    def dma(self, out, in_, lane, eng="sp", slow=False, track_out=True):
        kw = {"allow_slow_non_contiguous": True} if slow else {}
        wr = [out] if track_out else []
        rd = [] if track_out else [in_]
        return self.P.add(eng, lambda e: e.dma_start(out=out, in_=in_, **kw), reads=rd, writes=wr, lane=lane)

    def barrier(self, engs=("pe", "act", "dve", "pool")):
        lasts = [self.P.eng_ops[e][-1] for e in engs if self.P.eng_ops[e]]
        for e in engs:
            if self.P.eng_ops[e]:
                self.P.add(e, None, extra=lasts)

    def sb(self, name, shape, dt):
        return self.st.enter_context(self.nc.sbuf_tensor(name, shape, dt))

class Arena:
    def __init__(self, t, words):
        self.t = t; self.words = words; self.off = 0

    def f32(self, n):
        a = self.t[:, self.off:self.off + n]; self.off += n
        assert self.off <= self.words, ("arena overflow", self.off, self.words)
        return a

    def bf16(self, n):
        w = (n + 1) // 2
        a = self.t[:, self.off:self.off + w].bitcast(BF16); self.off += w
        assert self.off <= self.words, ("arena overflow", self.off, self.words)
        return a[:, 0:n]

    def mark(self):
        return self.off

    def release(self, m):
        self.off = m


def R_(ap):
    return ap


def RR(ap):
    import os
    return ap if os.environ.get("NO_F32R") else ap.bitcast(F32R)


def build(nc, dbg=(), stop=None, nlayers=L):
    def dram(name, shape, kind="ExternalInput"):
        return nc.dram_tensor(name, list(shape), F32, kind=kind).ap()
    x_d = dram("x", [T, D]); ctx_d = dram("ctx", [CL, D]); c_d = dram("c", [D]); cc_d = dram("c_ctx", [D])
    wmod_d = dram("w_mod", [L, D, 6 * D]); bmod_d = dram("b_mod", [L, 6 * D]); gattn_d = dram("g_attn", [L, D])
    win_d = dram("w_in", [L, D, DIN]); conv_d = dram("gdn_conv_w", [L, 5, 1536])
    alog_d = dram("gdn_a_log", [L, 16]); dtb_d = dram("gdn_dt_bias", [L, 16]); gng_d = dram("gdn_norm_g", [L, 64])
    gaq_d = dram("ga_q_norm_g", [L, 64]); gak_d = dram("ga_k_norm_g", [L, 64])
    waq_d = dram("wa_q_norm_g", [L, 64]); wak_d = dram("wa_k_norm_g", [L, 64]); sink_d = dram("wa_sink", [L, 4])
    wout_d = dram("w_out", [L, D, D]); gmlp_d = dram("g_mlp", [L, D])
    w1_d = dram("w_mlp_in", [L, D, DFF]); w2_d = dram("w_mlp_out", [L, DFF, D])
    cst_d = dram("consts", [128, NCONST]); ropec_d = dram("ropec", [128, NT]); ropes_d = dram("ropes", [128, NT])
    wmask_d = dram("wmask", [128, 6 * 512])
    out_d = dram("out", [T, D], kind="ExternalOutput")
    dbg_d = {}
    for (nm, shape) in dbg:
        dbg_d[nm] = dram("dbg_" + nm, shape, kind="ExternalOutput")

    st = ExitStack()
    with st:
        K = KB(nc, st, stop=stop)
        P = K.P
        xT = K.sb("xT", [128, 8, NT], F32)
        xnT = K.sb("xnT", [128, 8, NT], BF16)
        cst = K.sb("cst", [128, NCONST], F32)
        cstR = K.sb("cstR", [128, 256], F32)
        dbl = K.sb("dbl", [128, 6 * 512], F32)
        cb = K.sb("cb", [128, 512], BF16)
        modT = [K.sb("modT%d" % l, [128, 48, 2], F32) for l in range(L)]
        gsA = K.sb("gsA", [128, 8, 2], F32); gsM = K.sb("gsM", [128, 8, 2], F32)
        small = K.sb("small", [128, 64], F32)
        AW = (nc.sbuf_bytes_remaining - 2048) // 4
        arena_t = K.sb("arena", [128, AW], F32)
        AR = Arena(arena_t, AW)
        ps = st.enter_context(nc.psum_tensor("ps", [128, 4096], F32))

        def bank(i):
            return ps[:, i * 512:(i + 1) * 512]

        ident = cst[:, C_ID:C_ID + 128]
        identb = cb[:, 0:128]; onesb = cb[:, 128:256]; blkb = cb[:, 256:384]
        eps_norm = small[:, 0:1]
        one_c = small[:, 1:2]

        def dump(name, ap_src, dst=None):
            if name in dbg_d:
                K.dma(dbg_d[name] if dst is None else dst, ap_src, lane="dbg_" + name, track_out=False, eng="pool")

        K.dma(cst[:], cst_d, lane="cst")
        K.copy("dve", R_(cstR[:]), cst[:, C_AF:C_AF + 256])
        K.copy("dve", identb, cst[:, C_ID:C_ID + 128])
        K.copy("dve", onesb, cst[:, C_ONES:C_ONES + 128])
        K.copy("dve", blkb, cst[:, C_BLK:C_BLK + 128])
        K.memset("dve", eps_norm, 1e-6)
        K.memset("dve", one_c, 1.0)

        m0 = AR.mark()
        xin = [AR.f32(1024) for _ in range(2)]
        for t in range(NTILE):
            src = ctx_d[t * 128:(t + 1) * 128, :] if t < 2 else x_d[(t - 2) * 128:(t - 1) * 128, :]
            xi = xin[t % 2]
            K.dma(xi, src, lane="xin%d" % (t % 2))
            pb = ps[:, (t % 2) * 1024:(t % 2) * 1024 + 1024]
            for cch in range(8):
                K.tr(pb[:, cch * 128:(cch + 1) * 128], xi[:, cch * 128:(cch + 1) * 128], ident)
            K.copy("act" if t % 2 else "dve", xT[:, :, t * 128:(t + 1) * 128],
                   pb.rearrange("p (c n) -> p c n", c=8))
        AR.release(m0)

        m0 = AR.mark()
        craw = AR.f32(16).rearrange("p (k r) -> p k r", r=2)
        K.dma(craw[:, :, 0], c_d.rearrange("(k p) -> p k", p=128), lane="c0", slow=True)
        K.dma(craw[:, :, 1], cc_d.rearrange("(k p) -> p k", p=128), lane="c1", slow=True)
        scb = AR.bf16(16).rearrange("p (k r) -> p k r", r=2)
        K.act(scb, craw, AF.Silu)
        modrow = AR.f32(6144)
        brow = AR.f32(6144)
        wmb = [AR.bf16(8 * 512).rearrange("p (k n) -> p k n", k=8) for _ in range(2)]
        for l in range(nlayers):
            K.dma(brow[0:2, :], bmod_d[l:l + 1, :].partition_broadcast(2).rearrange("p o n -> p (o n)"), lane="brow")
            for j in range(12):
                wb = wmb[j % 2]
                K.dma(wb, wmod_d[l].rearrange("(k p) n -> p k n", p=128)[:, :, j * 512:(j + 1) * 512],
                      lane="wmb%d" % (j % 2), eng="pool")
                pb = bank(j % 2)
                for k in range(8):
                    K.mm(pb[0:2, :], scb[:, k, :], wb[:, k, :], start=(k == 0), stop=(k == 7))
                K.tt("dve", modrow[0:2, j * 512:(j + 1) * 512], pb[0:2, :], brow[0:2, j * 512:(j + 1) * 512], ALU.add)
            pb = bank(2)
            for ch in range(48):
                K.tr(pb[:, ch * 2:ch * 2 + 2], modrow[0:2, ch * 128:(ch + 1) * 128], ident[0:2, 0:2])
            K.copy("dve", modT[l][:], pb[:, 0:96].rearrange("p (c r) -> p c r", r=2))
            dump("modT%d" % l, modT[l][:])
        AR.release(m0)

        def norm_blocks(l, which, blocks):
            gs = gsA if which == 0 else gsM
            shoff = 0 if which == 0 else 24
            m1 = AR.mark()
            sqb = AR.bf16(8 * 512).rearrange("p (c n) -> p c n", c=8)
            rs = AR.f32(512); rstd = AR.f32(512)
            tmp = [AR.f32(512) for _ in range(2)]
            for (a0, n) in blocks:
                r = 1 if a0 < CL else 0
                K.act(sqb[:, :, 0:n], xT[:, :, a0:a0 + n], AF.Square)
                pb = bank(7)
                for cch in range(8):
                    K.mm(pb[:, 0:n], onesb, sqb[:, cch, 0:n], start=(cch == 0), stop=(cch == 7))
                K.act(rs[:, 0:n], pb[:, 0:n], AF.Sqrt, bias=eps_norm, scale=1.0 / D)
                K.recip(rstd[:, 0:n], rs[:, 0:n])
                for cch in range(8):
                    tm = tmp[cch % 2]
                    K.stt("dve", tm[:, 0:n], xT[:, cch, a0:a0 + n], gs[:, cch, r:r + 1], rstd[:, 0:n], ALU.mult, ALU.mult)
                    K.act(xnT[:, cch, a0:a0 + n], tm[:, 0:n], AF.Identity, bias=modT[l][:, shoff + cch, r:r + 1])
            AR.release(m1)

        def load_cols(dst, wd, l, col0, ncols, lane, nk=8):
            K.dma(dst, wd[l].rearrange("(k p) n -> p k n", p=128)[:, :, col0:col0 + ncols], lane=lane, eng="pool")

        def proj_fm(pb, w, blk):
            a0, n = blk
            for k in range(8):
                K.mm(pb[:, 0:n], w[:, k, :], xnT[:, k, a0:a0 + n], start=(k == 0), stop=(k == 7))

        def resid_add(l, which, pb, cch, blk):
            a0, n = blk
            r = 1 if a0 < CL else 0
            goff = 16 if which == 0 else 40
            K.stt("dve", xT[:, cch, a0:a0 + n], pb[:, 0:n], modT[l][:, goff + cch, r:r + 1], xT[:, cch, a0:a0 + n],
                  ALU.mult, ALU.add)

        for l in range(nlayers):
            last = (l == L - 1)
            oblocks = BLOCKS[1:] if last else BLOCKS
            m_layer = AR.mark()
            gT = AR.f32(16).rearrange("p (k r) -> p k r", r=2)
            K.dma(gT[:, :, 0], gattn_d[l].rearrange("(k p) -> p k", p=128), lane="gT0", slow=True)
            K.dma(gT[:, :, 1], gmlp_d[l].rearrange("(k p) -> p k", p=128), lane="gT1", slow=True)
            for r in range(2):
                K.stt("dve", gsA[:, :, r], modT[l][:, 8:16, r], 1.0, gT[:, :, 0], ALU.add, ALU.mult)
                K.stt("dve", gsM[:, :, r], modT[l][:, 32:40, r], 1.0, gT[:, :, 1], ALU.add, ALU.mult)
            norm_blocks(l, 0, BLOCKS)
            if l == 0:
                dump("xnT", xnT[:, :, :])
            if stop == "norm":
                break

            m_gdn = AR.mark()
            gates = AR.f32(18 * 16 * 6)
            m1 = AR.mark()
            gates2 = AR.f32(18 * 16 * 2)
            gv = lambda i: (gates[:, i * 288:(i + 1) * 288] if i < 6 else gates2[:, (i - 6) * 288:(i - 5) * 288]).rearrange("p (t c) -> p t c", c=16)
            beta_tm, nbeta_tm, g_tm, eG_tm, kes_tm, eGl_rep, Gam_tm, Gl_rep = [gv(i) for i in range(8)]
            wg = AR.bf16(8 * 32).rearrange("p (k n) -> p k n", k=8)
            load_cols(wg, win_d, l, 2048, 32, "wg")
            graw = AR.f32(18 * 32).rearrange("p (t c) -> p t c", c=32)
            pb = bank(0)
            for t in range(NTILE):
                for k in range(8):
                    K.mm(pb[:, t * 32:(t + 1) * 32] if t < 16 else bank(1)[:, (t - 16) * 32:(t - 15) * 32],
                         xnT[:, k, t * 128:(t + 1) * 128], wg[:, k, :], start=(k == 0), stop=(k == 7))
            K.copy("dve", graw[:, 0:16, :], pb.rearrange("p (t c) -> p t c", c=32))
            K.copy("dve", graw[:, 16:18, :], bank(1)[:, 0:64].rearrange("p (t c) -> p t c", c=32))
            rep = AR.f32(32)
            K.dma(rep[:, 0:16], dtb_d[l:l + 1, :].partition_broadcast(128).rearrange("p o n -> p (o n)"), lane="rep0")
            K.dma(rep[:, 16:32], alog_d[l:l + 1, :].partition_broadcast(128).rearrange("p o n -> p (o n)"), lane="rep1")
            negA = AR.f32(16)
            K.act(negA, rep[:, 16:32], AF.Exp)
            K.ts("dve", negA, negA, -1.0, ALU.mult)
            K.act(beta_tm, graw[:, :, 0:16], AF.Sigmoid)
            K.ts("dve", nbeta_tm, beta_tm, -1.0, ALU.mult)
            xa = AR.f32(288).rearrange("p (t c) -> p t c", c=16)
            ab = AR.f32(288).rearrange("p (t c) -> p t c", c=16)
            K.tt("dve", xa, graw[:, :, 16:32], rep[:, 0:16].unsqueeze(1).to_broadcast([128, 18, 16]), ALU.add)
            K.ts("dve", ab, xa, -1.0, ALU.mult)
            K.tt("dve", ab, ab, xa, ALU.max)
            K.act(ab, ab, AF.Exp, scale=-1.0)
            K.act(ab, ab, AF.Ln, bias=one_c)
            K.ts("dve", xa, xa, 0.0, ALU.max)
            K.tt("dve", xa, xa, ab, ALU.add)
            K.tt("dve", g_tm, xa, negA.unsqueeze(1).to_broadcast([128, 18, 16]), ALU.mult)
            pb = bank(2)
            for n_ in range(NTILE):
                for d_ in range(2):
                    Bm = cst[:, C_BF:C_BF + 128] if d_ == 0 else cst[:, C_BB:C_BB + 128]
                    K.mm(pb[:, n_ * 16 + d_ * 8:n_ * 16 + d_ * 8 + 8], Bm, g_tm[:, n_, d_ * 8:d_ * 8 + 8])
            K.copy("dve", Gam_tm, pb[:, 0:288].rearrange("p (t c) -> p t c", c=16))
            pb = bank(3)
            K.mm(pb[:, 0:288], cst[:, C_ONES:C_ONES + 128], gates[:, 2 * 288:3 * 288])
            K.copy("dve", Gl_rep, pb[:, 0:288].rearrange("p (t c) -> p t c", c=16))
            K.tt("dve", kes_tm, Gl_rep, Gam_tm, ALU.subtract)
            K.ts("dve", kes_tm, kes_tm, -40.0, ALU.max)
            K.act(kes_tm, kes_tm, AF.Exp)
            K.ts("dve", eG_tm, Gam_tm, -40.0, ALU.max)
            K.act(eG_tm, eG_tm, AF.Exp)
            K.ts("dve", eGl_rep, Gl_rep, -40.0, ALU.max)
            K.act(eGl_rep, eGl_rep, AF.Exp)
            if l == 0:
                dump("g_tm", g_tm); dump("beta_tm", beta_tm)
            AR.release(m1)
            convT = AR.f32(60).rearrange("p (q j) -> p q j", j=5)
            for j in range(5):
                K.dma(convT[:, :, j], conv_d[l][j].rearrange("(q p) -> p q", p=128), lane="convT%d" % j, slow=True)
            gng = AR.f32(64)
            K.dma(gng, gng_d[l:l + 1, :].partition_broadcast(128).rearrange("p o n -> p (o n)"), lane="gng")
            S_all = AR.f32(128)
            pgt = AR.f32(6 * 72)
            pgv = lambda i: pgt[:, i * 72:(i + 1) * 72].rearrange("p (t c) -> p t c", c=4)
            p_beta, p_nbeta, p_g, p_eG, p_kes, p_eGl = [pgv(i) for i in range(6)]
            qT = AR.f32(NT); kT = AR.f32(NT); vT = AR.f32(NT)
            o_acc = AR.f32(NT)
            o_acc3 = o_acc.rearrange("p (t c) -> p t c", c=128)
            wq4 = AR.bf16(8 * 3 * 128).rearrange("p (k t n) -> p k t n", k=8, t=3)
            m_ov = AR.mark()
            dg = AR.bf16(5 * 128).rearrange("p (j n) -> p j n", j=5)
            raw = AR.bf16(2312)
            nsq = AR.bf16(512); nrs = AR.f32(512); nrstd = AR.f32(512)
            AR.release(m_ov)
            zsT = AR.bf16(NT); oTp = AR.bf16(NT); wo = AR.bf16(1024); ors = AR.f32(36); orstd = AR.f32(36)
            AR.release(m_ov)
            def f512():
                return AR.f32(512)
            gB = f512(); iT = f512()
            PTb = [dbl[:, 0:512], dbl[:, 512:1024]]; Pb = [dbl[:, 1024:1536], dbl[:, 1536:2048]]
            Yb = [dbl[:, 2048:2560], dbl[:, 2560:3072]]
            dI = gB; Eb = f512(); dS = f512()
            kE = AR.f32(256); kend = AR.f32(256); vt = AR.f32(256); ub = AR.f32(256); vn = AR.f32(256)
            wT = f512()
            Q64 = f512(); Q128 = f512()
            print("arena used at GDN:", AR.off, "of", AR.words)

            def rawpos(a0):
                return a0 + 2 if a0 < CL else a0 + 6

            import os
            _pairs = [int(v) for v in os.environ.get('PAIRS', '0,1,2,3').split(',') if v != 'none']
            for pr in _pairs:
                for src_, dst_ in ((beta_tm, p_beta), (nbeta_tm, p_nbeta), (g_tm, p_g), (eG_tm, p_eG), (kes_tm, p_kes), (eGl_rep, p_eGl)):
                    for d_ in range(2):
                        K.copy("dve", dst_[:, :, d_ * 2:d_ * 2 + 2], src_[:, :, d_ * 8 + 2 * pr:d_ * 8 + 2 * pr + 2])
                K.memset("pool", raw, 0.0)
                for ty in range(3):
                    K.dma(wq4[:, :, ty, :], win_d[l].rearrange("(k p) n -> p k n", p=128)[:, :, ty * 512 + pr * 128: ty * 512 + pr * 128 + 128],
                          lane="wq4_%d" % ty, eng="pool")
                for ty in range(3):
                    for (a0, n) in BLOCKS:
                        pb = bank(a0 // 512 % 2)
                        for k in range(8):
                            K.mm(pb[:, 0:n], wq4[:, k, ty, :], xnT[:, k, a0:a0 + n], start=(k == 0), stop=(k == 7))
                        K.copy("act", raw[:, rawpos(a0):rawpos(a0) + n], pb[:, 0:n])
                    for j in range(5):
                        K.ts("dve", dg[:, j, :], identb, convT[:, ty * 4 + pr, j:j + 1], ALU.mult)
                    dst = (qT, kT, vT)[ty]
                    for (a0, n) in BLOCKS:
                        pb = bank(2 + a0 // 512 % 2)
                        for j in range(5):
                            K.mm(pb[:, 0:n], dg[:, j, :], raw[:, rawpos(a0) + j - 2:rawpos(a0) + j - 2 + n],
                                 start=(j == 0), stop=(j == 4))
                        if ty == 2:
                            K.act(R_(vT[:, a0:a0 + n]), pb[:, 0:n], AF.Silu)
                        else:
                            K.act(o_acc[:, a0:a0 + n], pb[:, 0:n], AF.Silu)
                            K.act(nsq[:, 0:n], o_acc[:, a0:a0 + n], AF.Square)
                            pb2 = bank(4 + a0 // 512 % 2)
                            K.mm(pb2[:, 0:n], blkb, nsq[:, 0:n])
                            K.act(nrs[:, 0:n], pb2[:, 0:n], AF.Sqrt, bias=eps_norm, scale=1.0)
                            K.recip(nrstd[:, 0:n], nrs[:, 0:n])
                            K.stt("dve", R_(dst[:, a0:a0 + n]), o_acc[:, a0:a0 + n], 0.125 if ty == 0 else 1.0,
                                  nrstd[:, 0:n], ALU.mult, ALU.mult)
                if l == 0 and _pairs and pr == _pairs[0]:
                    dump("gq0", qT); dump("gk0", kT); dump("gv0", vT)
                if stop == "gdnproj":
                    break
                K.memset("dve", R_(S_all), 0.0)
                K.memset("pool", o_acc, 0.0)
                import os
                for s in range(int(os.environ.get('GDN_STEPS', NTILE))):
                    info = []
                    for b in range(4):
                        d_, m_ = b // 2, b % 2
                        n_ = ORDER[d_][s]
                        info.append((d_, m_, n_, d_ * 2 + m_))
                    bs = lambda b: slice(b * 128, (b + 1) * 128)
                    hs = lambda b: slice(b * 64, (b + 1) * 64)
                    for b, (d_, m_, n_, col) in enumerate(info):
                        tok = slice(n_ * 128, (n_ + 1) * 128); rows = slice(64 * m_, 64 * m_ + 64)
                        K.mm(bank(m_)[:, d_ * 128:(d_ + 1) * 128], R_(kT[rows, tok]), R_(kT[rows, tok]))
                        K.mm(bank(m_)[:, 256 + d_ * 128:256 + (d_ + 1) * 128], R_(kT[rows, tok]), R_(qT[rows, tok]))
                        Bm = cst[:, C_BF:C_BF + 128] if d_ == 0 else cst[:, C_BB:C_BB + 128]
                        K.ts("dve", R_(gB[:, bs(b)]), Bm, p_g[:, n_, col:col + 1], ALU.mult)
                        Am = cstR[:, 0:128] if d_ == 0 else cstR[:, 128:256]
                        K.mm(bank(2)[:, bs(b)], R_(Am), R_(gB[:, bs(b)]))
                    if (os.environ.get('CUTALL') or s == int(os.environ.get('CUTSTEP', 0))) and int(os.environ.get('CUT', 99)) == 1:
                        continue
                    if os.environ.get('PHASEBAR'):
                        K.barrier()
                    K.ts("dve", Eb, bank(2), -40.0, ALU.max)
                    K.act(Eb, Eb, AF.Exp)
                    K.tt("dve", dI, Eb, cst[:, C_MINC:C_MINC + 512], ALU.mult)
                    K.tt("dve", dS, Eb, cst[:, C_MSTR:C_MSTR + 512], ALU.mult)
                    PT, Pm, Y = PTb[0], Pb[0], Yb[0]
                    for b, (d_, m_, n_, col) in enumerate(info):
                        K.stt("dve", RR(PT[:, bs(b)]), bank(m_)[:, d_ * 128:(d_ + 1) * 128], p_nbeta[:, n_, col:col + 1], dS[:, bs(b)],
                              ALU.mult, ALU.mult)
                    for b, (d_, m_, n_, col) in enumerate(info):
                        K.tt("dve", R_(iT[:, bs(b)]), bank(m_)[:, 256 + d_ * 128:256 + (d_ + 1) * 128], dI[:, bs(b)], ALU.mult)
                    if (os.environ.get('CUTALL') or s == int(os.environ.get('CUTSTEP', 0))) and int(os.environ.get('CUT', 99)) == 2:
                        continue
                    if os.environ.get('PHASEBAR'):
                        K.barrier()
                    for b in range(4):
                        K.tr(bank(3)[:, bs(b)], PT[:, bs(b)], ident)
                    K.copy("dve", RR(Pm), bank(3))
                    v4 = lambda t_: t_.rearrange("p (b n) -> p b n", b=4)
                    mk = lambda c0: cst[:, c0:c0 + 128].unsqueeze(1).to_broadcast([128, 4, 128])
                    K.tt("dve", v4(Q64), v4(Pm), mk(C_B64), ALU.mult)
                    K.tt("dve", v4(Q128), v4(Pm), mk(C_BOFF), ALU.mult)
                    K.tt("dve", v4(RR(PT)), v4(PT), mk(C_B32), ALU.mult)
                    K.tt("dve", v4(RR(Pm)), v4(Pm), mk(C_B32), ALU.mult)
                    K.tt("dve", RR(Y), PT, cst[:, C_ID4:C_ID4 + 512], ALU.add)
                    if (os.environ.get('CUTALL') or s == int(os.environ.get('CUTSTEP', 0))) and int(os.environ.get('CUT', 99)) == 3:
                        continue
                    if os.environ.get('PHASEBAR'):
                        K.barrier()
                    for d_ in range(2):
                        n_ = ORDER[d_][s]
                        tok = slice(n_ * 128, (n_ + 1) * 128)
                        K.tr(bank(6)[:, d_ * 128:(d_ + 1) * 128], kT[:, tok], ident)
                        K.tr(bank(6)[:, 256 + d_ * 128:256 + (d_ + 1) * 128], vT[:, tok], ident)
                    K.copy("dve", R_(vt), bank(6)[:, 256:512])
                    for b, (d_, m_, n_, col) in enumerate(info):
                        if os.environ.get("P3") == "noke":
                            continue
                        if os.environ.get("P3") == "const":
                            K.ts("dve", R_(kE[:, hs(b)]), bank(6)[:, hs(b)], 0.5, ALU.mult)
                            K.ts("dve", R_(kend[:, hs(b)]), bank(6)[:, hs(b)], 0.5, ALU.mult)
                            continue
                        K.ts("dve", R_(kE[:, hs(b)]), bank(6)[:, hs(b)], p_eG[:, n_, col:col + 1], ALU.mult)
                        K.ts("dve", R_(kend[:, hs(b)]), bank(6)[:, hs(b)], p_kes[:, n_, col:col + 1], ALU.mult)
                    if (os.environ.get('CUTALL') or s == int(os.environ.get('CUTSTEP', 0))) and int(os.environ.get('CUT', 99)) == 4:
                        continue
                    if os.environ.get('PHASEBAR'):
                        K.barrier()
                    cur = 0
                    for kk in range(1, 5):
                        PTn, Pn, Yn = PTb[1 - cur], Pb[1 - cur], Yb[1 - cur]
                        PTc, Pc, Yc = PTb[cur], Pb[cur], Yb[cur]
                        for b in range(4):
                            K.mm(bank(4)[:, bs(b)], RR(Pc[:, bs(b)]), RR(PTc[:, bs(b)]))
                        for b in range(4):
                            K.mm(bank(5)[:, bs(b)], RR(PTc[:, bs(b)]), RR(Pc[:, bs(b)]))
                        K.copy("dve", RR(PTn), bank(4))
                        K.copy("dve", RR(Pn), bank(5))
                        for b in range(4):
                            K.mm(bank(7)[:, bs(b)], RR(Pn[:, bs(b)]), RR(Yc[:, bs(b)]))
                        K.tt("dve", RR(Yn), bank(7), Yc, ALU.add)
                        if os.environ.get('PHASEBAR'):
                            K.barrier()
                        cur = 1 - cur
                    for Qm in (Q64, Q128):
                        Yc = Yb[cur]; Yn = Yb[1 - cur]
                        for b in range(4):
                            K.tr(bank(4)[:, bs(b)], Yc[:, bs(b)], ident)
                        K.copy("act", dS, bank(4))
                        for b in range(4):
                            K.mm(bank(5)[:, bs(b)], Qm[:, bs(b)], Yc[:, bs(b)])
                        K.copy("dve", Eb, bank(5))
                        for b in range(4):
                            K.mm(bank(7)[:, bs(b)], dS[:, bs(b)], Eb[:, bs(b)])
                        K.tt("dve", RR(Yn), bank(7), Yc, ALU.add)
                        cur = 1 - cur
                    Y = Yb[cur]
                    if (os.environ.get('CUTALL') or s == int(os.environ.get('CUTSTEP', 0))) and int(os.environ.get('CUT', 99)) == 5:
                        continue
                    if os.environ.get('PHASEBAR'):
                        K.barrier()
                    for b, (d_, m_, n_, col) in enumerate(info):
                        K.mm(bank(0)[:, hs(b)], R_(Y[:, bs(b)]), R_(vt[:, hs(b)]))
                        K.mm(bank(1)[:, bs(b)], R_(kE[:, d_ * 128:(d_ + 1) * 128]), R_(Y[:, bs(b)]))
                    if (os.environ.get('CUTALL') or s == int(os.environ.get('CUTSTEP', 0))) and int(os.environ.get('CUT', 99)) == 51:
                        continue
                    if os.environ.get('PHASEBAR'):
                        K.barrier()
                    for b, (d_, m_, n_, col) in enumerate(info):
                        K.ts("dve", ub[:, hs(b)], bank(0)[:, hs(b)], p_beta[:, n_, col:col + 1], ALU.mult)
                    if (os.environ.get('CUTALL') or s == int(os.environ.get('CUTSTEP', 0))) and int(os.environ.get('CUT', 99)) == 52:
                        continue
                    if os.environ.get('PHASEBAR'):
                        K.barrier()
                    for b, (d_, m_, n_, col) in enumerate(info):
                        rows = slice(64 * m_, 64 * m_ + 64)
                        cc_ = int(os.environ.get('CUT2', 0))
                        if (cc_ == 1 and m_ == 1) or (cc_ == 2 and m_ == 0):
                            continue
                        if m_ == 0 and os.environ.get('ACTCOPY'):
                            K.act(wT[rows, bs(b)], bank(1)[rows, bs(b)], AF.Copy)
                        else:
                            K.copy("dve", wT[rows, bs(b)], bank(1)[rows, bs(b)])
                    if (os.environ.get('CUTALL') or s == int(os.environ.get('CUTSTEP', 0))) and int(os.environ.get('CUT', 99)) == 6:
                        continue
                    if os.environ.get('PHASEBAR'):
                        K.barrier()
                    if os.environ.get('PHASEBAR'):
                        K.barrier()
                    for b, (d_, m_, n_, col) in enumerate(info):
                        rows = slice(64 * m_, 64 * m_ + 64)
                        Sb = S_all[rows, d_ * 64:(d_ + 1) * 64]
                        K.mm(bank(2 + 2 * m_)[:, hs(b)], R_(wT[rows, bs(b)]), R_(Sb))
                    if os.environ.get('PHASEBAR'):
                        K.barrier()
                    if os.environ.get('CUTALL') and int(os.environ.get('CUT', 99)) == 8:
                        continue
                    for b, (d_, m_, n_, col) in enumerate(info):
                        K.stt("dve", R_(vn[:, hs(b)]), bank(2 + 2 * m_)[:, hs(b)], p_nbeta[:, n_, col:col + 1], ub[:, hs(b)],
                              ALU.mult, ALU.add)
                    if os.environ.get('PHASEBAR'):
                        K.barrier()
                    if os.environ.get('CUTALL') and int(os.environ.get('CUT', 99)) == 9:
                        continue
                    for b, (d_, m_, n_, col) in enumerate(info):
                        tok = slice(n_ * 128, (n_ + 1) * 128); rows = slice(64 * m_, 64 * m_ + 64)
                        Sb = S_all[rows, d_ * 64:(d_ + 1) * 64]
                        if not (last and n_ < 2):
                            K.mm(bank(3 + 2 * m_)[:, hs(b)], R_(qT[rows, tok]), R_(Sb))
                            K.mm(bank(7)[:, hs(b)], R_(iT[:, bs(b)]), R_(vn[:, hs(b)]))
                        K.mm(bank(6)[:, hs(b)], R_(kend[:, d_ * 128:(d_ + 1) * 128]), R_(vn[:, hs(b)]))
                    if os.environ.get('PHASEBAR'):
                        K.barrier()
                    if os.environ.get('CUTALL') and int(os.environ.get('CUT', 99)) == 10:
                        continue
                    for b, (d_, m_, n_, col) in enumerate(info):
                        rows = slice(64 * m_, 64 * m_ + 64)
                        Sb = S_all[rows, d_ * 64:(d_ + 1) * 64]
                        if not (last and n_ < 2):
                            oa = o_acc3[:, n_, 64 * m_:64 * m_ + 64]
                            K.tt("dve", oa, bank(7)[:, hs(b)], oa, ALU.add)
                            K.stt("dve", oa, bank(3 + 2 * m_)[:, hs(b)], p_eG[:, n_, col:col + 1], oa, ALU.mult, ALU.add)
                        K.stt("dve", R_(Sb), Sb, p_eGl[rows, n_, col:col + 1], bank(6)[rows, hs(b)], ALU.mult, ALU.add)
                    if os.environ.get("STEPBAR"):
                        K.barrier()
                if l == 0 and _pairs and pr == _pairs[0]:
                    dump("oacc0", o_acc)
                if stop == "gdnscan":
                    break
                sqf = qT
                K.act(sqf, o_acc, AF.Square)
                if int(os.environ.get('CUT3', 99)) == 1:
                    break
                ss = ors; rr = orstd
                P.add("dve", lambda e, ss=ss, sqf=sqf: e.reduce_sum(ss, sqf.rearrange("p (g c) -> p g c", c=64), AX.X),
                      reads=[sqf], writes=[ss])
                if int(os.environ.get('CUT3', 99)) == 2:
                    break
                K.act(ss, ss, AF.Sqrt, bias=eps_norm, scale=1.0 / 64)
                K.recip(rr, ss)
                if int(os.environ.get('CUT3', 99)) == 3:
                    break
                og = o_acc.rearrange("p (g c) -> p g c", c=64)
                K.tt("dve", og, og, rr.unsqueeze(2).to_broadcast([128, 36, 64]), ALU.mult)
                if int(os.environ.get('CUT3', 99)) == 4:
                    break
                K.tt("dve", og, og, gng.unsqueeze(1).to_broadcast([128, 36, 64]), ALU.mult)
                if int(os.environ.get('CUT3', 99)) == 5:
                    break
                K.dma(wq4[:, :, 0, :], win_d[l].rearrange("(k p) n -> p k n", p=128)[:, :, 1536 + pr * 128: 1536 + pr * 128 + 128],
                      lane="wq4_0", eng="pool")
                for (a0, n) in BLOCKS:
                    pb = bank(a0 // 512 % 2)
                    for k in range(8):
                        K.mm(pb[:, 0:n], wq4[:, k, 0, :], xnT[:, k, a0:a0 + n], start=(k == 0), stop=(k == 7))
                    K.act(zsT[:, a0:a0 + n], pb[:, 0:n], AF.Silu)
                if int(os.environ.get('CUT3', 99)) == 6:
                    break
                K.dma(wo, wout_d[l][pr * 128:(pr + 1) * 128, :], lane="wo", eng="pool")
                for (a0, n) in BLOCKS:
                    pb = bank(2 + a0 // 512 % 2)
                    for i_ in range(n // 128):
                        t_ = a0 // 128 + i_
                        K.tr(pb[:, i_ * 128:(i_ + 1) * 128], o_acc3[:, t_, :], ident)
                    K.tt("dve", oTp[:, a0:a0 + n], pb[:, 0:n], zsT[:, a0:a0 + n], ALU.mult)
                if l == 0 and _pairs and pr == _pairs[0]:
                    dump("oTp0", oTp)
                if int(os.environ.get('CUT3', 99)) == 7:
                    break
                for (a0, n) in oblocks:
                    for cch in range(8):
                        pb = bank(4 + cch % 4)
                        K.mm(pb[:, 0:n], wo[:, cch * 128:(cch + 1) * 128], oTp[:, a0:a0 + n])
                        resid_add(l, 0, pb, cch, (a0, n))
            AR.release(m_gdn)
            if stop in ("gdnproj", "gdnscan", "gdn"):
                break

            m_att = AR.mark()
            qTa = AR.bf16(2 * NT).rearrange("p (c n) -> p c n", c=2)
            kTa = AR.bf16(NT)
            vtm = AR.bf16(18 * 2 * 66).rearrange("p (t k c) -> p t k c", t=18, k=2)
            oTa = AR.bf16(4 * NT).rearrange("p (h n) -> p h n", h=4)
            gcol = AR.f32(2); esk = AR.f32(4)
            m_aov = AR.mark()
            cosb = [AR.f32(512) for _ in range(2)]; sinb = [AR.f32(512) for _ in range(2)]
            wqk = AR.bf16(8 * 3 * 128).rearrange("p (k c n) -> p k c n", k=8, c=3)
            wv = AR.bf16(8 * 128).rearrange("p (k n) -> p k n", k=8)
            Rg = AR.f32(2 * 128)
            qraw = AR.f32(512); t1 = AR.f32(512); t2 = AR.f32(512); ars = AR.f32(512); arstd = AR.f32(512)
            asq = AR.bf16(512)
            AR.release(m_aov)
            wmk = AR.bf16(6 * 512)
            wo4 = AR.bf16(4 * 1024).rearrange("p (h n) -> p h n", h=4)
            PTs = [AR.bf16(512) for _ in range(3)]
            rden = AR.f32(512); rdr = AR.f32(512)
            print("arena used at ATT:", AR.off, "of", AR.words)
            for grp in range(2):
                base = BASE_B if grp == 0 else BASE_C
                gq_d, gk_d = (gaq_d, gak_d) if grp == 0 else (waq_d, wak_d)
                wrows = 512 if grp == 0 else 768
                wv_in = win_d[l].rearrange("(k p) n -> p k n", p=128)
                for ci, heads in enumerate(((0, 2), (1, 3))):
                    for hi, h in enumerate(heads):
                        K.dma(wqk[:, :, ci, hi * 64:(hi + 1) * 64], wv_in[:, :, base + h * 64: base + h * 64 + 64],
                              lane="wqk%d%d" % (ci, hi), eng="pool")
                K.dma(wqk[:, :, 2, :], wv_in[:, :, base + 256: base + 384], lane="wqk2", eng="pool")
                K.dma(wv, wv_in[:, :, base + 384: base + 512], lane="wv", eng="pool")
                for hh in range(2):
                    K.dma(gcol[hh * 64:(hh + 1) * 64, 0:1], gq_d[l].rearrange("(p o) -> p o", o=1), lane="gq%d" % hh, slow=True)
                    K.dma(gcol[hh * 64:(hh + 1) * 64, 1:2], gk_d[l].rearrange("(p o) -> p o", o=1), lane="gk%d" % hh, slow=True)
                for qk in range(2):
                    K.ts("dve", R_(Rg[:, qk * 128:(qk + 1) * 128]), cst[:, C_R:C_R + 128], gcol[:, qk:qk + 1], ALU.mult)
                if grp == 1:
                    K.dma(esk[64:65, 0:4], sink_d[l:l + 1, :], lane="sink")
                    K.act(esk[64:65, 0:4], esk[64:65, 0:4], AF.Exp)
                K.memset("pool", vtm[:, :, :, 64:65], 1.0)
                for t in range(NTILE):
                    pb = bank(t % 2)
                    for k in range(8):
                        K.mm(pb[:, 0:128], xnT[:, k, t * 128:(t + 1) * 128], wv[:, k, :], start=(k == 0), stop=(k == 7))
                    K.copy("act", vtm[:, t, :, 0:64], pb[:, 0:128].rearrange("p (k c) -> p k c", k=2))
                rpi = 0
                for ci in range(3):
                    qk = 0 if ci < 2 else 1
                    for (a0, n) in BLOCKS:
                        cosT_b = cosb[rpi % 2]; sinT_b = sinb[rpi % 2]
                        K.dma(cosT_b[:, 0:n], ropec_d[:, a0:a0 + n], lane="cos%d" % (rpi % 2))
                        K.dma(sinT_b[:, 0:n], ropes_d[:, a0:a0 + n], lane="sin%d" % (rpi % 2))
                        rpi += 1
                        pb = bank(2 + a0 // 512 % 2)
                        for k in range(8):
                            K.mm(pb[:, 0:n], wqk[:, k, ci, :], xnT[:, k, a0:a0 + n], start=(k == 0), stop=(k == 7))
                        K.copy("act", R_(qraw[:, 0:n]), pb[:, 0:n])
                        K.act(asq[:, 0:n], pb[:, 0:n], AF.Square)
                        pb2 = bank(4 + a0 // 512 % 2)
                        K.mm(pb2[:, 0:n], blkb, asq[:, 0:n])
                        K.act(ars[:, 0:n], pb2[:, 0:n], AF.Sqrt, bias=eps_norm, scale=1.0 / 64)
                        K.recip(arstd[:, 0:n], ars[:, 0:n])
                        pb3 = bank(6 + a0 // 512 % 2)
                        K.mm(pb3[:, 0:n], R_(Rg[:, qk * 128:(qk + 1) * 128]), R_(qraw[:, 0:n]))
                        K.stt("dve", t1[:, 0:n], qraw[:, 0:n], gcol[:, qk:qk + 1], cosT_b[:, 0:n], ALU.mult, ALU.mult)
                        K.tt("dve", t2[:, 0:n], pb3[:, 0:n], sinT_b[:, 0:n], ALU.mult)
                        K.tt("dve", t1[:, 0:n], t1[:, 0:n], t2[:, 0:n], ALU.add)
                        dst = qTa[:, ci, a0:a0 + n] if ci < 2 else kTa[:, a0:a0 + n]
                        K.tt("dve", dst, t1[:, 0:n], arstd[:, 0:n], ALU.mult)
                if l == 0:
                    dump("qTa%d" % grp, qTa); dump("kTa%d" % grp, kTa)
                K.dma(wo4[0:64], wout_d[l][wrows:wrows + 256, :].rearrange("(h p) n -> p h n", p=64), lane="wo4", eng="pool")
                K.dma(wmk, wmask_d, lane="wmk", eng="pool")
                pti = 0
                for h in range(4):
                    ci = h % 2; kv = h // 2; rows = slice(64 * kv, 64 * kv + 64)
                    qblocks = [(256 + 512 * i, 512) for i in range(4)] + ([] if last else [(0, 256)])
                    for (a0, n) in qblocks:
                        isctx = a0 < CL
                        if isctx:
                            kts = [(0, None), (1, None)]
                        elif grp == 0:
                            kts = [(t, None) for t in range(NTILE)]
                        else:
                            qb = (a0 - 256) // 512
                            kts = [(0, None), (1, None)]
                            for rel in range(-1, 5):
                                lt = 4 * qb + rel
                                if 0 <= lt < 16:
                                    kts.append((lt + 2, rel + 1))
                        ob = bank(4 + (pti % 2))
                        for i_, (kt, mi) in enumerate(kts):
                            sb_ = bank(pti % 3)
                            PTt = PTs[pti % 3]; pti += 1
                            K.mm(sb_[:, 0:n], kTa[rows, kt * 128:(kt + 1) * 128], qTa[rows, ci, a0:a0 + n])
                            K.act(PTt[:, 0:n], sb_[:, 0:n], AF.Exp, scale=0.125)
                            if mi is not None:
                                K.tt("dve", PTt[:, 0:n], PTt[:, 0:n], wmk[:, mi * 512:mi * 512 + n], ALU.mult)
                            K.mm(ob[0:65, 0:n], vtm[:, kt, kv, 0:65], PTt[:, 0:n], start=(i_ == 0), stop=(i_ == len(kts) - 1))
                        if grp == 1:
                            K.ts("dve", rden[64:65, 0:n], ob[64:65, 0:n], esk[64:65, h:h + 1], ALU.add)
                            K.recip(rden[64:65, 0:n], rden[64:65, 0:n])
                        else:
                            K.recip(rden[64:65, 0:n], ob[64:65, 0:n])
                        rb_ = bank(6 + (pti % 2))
                        K.mm(rb_[0:64, 0:n], cst[64:65, C_ONES:C_ONES + 64], rden[64:65, 0:n])
                        K.copy("dve", rdr[0:64, 0:n], rb_[0:64, 0:n])
                        K.tt("dve", oTa[0:64, h, a0:a0 + n], ob[0:64, 0:n], rdr[0:64, 0:n], ALU.mult)
                if l == 0:
                    dump("oTa%d" % grp, oTa[0:64, :, :])
                for (a0, n) in oblocks:
                    for cch in range(8):
                        pb = bank(cch % 4)
                        for h in range(4):
                            K.mm(pb[:, 0:n], wo4[0:64, h, cch * 128:(cch + 1) * 128], oTa[0:64, h, a0:a0 + n],
                                 start=(h == 0), stop=(h == 3))
                        resid_add(l, 0, pb, cch, (a0, n))
            AR.release(m_att)
            if l == 0:
                dump("xT_attn", xT[:, :, :])
            if stop == "attn":
                break

            m_mlp = AR.mark()
            norm_blocks(l, 1, oblocks)
            h1 = AR.bf16(32 * 512).rearrange("p (f n) -> p f n", f=32)
            hr = [AR.bf16(512) for _ in range(2)]
            w1b = [AR.bf16(8 * 512).rearrange("p (k n) -> p k n", k=8) for _ in range(2)]
            w2b = [AR.bf16(32 * 128).rearrange("p (f n) -> p f n", f=32) for _ in range(2)]
            w1i = 0; w2i = 0
            for (a0, n) in oblocks:
                for fb in range(8):
                    w1 = w1b[w1i % 2]
                    load_cols(w1, w1_d, l, fb * 512, 512, "w1b%d" % (w1i % 2)); w1i += 1
                    for f4 in range(4):
                        f = fb * 4 + f4
                        pb = bank(f % 4)
                        for k in range(8):
                            K.mm(pb[:, 0:n], w1[:, k, f4 * 128:(f4 + 1) * 128], xnT[:, k, a0:a0 + n], start=(k == 0), stop=(k == 7))
                        hb = hr[f % 2]
                        K.act(hb[:, 0:n], pb[:, 0:n], AF.Relu)
                        K.tt("dve", h1[:, f, 0:n], hb[:, 0:n], hb[:, 0:n], ALU.mult)
                for cch in range(8):
                    w2 = w2b[w2i % 2]
                    K.dma(w2, w2_d[l].rearrange("(f p) n -> p f n", p=128)[:, :, cch * 128:(cch + 1) * 128],
                          lane="w2b%d" % (w2i % 2), eng="pool"); w2i += 1
                    pb = bank(4 + cch % 4)
                    for f in range(32):
                        K.mm(pb[:, 0:n], w2[:, f, :], h1[:, f, 0:n], start=(f == 0), stop=(f == 31))
                    resid_add(l, 1, pb, cch, (a0, n))
            AR.release(m_mlp)
            AR.release(m_layer)
            if l == 0:
                dump("xT_l0", xT[:, :, :])

        m0 = AR.mark()
        xo = [AR.f32(1024) for _ in range(2)]
        outs = []
        for t in range(16):
            a0 = CL + t * 128
            pb = ps[:, (t % 2) * 1024:(t % 2) * 1024 + 1024]
            for cch in range(8):
                K.tr(pb[:, cch * 128:(cch + 1) * 128], xT[:, cch, a0:a0 + 128], ident)
            K.copy("act" if t % 2 else "dve", xo[t % 2], pb)
            outs.append(K.dma(out_d[t * 128:(t + 1) * 128, :], xo[t % 2], lane="xo%d" % (t % 2), track_out=False))
        dbg_ops = [P.lane_last[k] for k in P.lane_last if str(k).startswith("dbg_")]
        P.add("sp", None, extra=outs + dbg_ops)
        print("ops:", len(P.ops), {e: len(v) for e, v in P.eng_ops.items()})
        P.finalize(st)
        print("waits:", P.n_wait)
    return nc


def _prep_common(inputs):
    f = lambda a: np.ascontiguousarray(np.asarray(a, dtype=np.float32))
    com = {
        "c_ctx": f(inputs["c_ctx"]), "w_mod": f(inputs["w_mod"]), "b_mod": f(inputs["b_mod"]),
        "g_attn": f(inputs["g_attn"]), "w_in": f(inputs["w_in"]), "gdn_conv_w": f(inputs["gdn_conv_w"]),
        "gdn_a_log": f(inputs["gdn_a_log"]).reshape(L, 16), "gdn_dt_bias": f(inputs["gdn_dt_bias"]).reshape(L, 16),
        "gdn_norm_g": f(inputs["gdn_norm_g"]), "ga_q_norm_g": f(inputs["ga_q_norm_g"]),
        "ga_k_norm_g": f(inputs["ga_k_norm_g"]), "wa_q_norm_g": f(inputs["wa_q_norm_g"]),
        "wa_k_norm_g": f(inputs["wa_k_norm_g"]), "wa_sink": f(inputs["wa_sink"]), "w_out": f(inputs["w_out"]),
        "g_mlp": f(inputs["g_mlp"]), "w_mlp_in": f(inputs["w_mlp_in"]), "w_mlp_out": f(inputs["w_mlp_out"]),
        "consts": make_consts(),
    }
    cosT, sinT = make_rope()
    com["ropec"] = cosT; com["ropes"] = sinT
    com["wmask"] = np.ascontiguousarray(make_wmask().transpose(1, 0, 2).reshape(128, 6 * 512))
    return com


def kernel(**inputs):
    nc = build(bass.Bass("TRN2", target_bir_lowering=False))
    com = _prep_common(inputs)
    x = np.asarray(inputs["x"], np.float32); c = np.asarray(inputs["c"], np.float32)
    ctx = np.asarray(inputs["ctx"], np.float32)
    in_maps = []
    for b in range(8):
        m = dict(com)
        m["x"] = np.ascontiguousarray(x[b]); m["ctx"] = np.ascontiguousarray(ctx[b]); m["c"] = np.ascontiguousarray(c[b])
        in_maps.append(m)
    res = run_bass_kernel_spmd(nc, in_maps, core_ids=list(range(8)))
    return np.stack([np.asarray(res.results[b]["out"], np.float32) for b in range(8)], axis=0)
```

```python
from concourse.bass_utils import run_bass_kernel_spmd
import numpy as np
import concourse.bass as bass
import concourse.mybir as mybir

F32 = mybir.dt.float32
BF16 = mybir.dt.bfloat16
F32R = mybir.dt.float32r
ALU = mybir.AluOpType
AF = mybir.ActivationFunctionType
AX = mybir.AxisListType

_ESZ = {}


def _esize(dt):
    if dt not in _ESZ:
        _ESZ[dt] = mybir.dt.size(dt) if hasattr(mybir.dt, "size") else None
    return _ESZ[dt]


def esize(dt):
    s = str(dt)
    if "float32" in s or "int32" in s:
        return 4
    if "bfloat16" in s or "float16" in s or "int16" in s:
        return 2
    if "int8" in s or "float8" in s:
        return 1
    raise ValueError(s)


def region(ap):
    pat = ap.ap
    es = esize(ap.dtype)
    pstep, pcnt = pat[0]
    off = ap.offset
    if pstep == 0:
        p0 = 0
        f0 = off
    else:
        p0 = off // pstep
        f0 = off % pstep
    ext = 1
    for st, cn in pat[1:]:
        ext += (cn - 1) * abs(st)
    b0, b1 = f0 * es, (f0 + ext) * es
    if ap.tensor.name == "ps":
        b0 = (b0 // 2048) * 2048
        b1 = ((b1 + 2047) // 2048) * 2048
        return (ap.tensor.name, 0, 128, b0, b1)
    return (ap.tensor.name, p0, p0 + pcnt, b0, b1)


def _overlap(a, b):
    return a[1] < b[2] and b[1] < a[2] and a[3] < b[4] and b[3] < a[4]


def _covers(a, b):
    return a[1] <= b[1] and a[2] >= b[2] and a[3] <= b[3] and a[4] >= b[4]


STRICT_SAME_ENGINE = False


class Prog:
    ENGS = ("pe", "act", "dve", "pool", "sp")

    def __init__(self, nc):
        self.nc = nc
        self.ops = []
        self.eng_ops = {e: [] for e in self.ENGS}
        self.track = {}
        self.lane_last = {}
        self.lane_cnt = {}

    def add(self, eng, fn, reads=(), writes=(), lane=None, extra=()):
        idx = len(self.ops)
        deps = set((d, 'raw') for d in extra)
        rregs = [region(a) for a in reads]
        wregs = [region(a) for a in writes]
        for r in rregs:
            st = self.track.setdefault(r[0], {"w": [], "r": []})
            for (wr, wop) in st["w"]:
                if _overlap(r, wr):
                    deps.add((wop, "raw"))
        for w in wregs:
            st = self.track.setdefault(w[0], {"w": [], "r": []})
            for (wr, wop) in st["w"]:
                if _overlap(w, wr):
                    deps.add((wop, "waw"))
            for (rr, rop) in st["r"]:
                if _overlap(w, rr):
                    deps.add((rop, "war"))
        for w in wregs:
            st = self.track[w[0]]
            st["w"] = [(wr, wop) for (wr, wop) in st["w"] if not _covers(w, wr)]
            st["r"] = [(rr, rop) for (rr, rop) in st["r"] if not _covers(w, rr)]
            st["w"].append((w, idx))
        for r in rregs:
            st = self.track[r[0]]
            st["r"] = [(rr, rop) for (rr, rop) in st["r"]
                       if rop == idx or not (self.ops[rop]["eng"] == eng and self.ops[rop]["lane"] is None
                               and lane is None and _covers(r, rr))]
            st["r"].append((r, idx))
        if lane is not None:
            if lane in self.lane_last:
                deps.add((self.lane_last[lane], "lane"))
            self.lane_last[lane] = idx
            self.lane_cnt[lane] = self.lane_cnt.get(lane, 0) + 1
        op = dict(eng=eng, fn=fn, deps=deps, lane=lane, idx=idx,
                  pos=len(self.eng_ops[eng]) + 1,
                  lpos=self.lane_cnt.get(lane, 0) if lane is not None else 0)
        self.ops.append(op)
        self.eng_ops[eng].append(idx)
        return idx

    def finalize(self, stack):
        nc = self.nc
        ops = self.ops
        def dom(o):
            return ("L", o["lane"]) if o["lane"] is not None else ("E", o["eng"])

        def dpos(o):
            return o["lpos"] if o["lane"] is not None else o["pos"]

        clk = {e: {} for e in self.ENGS}
        vcs = [None] * len(ops)
        waits = [None] * len(ops)
        signal = [False] * len(ops)
        for o in ops:
            e = o["eng"]
            c = clk[e]
            ws = []
            for (d, kind) in sorted(o["deps"], key=lambda t: -t[0]):
                od = ops[d]
                if od["lane"] is None and od["eng"] == e:
                    if e == "pe" or e == "sp" or (kind != "raw" and not STRICT_SAME_ENGINE):
                        continue
                dd, dp = dom(od), dpos(od)
                if c.get(dd, 0) >= dp:
                    continue
                ws.append(d)
                for k, v in vcs[d].items():
                    if c.get(k, 0) < v:
                        c[k] = v
            waits[o["idx"]] = ws
            for d in ws:
                signal[d] = True
            vc = dict(c)
            vc[dom(o)] = dpos(o)
            vcs[o["idx"]] = vc
        import os as _os
        LIM = int(_os.environ.get("SEMLIM", 1000))
        LIML = 60
        sem = {}
        cnt = {}
        skey = [None] * len(ops)
        sval = [0] * len(ops)

        def getsem(key):
            if key not in sem:
                sem[key] = stack.enter_context(nc.semaphore("s%d" % len(sem)))
            return sem[key]

        for o in ops:
            d = dom(o)
            if o["lane"] is not None:
                cnt[d] = cnt.get(d, 0) + 1
                ep = (cnt[d] - 1) // LIML
                skey[o["idx"]] = (d, ep)
                sval[o["idx"]] = ((cnt[d] - 1) % LIML + 1) * 16
                getsem((d, ep))
            elif signal[o["idx"]]:
                cnt[d] = cnt.get(d, 0) + 1
                ep = (cnt[d] - 1) // LIM
                skey[o["idx"]] = (d, ep)
                sval[o["idx"]] = (cnt[d] - 1) % LIM + 1
                getsem((d, ep))
        self.n_sem = len(sem)
        self.n_wait = sum(len(w) for w in waits)
        block = stack.enter_context(nc.Block())
        engs = {"pe": block.tensor, "act": block.scalar, "dve": block.vector,
                "pool": block.gpsimd, "sp": block.sync}

        def make(e):
            def body(eng):
                for i in self.eng_ops[e]:
                    o = ops[i]
                    for d in waits[i]:
                        eng.wait_ge(sem[skey[d]], sval[d])
                    if o["fn"] is None:
                        continue
                    ins = o["fn"](eng)
                    if o["lane"] is not None:
                        ins.then_inc(sem[skey[i]], 16)
                    elif signal[i]:
                        ins.then_inc(sem[skey[i]], 1)
            return body

        for e in self.ENGS:
            if self.eng_ops[e]:
                engs[e](make(e))
from contextlib import ExitStack

D = 1024; T = 2048; CL = 256; NT = 2304; NTILE = 18; DFF = 4096; DIN = 3104; L = 2
BLOCKS = [(0, 256)] + [(256 + 512 * i, 512) for i in range(4)]
BASE_B = 2080; BASE_C = 2592
ORDER = [list(range(18)), [1, 0] + list(range(17, 1, -1))]


def make_consts():
    k = np.arange(128)[:, None]; j = np.arange(128)[None, :]
    ident = (k == j)
    A_f = (k > j); A_b = (k < j); B_f = (k <= j); B_b = (k >= j)
    Minc_f = (j >= k); Minc_b = (j <= k); Mstr_f = (j > k); Mstr_b = (j < k)
    blk = (k // 64 == j // 64)
    ones = np.ones((128, 128))
    R = np.zeros((128, 128))
    for d in range(128):
        loc = d % 32
        if loc < 16:
            R[d + 16, d] = -1.0
        else:
            R[d - 16, d] = 1.0
    b32 = (k // 32 == j // 32); b64 = (k // 64 == j // 64)
    cols = [ident, A_f, A_b, B_f, B_b, Minc_f, Minc_f, Minc_b, Minc_b, Mstr_f, Mstr_f, Mstr_b, Mstr_b,
            blk, ones, R, ident, ident, ident, ident, b32, b64 & ~b32, ~b64]
    return np.concatenate([np.asarray(c, np.float32) for c in cols], axis=1)


C_ID = 0; C_AF = 128; C_AB = 256; C_BF = 384; C_BB = 512; C_MINC = 640; C_MSTR = 1152
C_BLK = 1664; C_ONES = 1792; C_R = 1920; C_ID4 = 2048; C_B32 = 2560; C_B64 = 2688; C_BOFF = 2816; NCONST = 2944


def make_rope():
    half = 16
    inv = 10000.0 ** (-np.arange(half, dtype=np.float64) / half)
    t = np.arange(T)
    row = (t // 64).astype(np.float64); col = (t % 64).astype(np.float64)
    cosT = np.ones((128, NT), np.float32); sinT = np.zeros((128, NT), np.float32)
    for p in range(128):
        d = p % 64; grp = d // 32; f = d % 16
        pos = row if grp == 0 else col
        ang = (pos.astype(np.float32) * np.float32(inv[f]).astype(np.float32)).astype(np.float32)
        cosT[p, CL:] = np.cos(ang); sinT[p, CL:] = np.sin(ang)
    return cosT, sinT


def make_wmask():
    m = np.zeros((6, 128, 512), np.float32)
    jj = np.arange(128)[:, None]; ii = np.arange(512)[None, :]
    for r in range(6):
        rel = r - 1
        m[r] = (np.abs(ii - jj - 128 * rel) <= 128)
    return m


class KB:
    def __init__(self, nc, stack, dbg=None, stop=None):
        self.nc = nc; self.st = stack; self.P = Prog(nc); self.dbg = dbg or {}
        self.stop = stop
        self.lane_rr = 0

    def mm(self, out, lhsT, rhs, start=True, stop=True):
        self.P.add("pe", lambda e: e.matmul(out, lhsT, rhs, start=start, stop=stop), reads=[lhsT, rhs], writes=[out])

    def tr(self, out, in_, ident):
        self.P.add("pe", lambda e: e.transpose(out, in_, ident), reads=[in_, ident], writes=[out])

    def act(self, out, in_, func, bias=None, scale=None):
        kw = {}
        rd = [in_]
        if bias is not None:
            kw["bias"] = bias
            if not isinstance(bias, (int, float)):
                rd.append(bias)
        if scale is not None:
            kw["scale"] = scale
            if not isinstance(scale, (int, float)):
                rd.append(scale)
        self.P.add("act", lambda e: e.activation(out, in_, func, **kw), reads=rd, writes=[out])

    def tt(self, eng, out, in0, in1, op):
        self.P.add(eng, lambda e: e.tensor_tensor(out, in0, in1, op), reads=[in0, in1], writes=[out])

    def ts(self, eng, out, in0, s1, op0, s2=None, op1=None):
        rd = [in0] + [s for s in (s1, s2) if s is not None and not isinstance(s, (int, float))]
        if op1 is None:
            self.P.add(eng, lambda e: e.tensor_scalar(out, in0, s1, None, op0), reads=rd, writes=[out])
        else:
            self.P.add(eng, lambda e: e.tensor_scalar(out, in0, s1, s2, op0, op1), reads=rd, writes=[out])

    def stt(self, eng, out, in0, scalar, in1, op0, op1):
        rd = [in0, in1] + ([] if isinstance(scalar, (int, float)) else [scalar])
        self.P.add(eng, lambda e: e.scalar_tensor_tensor(out, in0, scalar, in1, op0, op1), reads=rd, writes=[out])

    def copy(self, eng, out, in_):
        if eng == "act":
            self.P.add("act", lambda e: e.activation(out, in_, AF.Copy), reads=[in_], writes=[out])
        else:
            self.P.add(eng, lambda e: e.tensor_copy(out, in_), reads=[in_], writes=[out])

    def memset(self, eng, out, val):
        self.P.add(eng, lambda e: e.memset(out, val), writes=[out])

    def recip(self, out, in_):
        self.P.add("dve", lambda e: e.reciprocal(out, in_), reads=[in_], writes=[out])

    def dma(self, out, in_, lane, eng="sp", slow=False, track_out=True):
        kw = {"allow_slow_non_contiguous": True} if slow else {}
        wr = [out] if track_out else []
        rd = [] if track_out else [in_]
        return self.P.add(eng, lambda e: e.dma_start(out=out, in_=in_, **kw), reads=rd, writes=wr, lane=lane)

    def barrier(self, engs=("pe", "act", "dve", "pool")):
        lasts = [self.P.eng_ops[e][-1] for e in engs if self.P.eng_ops[e]]
        for e in engs:
            if self.P.eng_ops[e]:
                self.P.add(e, None, extra=lasts)

    def sb(self, name, shape, dt):
        return self.st.enter_context(self.nc.sbuf_tensor(name, shape, dt))

class Arena:
    def __init__(self, t, words):
        self.t = t; self.words = words; self.off = 0

    def f32(self, n):
        a = self.t[:, self.off:self.off + n]; self.off += n
        assert self.off <= self.words, ("arena overflow", self.off, self.words)
        return a

    def bf16(self, n):
        w = (n + 1) // 2
        a = self.t[:, self.off:self.off + w].bitcast(BF16); self.off += w
        assert self.off <= self.words, ("arena overflow", self.off, self.words)
        return a[:, 0:n]

    def mark(self):
        return self.off

    def release(self, m):
        self.off = m


def R_(ap):
    return ap


def RR(ap):
    import os
    return ap.bitcast(F32R) if os.environ.get("USE_F32R") else ap


def build(nc, dbg=(), stop=None, nlayers=L):
    def dram(name, shape, kind="ExternalInput"):
        return nc.dram_tensor(name, list(shape), F32, kind=kind).ap()
    x_d = dram("x", [T, D]); ctx_d = dram("ctx", [CL, D]); c_d = dram("c", [D]); cc_d = dram("c_ctx", [D])
    wmod_d = dram("w_mod", [L, D, 6 * D]); bmod_d = dram("b_mod", [L, 6 * D]); gattn_d = dram("g_attn", [L, D])
    win_d = dram("w_in", [L, D, DIN]); conv_d = dram("gdn_conv_w", [L, 5, 1536])
    alog_d = dram("gdn_a_log", [L, 16]); dtb_d = dram("gdn_dt_bias", [L, 16]); gng_d = dram("gdn_norm_g", [L, 64])
    gaq_d = dram("ga_q_norm_g", [L, 64]); gak_d = dram("ga_k_norm_g", [L, 64])
    waq_d = dram("wa_q_norm_g", [L, 64]); wak_d = dram("wa_k_norm_g", [L, 64]); sink_d = dram("wa_sink", [L, 4])
    wout_d = dram("w_out", [L, D, D]); gmlp_d = dram("g_mlp", [L, D])
    w1_d = dram("w_mlp_in", [L, D, DFF]); w2_d = dram("w_mlp_out", [L, DFF, D])
    cst_d = dram("consts", [128, NCONST]); ropec_d = dram("ropec", [128, NT]); ropes_d = dram("ropes", [128, NT])
    wmask_d = dram("wmask", [128, 6 * 512])
    out_d = dram("out", [T, D], kind="ExternalOutput")
    dbg_d = {}
    for (nm, shape) in dbg:
        dbg_d[nm] = dram("dbg_" + nm, shape, kind="ExternalOutput")

    st = ExitStack()
    with st:
        K = KB(nc, st, stop=stop)
        P = K.P
        xT = K.sb("xT", [128, 8, NT], F32)
        xnT = K.sb("xnT", [128, 8, NT], BF16)
        cst = K.sb("cst", [128, NCONST], F32)
        cstR = K.sb("cstR", [128, 256], F32)
        dbl = K.sb("dbl", [128, 6 * 512], F32)
        cb = K.sb("cb", [128, 512], BF16)
        modT = [K.sb("modT%d" % l, [128, 48, 2], F32) for l in range(L)]
        gsA = K.sb("gsA", [128, 8, 2], F32); gsM = K.sb("gsM", [128, 8, 2], F32)
        small = K.sb("small", [128, 64], F32)
        AW = (nc.sbuf_bytes_remaining - 2048) // 4
        arena_t = K.sb("arena", [128, AW], F32)
        AR = Arena(arena_t, AW)
        ps = st.enter_context(nc.psum_tensor("ps", [128, 4096], F32))

        def bank(i):
            return ps[:, i * 512:(i + 1) * 512]

        ident = cst[:, C_ID:C_ID + 128]
        identb = cb[:, 0:128]; onesb = cb[:, 128:256]; blkb = cb[:, 256:384]
        eps_norm = small[:, 0:1]
        one_c = small[:, 1:2]

        def dump(name, ap_src, dst=None):
            if name in dbg_d:
                K.dma(dbg_d[name] if dst is None else dst, ap_src, lane="dbg_" + name, track_out=False, eng="pool")

        K.dma(cst[:], cst_d, lane="cst")
        K.copy("dve", R_(cstR[:]), cst[:, C_AF:C_AF + 256])
        K.copy("dve", identb, cst[:, C_ID:C_ID + 128])
        K.copy("dve", onesb, cst[:, C_ONES:C_ONES + 128])
        K.copy("dve", blkb, cst[:, C_BLK:C_BLK + 128])
        K.memset("dve", eps_norm, 1e-6)
        K.memset("dve", one_c, 1.0)

        m0 = AR.mark()
        xin = [AR.f32(1024) for _ in range(2)]
        for t in range(NTILE):
            src = ctx_d[t * 128:(t + 1) * 128, :] if t < 2 else x_d[(t - 2) * 128:(t - 1) * 128, :]
            xi = xin[t % 2]
            K.dma(xi, src, lane="xin%d" % (t % 2))
            pb = ps[:, (t % 2) * 1024:(t % 2) * 1024 + 1024]
            for cch in range(8):
                K.tr(pb[:, cch * 128:(cch + 1) * 128], xi[:, cch * 128:(cch + 1) * 128], ident)
            K.copy("act" if t % 2 else "dve", xT[:, :, t * 128:(t + 1) * 128],
                   pb.rearrange("p (c n) -> p c n", c=8))
        AR.release(m0)

        m0 = AR.mark()
        craw = AR.f32(16).rearrange("p (k r) -> p k r", r=2)
        K.dma(craw[:, :, 0], c_d.rearrange("(k p) -> p k", p=128), lane="c0", slow=True)
        K.dma(craw[:, :, 1], cc_d.rearrange("(k p) -> p k", p=128), lane="c1", slow=True)
        scb = AR.bf16(16).rearrange("p (k r) -> p k r", r=2)
        K.act(scb, craw, AF.Silu)
        modrow = AR.f32(6144)
        brow = AR.f32(6144)
        wmb = [AR.bf16(8 * 512).rearrange("p (k n) -> p k n", k=8) for _ in range(2)]
        for l in range(nlayers):
            K.dma(brow[0:2, :], bmod_d[l:l + 1, :].partition_broadcast(2).rearrange("p o n -> p (o n)"), lane="brow")
            for j in range(12):
                wb = wmb[j % 2]
                K.dma(wb, wmod_d[l].rearrange("(k p) n -> p k n", p=128)[:, :, j * 512:(j + 1) * 512],
                      lane="wmb%d" % (j % 2), eng="pool")
                pb = bank(j % 2)
                for k in range(8):
                    K.mm(pb[0:2, :], scb[:, k, :], wb[:, k, :], start=(k == 0), stop=(k == 7))
                K.tt("dve", modrow[0:2, j * 512:(j + 1) * 512], pb[0:2, :], brow[0:2, j * 512:(j + 1) * 512], ALU.add)
            pb = bank(2)
            for ch in range(48):
                K.tr(pb[:, ch * 2:ch * 2 + 2], modrow[0:2, ch * 128:(ch + 1) * 128], ident[0:2, 0:2])
            K.copy("dve", modT[l][:], pb[:, 0:96].rearrange("p (c r) -> p c r", r=2))
            dump("modT%d" % l, modT[l][:])
        AR.release(m0)

        def norm_blocks(l, which, blocks):
            gs = gsA if which == 0 else gsM
            shoff = 0 if which == 0 else 24
            m1 = AR.mark()
            sqb = AR.bf16(8 * 512).rearrange("p (c n) -> p c n", c=8)
            rs = AR.f32(512); rstd = AR.f32(512)
            tmp = [AR.f32(512) for _ in range(2)]
            for (a0, n) in blocks:
                r = 1 if a0 < CL else 0
                K.act(sqb[:, :, 0:n], xT[:, :, a0:a0 + n], AF.Square)
                pb = bank(7)
                for cch in range(8):
                    K.mm(pb[:, 0:n], onesb, sqb[:, cch, 0:n], start=(cch == 0), stop=(cch == 7))
                K.act(rs[:, 0:n], pb[:, 0:n], AF.Sqrt, bias=eps_norm, scale=1.0 / D)
                K.recip(rstd[:, 0:n], rs[:, 0:n])
                for cch in range(8):
                    tm = tmp[cch % 2]
                    K.stt("dve", tm[:, 0:n], xT[:, cch, a0:a0 + n], gs[:, cch, r:r + 1], rstd[:, 0:n], ALU.mult, ALU.mult)
                    K.act(xnT[:, cch, a0:a0 + n], tm[:, 0:n], AF.Identity, bias=modT[l][:, shoff + cch, r:r + 1])
            AR.release(m1)

        def load_cols(dst, wd, l, col0, ncols, lane, nk=8):
            K.dma(dst, wd[l].rearrange("(k p) n -> p k n", p=128)[:, :, col0:col0 + ncols], lane=lane, eng="pool")

        def proj_fm(pb, w, blk):
            a0, n = blk
            for k in range(8):
                K.mm(pb[:, 0:n], w[:, k, :], xnT[:, k, a0:a0 + n], start=(k == 0), stop=(k == 7))

        def resid_add(l, which, pb, cch, blk):
            a0, n = blk
            r = 1 if a0 < CL else 0
            goff = 16 if which == 0 else 40
            K.stt("dve", xT[:, cch, a0:a0 + n], pb[:, 0:n], modT[l][:, goff + cch, r:r + 1], xT[:, cch, a0:a0 + n],
                  ALU.mult, ALU.add)

        for l in range(nlayers):
            last = (l == L - 1)
            oblocks = BLOCKS[1:] if last else BLOCKS
            m_layer = AR.mark()
            gT = AR.f32(16).rearrange("p (k r) -> p k r", r=2)
            K.dma(gT[:, :, 0], gattn_d[l].rearrange("(k p) -> p k", p=128), lane="gT0", slow=True)
            K.dma(gT[:, :, 1], gmlp_d[l].rearrange("(k p) -> p k", p=128), lane="gT1", slow=True)
            for r in range(2):
                K.stt("dve", gsA[:, :, r], modT[l][:, 8:16, r], 1.0, gT[:, :, 0], ALU.add, ALU.mult)
                K.stt("dve", gsM[:, :, r], modT[l][:, 32:40, r], 1.0, gT[:, :, 1], ALU.add, ALU.mult)
            norm_blocks(l, 0, BLOCKS)
            if l == 0:
                dump("xnT", xnT[:, :, :])
            if stop == "norm":
                break

            m_gdn = AR.mark()
            gates = AR.f32(18 * 16 * 6)
            m1 = AR.mark()
            gates2 = AR.f32(18 * 16 * 2)
            gv = lambda i: (gates[:, i * 288:(i + 1) * 288] if i < 6 else gates2[:, (i - 6) * 288:(i - 5) * 288]).rearrange("p (t c) -> p t c", c=16)
            beta_tm, nbeta_tm, g_tm, eG_tm, kes_tm, eGl_rep, Gam_tm, Gl_rep = [gv(i) for i in range(8)]
            wg = AR.bf16(8 * 32).rearrange("p (k n) -> p k n", k=8)
            load_cols(wg, win_d, l, 2048, 32, "wg")
            graw = AR.f32(18 * 32).rearrange("p (t c) -> p t c", c=32)
            pb = bank(0)
            for t in range(NTILE):
                for k in range(8):
                    K.mm(pb[:, t * 32:(t + 1) * 32] if t < 16 else bank(1)[:, (t - 16) * 32:(t - 15) * 32],
                         xnT[:, k, t * 128:(t + 1) * 128], wg[:, k, :], start=(k == 0), stop=(k == 7))
            K.copy("dve", graw[:, 0:16, :], pb.rearrange("p (t c) -> p t c", c=32))
            K.copy("dve", graw[:, 16:18, :], bank(1)[:, 0:64].rearrange("p (t c) -> p t c", c=32))
            rep = AR.f32(32)
            K.dma(rep[:, 0:16], dtb_d[l:l + 1, :].partition_broadcast(128).rearrange("p o n -> p (o n)"), lane="rep0")
            K.dma(rep[:, 16:32], alog_d[l:l + 1, :].partition_broadcast(128).rearrange("p o n -> p (o n)"), lane="rep1")
            negA = AR.f32(16)
            K.act(negA, rep[:, 16:32], AF.Exp)
            K.ts("dve", negA, negA, -1.0, ALU.mult)
            K.act(beta_tm, graw[:, :, 0:16], AF.Sigmoid)
            K.ts("dve", nbeta_tm, beta_tm, -1.0, ALU.mult)
            xa = AR.f32(288).rearrange("p (t c) -> p t c", c=16)
            ab = AR.f32(288).rearrange("p (t c) -> p t c", c=16)
            K.tt("dve", xa, graw[:, :, 16:32], rep[:, 0:16].unsqueeze(1).to_broadcast([128, 18, 16]), ALU.add)
            K.ts("dve", ab, xa, -1.0, ALU.mult)
            K.tt("dve", ab, ab, xa, ALU.max)
            K.act(ab, ab, AF.Exp, scale=-1.0)
            K.act(ab, ab, AF.Ln, bias=one_c)
            K.ts("dve", xa, xa, 0.0, ALU.max)
            K.tt("dve", xa, xa, ab, ALU.add)
            K.tt("dve", g_tm, xa, negA.unsqueeze(1).to_broadcast([128, 18, 16]), ALU.mult)
            pb = bank(2)
            for n_ in range(NTILE):
                for d_ in range(2):
                    Bm = cst[:, C_BF:C_BF + 128] if d_ == 0 else cst[:, C_BB:C_BB + 128]
                    K.mm(pb[:, n_ * 16 + d_ * 8:n_ * 16 + d_ * 8 + 8], Bm, g_tm[:, n_, d_ * 8:d_ * 8 + 8])
            K.copy("dve", Gam_tm, pb[:, 0:288].rearrange("p (t c) -> p t c", c=16))
            pb = bank(3)
            K.mm(pb[:, 0:288], cst[:, C_ONES:C_ONES + 128], gates[:, 2 * 288:3 * 288])
            K.copy("dve", Gl_rep, pb[:, 0:288].rearrange("p (t c) -> p t c", c=16))
            K.tt("dve", kes_tm, Gl_rep, Gam_tm, ALU.subtract)
            K.ts("dve", kes_tm, kes_tm, -40.0, ALU.max)
            K.act(kes_tm, kes_tm, AF.Exp)
            K.ts("dve", eG_tm, Gam_tm, -40.0, ALU.max)
            K.act(eG_tm, eG_tm, AF.Exp)
            K.ts("dve", eGl_rep, Gl_rep, -40.0, ALU.max)
            K.act(eGl_rep, eGl_rep, AF.Exp)
            if l == 0:
                dump("g_tm", g_tm); dump("beta_tm", beta_tm)
            AR.release(m1)
            convT = AR.f32(60).rearrange("p (q j) -> p q j", j=5)
            for j in range(5):
                K.dma(convT[:, :, j], conv_d[l][j].rearrange("(q p) -> p q", p=128), lane="convT%d" % j, slow=True)
            gng = AR.f32(64)
            K.dma(gng, gng_d[l:l + 1, :].partition_broadcast(128).rearrange("p o n -> p (o n)"), lane="gng")
            S_all = AR.f32(128)
            pgt = AR.f32(6 * 72)
            pgv = lambda i: pgt[:, i * 72:(i + 1) * 72].rearrange("p (t c) -> p t c", c=4)
            p_beta, p_nbeta, p_g, p_eG, p_kes, p_eGl = [pgv(i) for i in range(6)]
            qT = AR.f32(NT); kT = AR.f32(NT); vT = AR.f32(NT)
            o_acc = AR.f32(NT)
            o_acc3 = o_acc.rearrange("p (t c) -> p t c", c=128)
            wq4 = AR.bf16(8 * 3 * 128).rearrange("p (k t n) -> p k t n", k=8, t=3)
            m_ov = AR.mark()
            dg = AR.bf16(5 * 128).rearrange("p (j n) -> p j n", j=5)
            raw = AR.bf16(2312)
            nsq = AR.bf16(512); nrs = AR.f32(512); nrstd = AR.f32(512)
            AR.release(m_ov)
            zsT = AR.bf16(NT); oTp = AR.bf16(NT); wo = AR.bf16(1024); ors = AR.f32(36); orstd = AR.f32(36)
            AR.release(m_ov)
            def f512():
                return AR.f32(512)
            gB = f512(); iT = f512()
            PTb = [dbl[:, 0:512], dbl[:, 512:1024]]; Pb = [dbl[:, 1024:1536], dbl[:, 1536:2048]]
            Yb = [dbl[:, 2048:2560], dbl[:, 2560:3072]]
            dI = gB; Eb = f512(); dS = f512()
            kE = AR.f32(256); kend = AR.f32(256); vt = AR.f32(256); ub = AR.f32(256); vn = AR.f32(256)
            wT = f512()
            Q64 = f512(); Q128 = f512()
            print("arena used at GDN:", AR.off, "of", AR.words)

            def rawpos(a0):
                return a0 + 2 if a0 < CL else a0 + 6

            import os
            _pairs = [int(v) for v in os.environ.get('PAIRS', '0,1,2,3').split(',') if v != 'none']
            for pr in _pairs:
                for src_, dst_ in ((beta_tm, p_beta), (nbeta_tm, p_nbeta), (g_tm, p_g), (eG_tm, p_eG), (kes_tm, p_kes), (eGl_rep, p_eGl)):
                    for d_ in range(2):
                        K.copy("dve", dst_[:, :, d_ * 2:d_ * 2 + 2], src_[:, :, d_ * 8 + 2 * pr:d_ * 8 + 2 * pr + 2])
                K.memset("pool", raw, 0.0)
                for ty in range(3):
                    K.dma(wq4[:, :, ty, :], win_d[l].rearrange("(k p) n -> p k n", p=128)[:, :, ty * 512 + pr * 128: ty * 512 + pr * 128 + 128],
                          lane="wq4_%d" % ty, eng="pool")
                for ty in range(3):
                    for (a0, n) in BLOCKS:
                        pb = bank(a0 // 512 % 2)
                        for k in range(8):
                            K.mm(pb[:, 0:n], wq4[:, k, ty, :], xnT[:, k, a0:a0 + n], start=(k == 0), stop=(k == 7))
                        K.copy("act", raw[:, rawpos(a0):rawpos(a0) + n], pb[:, 0:n])
                    for j in range(5):
                        K.ts("dve", dg[:, j, :], identb, convT[:, ty * 4 + pr, j:j + 1], ALU.mult)
                    dst = (qT, kT, vT)[ty]
                    for (a0, n) in BLOCKS:
                        pb = bank(2 + a0 // 512 % 2)
                        for j in range(5):
                            K.mm(pb[:, 0:n], dg[:, j, :], raw[:, rawpos(a0) + j - 2:rawpos(a0) + j - 2 + n],
                                 start=(j == 0), stop=(j == 4))
                        if ty == 2:
                            K.act(R_(vT[:, a0:a0 + n]), pb[:, 0:n], AF.Silu)
                        else:
                            K.act(o_acc[:, a0:a0 + n], pb[:, 0:n], AF.Silu)
                            K.act(nsq[:, 0:n], o_acc[:, a0:a0 + n], AF.Square)
                            pb2 = bank(4 + a0 // 512 % 2)
                            K.mm(pb2[:, 0:n], blkb, nsq[:, 0:n])
                            K.act(nrs[:, 0:n], pb2[:, 0:n], AF.Sqrt, bias=eps_norm, scale=1.0)
                            K.recip(nrstd[:, 0:n], nrs[:, 0:n])
                            K.stt("dve", R_(dst[:, a0:a0 + n]), o_acc[:, a0:a0 + n], 0.125 if ty == 0 else 1.0,
                                  nrstd[:, 0:n], ALU.mult, ALU.mult)
                if l == 0 and _pairs and pr == _pairs[0]:
                    dump("gq0", qT); dump("gk0", kT); dump("gv0", vT)
                if stop == "gdnproj":
                    break
                K.memset("dve", R_(S_all), 0.0)
                K.memset("pool", o_acc, 0.0)
                import os
                for s in range(int(os.environ.get('GDN_STEPS', NTILE))):
                    info = []
                    for b in range(4):
                        d_, m_ = b // 2, b % 2
                        n_ = ORDER[d_][s]
                        info.append((d_, m_, n_, d_ * 2 + m_))
                    bs = lambda b: slice(b * 128, (b + 1) * 128)
                    hs = lambda b: slice(b * 64, (b + 1) * 64)
                    for b, (d_, m_, n_, col) in enumerate(info):
                        tok = slice(n_ * 128, (n_ + 1) * 128); rows = slice(64 * m_, 64 * m_ + 64)
                        K.mm(bank(m_)[:, d_ * 128:(d_ + 1) * 128], R_(kT[rows, tok]), R_(kT[rows, tok]))
                        K.mm(bank(m_)[:, 256 + d_ * 128:256 + (d_ + 1) * 128], R_(kT[rows, tok]), R_(qT[rows, tok]))
                    for b, (d_, m_, n_, col) in enumerate(info):
                        Bm = cst[:, C_BF:C_BF + 128] if d_ == 0 else cst[:, C_BB:C_BB + 128]
                        K.ts("dve", R_(gB[:, bs(b)]), Bm, p_g[:, n_, col:col + 1], ALU.mult)
                        Am = cstR[:, 0:128] if d_ == 0 else cstR[:, 128:256]
                        K.mm(bank(2)[:, bs(b)], R_(Am), R_(gB[:, bs(b)]))
                    if (os.environ.get('CUTALL') or s == int(os.environ.get('CUTSTEP', 0))) and int(os.environ.get('CUT', 99)) == 1:
                        continue
                    if os.environ.get('PHASEBAR'):
                        K.barrier()
                    K.ts("dve", Eb, bank(2), -40.0, ALU.max)
                    K.act(Eb, Eb, AF.Exp)
                    K.tt("dve", dI, Eb, cst[:, C_MINC:C_MINC + 512], ALU.mult)
                    K.tt("dve", dS, Eb, cst[:, C_MSTR:C_MSTR + 512], ALU.mult)
                    PT, Pm, Y = PTb[0], Pb[0], Yb[0]
                    for b, (d_, m_, n_, col) in enumerate(info):
                        K.stt("dve", RR(PT[:, bs(b)]), bank(m_)[:, d_ * 128:(d_ + 1) * 128], p_nbeta[:, n_, col:col + 1], dS[:, bs(b)],
                              ALU.mult, ALU.mult)
                    for b, (d_, m_, n_, col) in enumerate(info):
                        K.tt("dve", R_(iT[:, bs(b)]), bank(m_)[:, 256 + d_ * 128:256 + (d_ + 1) * 128], dI[:, bs(b)], ALU.mult)
                    if (os.environ.get('CUTALL') or s == int(os.environ.get('CUTSTEP', 0))) and int(os.environ.get('CUT', 99)) == 2:
                        continue
                    if os.environ.get('PHASEBAR'):
                        K.barrier()
                    for b in range(4):
                        K.tr(bank(3)[:, bs(b)], PT[:, bs(b)], ident)
                    K.copy("dve", RR(Pm), bank(3))
                    v4 = lambda t_: t_.rearrange("p (b n) -> p b n", b=4)
                    mk = lambda c0: cst[:, c0:c0 + 128].unsqueeze(1).to_broadcast([128, 4, 128])
                    K.tt("dve", v4(Q64), v4(Pm), mk(C_B64), ALU.mult)
                    K.tt("dve", v4(Q128), v4(Pm), mk(C_BOFF), ALU.mult)
                    K.tt("dve", v4(RR(PT)), v4(PT), mk(C_B32), ALU.mult)
                    K.tt("dve", v4(RR(Pm)), v4(Pm), mk(C_B32), ALU.mult)
                    K.tt("dve", RR(Y), PT, cst[:, C_ID4:C_ID4 + 512], ALU.add)
                    if (os.environ.get('CUTALL') or s == int(os.environ.get('CUTSTEP', 0))) and int(os.environ.get('CUT', 99)) == 3:
                        continue
                    if os.environ.get('PHASEBAR'):
                        K.barrier()
                    for d_ in range(2):
                        n_ = ORDER[d_][s]
                        tok = slice(n_ * 128, (n_ + 1) * 128)
                        K.tr(bank(6)[:, d_ * 128:(d_ + 1) * 128], kT[:, tok], ident)
                        K.tr(bank(6)[:, 256 + d_ * 128:256 + (d_ + 1) * 128], vT[:, tok], ident)
                    K.copy("dve", R_(vt), bank(6)[:, 256:512])
                    for b, (d_, m_, n_, col) in enumerate(info):
                        if os.environ.get("P3") == "noke":
                            continue
                        if os.environ.get("P3") == "const":
                            K.ts("dve", R_(kE[:, hs(b)]), bank(6)[:, hs(b)], 0.5, ALU.mult)
                            K.ts("dve", R_(kend[:, hs(b)]), bank(6)[:, hs(b)], 0.5, ALU.mult)
                            continue
                        K.ts("dve", R_(kE[:, hs(b)]), bank(6)[:, hs(b)], p_eG[:, n_, col:col + 1], ALU.mult)
                        K.ts("dve", R_(kend[:, hs(b)]), bank(6)[:, hs(b)], p_kes[:, n_, col:col + 1], ALU.mult)
                    if (os.environ.get('CUTALL') or s == int(os.environ.get('CUTSTEP', 0))) and int(os.environ.get('CUT', 99)) == 4:
                        continue
                    if os.environ.get('PHASEBAR'):
                        K.barrier()
                    cur = 0
                    for kk in range(1, 5):
                        PTn, Pn, Yn = PTb[1 - cur], Pb[1 - cur], Yb[1 - cur]
                        PTc, Pc, Yc = PTb[cur], Pb[cur], Yb[cur]
                        for b in range(4):
                            K.mm(bank(4)[:, bs(b)], RR(Pc[:, bs(b)]), RR(PTc[:, bs(b)]))
                        for b in range(4):
                            K.mm(bank(5)[:, bs(b)], RR(PTc[:, bs(b)]), RR(Pc[:, bs(b)]))
                        K.copy("dve", RR(PTn), bank(4))
                        K.copy("dve", RR(Pn), bank(5))
                        for b in range(4):
                            K.mm(bank(7)[:, bs(b)], RR(Pn[:, bs(b)]), RR(Yc[:, bs(b)]))
                        K.tt("dve", RR(Yn), bank(7), Yc, ALU.add)
                        if os.environ.get('PHASEBAR'):
                            K.barrier()
                        cur = 1 - cur
                    for Qm in (Q64, Q128):
                        Yc = Yb[cur]; Yn = Yb[1 - cur]
                        for b in range(4):
                            K.tr(bank(4)[:, bs(b)], Yc[:, bs(b)], ident)
                        K.copy("act", dS, bank(4))
                        for b in range(4):
                            K.mm(bank(5)[:, bs(b)], Qm[:, bs(b)], Yc[:, bs(b)])
                        K.copy("dve", Eb, bank(5))
                        for b in range(4):
                            K.mm(bank(7)[:, bs(b)], dS[:, bs(b)], Eb[:, bs(b)])
                        K.tt("dve", RR(Yn), bank(7), Yc, ALU.add)
                        cur = 1 - cur
                    Y = Yb[cur]
                    if (os.environ.get('CUTALL') or s == int(os.environ.get('CUTSTEP', 0))) and int(os.environ.get('CUT', 99)) == 5:
                        continue
                    if os.environ.get('PHASEBAR'):
                        K.barrier()
                    for b, (d_, m_, n_, col) in enumerate(info):
                        K.mm(bank(0)[:, hs(b)], R_(Y[:, bs(b)]), R_(vt[:, hs(b)]))
                        K.mm(bank(1)[:, bs(b)], R_(kE[:, d_ * 128:(d_ + 1) * 128]), R_(Y[:, bs(b)]))
                    if (os.environ.get('CUTALL') or s == int(os.environ.get('CUTSTEP', 0))) and int(os.environ.get('CUT', 99)) == 51:
                        continue
                    if os.environ.get('PHASEBAR'):
                        K.barrier()
                    for b, (d_, m_, n_, col) in enumerate(info):
                        K.ts("dve", ub[:, hs(b)], bank(0)[:, hs(b)], p_beta[:, n_, col:col + 1], ALU.mult)
                    if (os.environ.get('CUTALL') or s == int(os.environ.get('CUTSTEP', 0))) and int(os.environ.get('CUT', 99)) == 52:
                        continue
                    if os.environ.get('PHASEBAR'):
                        K.barrier()
                    for b, (d_, m_, n_, col) in enumerate(info):
                        rows = slice(64 * m_, 64 * m_ + 64)
                        cc_ = int(os.environ.get('CUT2', 0))
                        if (cc_ == 1 and m_ == 1) or (cc_ == 2 and m_ == 0):
                            continue
                        if m_ == 0 and os.environ.get('ACTCOPY'):
                            K.act(wT[rows, bs(b)], bank(1)[rows, bs(b)], AF.Copy)
                        else:
                            K.copy("dve", wT[rows, bs(b)], bank(1)[rows, bs(b)])
                    if (os.environ.get('CUTALL') or s == int(os.environ.get('CUTSTEP', 0))) and int(os.environ.get('CUT', 99)) == 6:
                        continue
                    if os.environ.get('PHASEBAR'):
                        K.barrier()
                    if os.environ.get('PHASEBAR'):
                        K.barrier()
                    for b, (d_, m_, n_, col) in enumerate(info):
                        rows = slice(64 * m_, 64 * m_ + 64)
                        Sb = S_all[rows, d_ * 64:(d_ + 1) * 64]
                        K.mm(bank(2 + 2 * m_)[:, hs(b)], R_(wT[rows, bs(b)]), R_(Sb))
                    if os.environ.get('PHASEBAR'):
                        K.barrier()
                    if os.environ.get('CUTALL') and int(os.environ.get('CUT', 99)) == 8:
                        continue
                    for b, (d_, m_, n_, col) in enumerate(info):
                        K.stt("dve", R_(vn[:, hs(b)]), bank(2 + 2 * m_)[:, hs(b)], p_nbeta[:, n_, col:col + 1], ub[:, hs(b)],
                              ALU.mult, ALU.add)
                    if os.environ.get('PHASEBAR'):
                        K.barrier()
                    if os.environ.get('CUTALL') and int(os.environ.get('CUT', 99)) == 9:
                        continue
                    for b, (d_, m_, n_, col) in enumerate(info):
                        tok = slice(n_ * 128, (n_ + 1) * 128); rows = slice(64 * m_, 64 * m_ + 64)
                        Sb = S_all[rows, d_ * 64:(d_ + 1) * 64]
                        if not (last and n_ < 2):
                            K.mm(bank(3 + 2 * m_)[:, hs(b)], R_(qT[rows, tok]), R_(Sb))
                    for b, (d_, m_, n_, col) in enumerate(info):
                        if not (last and n_ < 2):
                            K.mm(bank(7)[:, hs(b)], R_(iT[:, bs(b)]), R_(vn[:, hs(b)]))
                    for b, (d_, m_, n_, col) in enumerate(info):
                        K.mm(bank(6)[:, hs(b)], R_(kend[:, d_ * 128:(d_ + 1) * 128]), R_(vn[:, hs(b)]))
                    if os.environ.get('PHASEBAR'):
                        K.barrier()
                    if os.environ.get('CUTALL') and int(os.environ.get('CUT', 99)) == 10:
                        continue
                    for b, (d_, m_, n_, col) in enumerate(info):
                        rows = slice(64 * m_, 64 * m_ + 64)
                        Sb = S_all[rows, d_ * 64:(d_ + 1) * 64]
                        if not (last and n_ < 2):
                            oa = o_acc3[:, n_, 64 * m_:64 * m_ + 64]
                            K.tt("dve", oa, bank(7)[:, hs(b)], oa, ALU.add)
                            K.stt("dve", oa, bank(3 + 2 * m_)[:, hs(b)], p_eG[:, n_, col:col + 1], oa, ALU.mult, ALU.add)
                        K.stt("dve", R_(Sb), Sb, p_eGl[rows, n_, col:col + 1], bank(6)[rows, hs(b)], ALU.mult, ALU.add)
                    if os.environ.get("STEPBAR"):
                        K.barrier()
                if l == 0 and _pairs and pr == _pairs[0]:
                    dump("oacc0", o_acc)
                if stop == "gdnscan":
                    break
                sqf = qT
                K.act(sqf, o_acc, AF.Square)
                if int(os.environ.get('CUT3', 99)) == 1:
                    break
                ss = ors; rr = orstd
                P.add("dve", lambda e, ss=ss, sqf=sqf: e.reduce_sum(ss, sqf.rearrange("p (g c) -> p g c", c=64), AX.X),
                      reads=[sqf], writes=[ss])
                if int(os.environ.get('CUT3', 99)) == 2:
                    break
                K.act(ss, ss, AF.Sqrt, bias=eps_norm, scale=1.0 / 64)
                K.recip(rr, ss)
                if int(os.environ.get('CUT3', 99)) == 3:
                    break
                og = o_acc.rearrange("p (g c) -> p g c", c=64)
                K.tt("dve", og, og, rr.unsqueeze(2).to_broadcast([128, 36, 64]), ALU.mult)
                if int(os.environ.get('CUT3', 99)) == 4:
                    break
                K.tt("dve", og, og, gng.unsqueeze(1).to_broadcast([128, 36, 64]), ALU.mult)
                if int(os.environ.get('CUT3', 99)) == 5:
                    break
                K.dma(wq4[:, :, 0, :], win_d[l].rearrange("(k p) n -> p k n", p=128)[:, :, 1536 + pr * 128: 1536 + pr * 128 + 128],
                      lane="wq4_0", eng="pool")
                for (a0, n) in BLOCKS:
                    pb = bank(a0 // 512 % 2)
                    for k in range(8):
                        K.mm(pb[:, 0:n], wq4[:, k, 0, :], xnT[:, k, a0:a0 + n], start=(k == 0), stop=(k == 7))
                    K.act(zsT[:, a0:a0 + n], pb[:, 0:n], AF.Silu)
                if int(os.environ.get('CUT3', 99)) == 6:
                    break
                K.dma(wo, wout_d[l][pr * 128:(pr + 1) * 128, :], lane="wo", eng="pool")
                for (a0, n) in BLOCKS:
                    pb = bank(2 + a0 // 512 % 2)
                    for i_ in range(n // 128):
                        t_ = a0 // 128 + i_
                        K.tr(pb[:, i_ * 128:(i_ + 1) * 128], o_acc3[:, t_, :], ident)
                    K.tt("dve", oTp[:, a0:a0 + n], pb[:, 0:n], zsT[:, a0:a0 + n], ALU.mult)
                if l == 0 and _pairs and pr == _pairs[0]:
                    dump("oTp0", oTp)
                if int(os.environ.get('CUT3', 99)) == 7:
                    break
                for (a0, n) in oblocks:
                    for cch in range(8):
                        pb = bank(4 + cch % 4)
                        K.mm(pb[:, 0:n], wo[:, cch * 128:(cch + 1) * 128], oTp[:, a0:a0 + n])
                        resid_add(l, 0, pb, cch, (a0, n))
            AR.release(m_gdn)
            if stop in ("gdnproj", "gdnscan", "gdn"):
                break

            m_att = AR.mark()
            qTa = AR.bf16(2 * NT).rearrange("p (c n) -> p c n", c=2)
            kTa = AR.bf16(NT)
            vtm = AR.bf16(18 * 2 * 66).rearrange("p (t k c) -> p t k c", t=18, k=2)
            oTa = AR.bf16(4 * NT).rearrange("p (h n) -> p h n", h=4)
            gcol = AR.f32(2); esk = AR.f32(4)
            m_aov = AR.mark()
            cosb = [AR.f32(512) for _ in range(2)]; sinb = [AR.f32(512) for _ in range(2)]
            wqk = AR.bf16(8 * 3 * 128).rearrange("p (k c n) -> p k c n", k=8, c=3)
            wv = AR.bf16(8 * 128).rearrange("p (k n) -> p k n", k=8)
            Rg = AR.f32(2 * 128)
            qraw = AR.f32(512); t1 = AR.f32(512); t2 = AR.f32(512); ars = AR.f32(512); arstd = AR.f32(512)
            asq = AR.bf16(512)
            AR.release(m_aov)
            wmk = AR.bf16(6 * 512)
            wo4 = AR.bf16(4 * 1024).rearrange("p (h n) -> p h n", h=4)
            PTs = [AR.bf16(512) for _ in range(3)]
            rden = AR.f32(512); rdr = AR.f32(512)
            print("arena used at ATT:", AR.off, "of", AR.words)
            for grp in range(2):
                base = BASE_B if grp == 0 else BASE_C
                gq_d, gk_d = (gaq_d, gak_d) if grp == 0 else (waq_d, wak_d)
                wrows = 512 if grp == 0 else 768
                wv_in = win_d[l].rearrange("(k p) n -> p k n", p=128)
                for ci, heads in enumerate(((0, 2), (1, 3))):
                    for hi, h in enumerate(heads):
                        K.dma(wqk[:, :, ci, hi * 64:(hi + 1) * 64], wv_in[:, :, base + h * 64: base + h * 64 + 64],
                              lane="wqk%d%d" % (ci, hi), eng="pool")
                K.dma(wqk[:, :, 2, :], wv_in[:, :, base + 256: base + 384], lane="wqk2", eng="pool")
                K.dma(wv, wv_in[:, :, base + 384: base + 512], lane="wv", eng="pool")
                for hh in range(2):
                    K.dma(gcol[hh * 64:(hh + 1) * 64, 0:1], gq_d[l].rearrange("(p o) -> p o", o=1), lane="gq%d" % hh, slow=True)
                    K.dma(gcol[hh * 64:(hh + 1) * 64, 1:2], gk_d[l].rearrange("(p o) -> p o", o=1), lane="gk%d" % hh, slow=True)
                for qk in range(2):
                    K.ts("dve", R_(Rg[:, qk * 128:(qk + 1) * 128]), cst[:, C_R:C_R + 128], gcol[:, qk:qk + 1], ALU.mult)
                if grp == 1:
                    K.dma(esk[64:65, 0:4], sink_d[l:l + 1, :], lane="sink")
                    K.act(esk[64:65, 0:4], esk[64:65, 0:4], AF.Exp)
                K.memset("pool", vtm[:, :, :, 64:65], 1.0)
                for t in range(NTILE):
                    pb = bank(t % 2)
                    for k in range(8):
                        K.mm(pb[:, 0:128], xnT[:, k, t * 128:(t + 1) * 128], wv[:, k, :], start=(k == 0), stop=(k == 7))
                    K.copy("act", vtm[:, t, :, 0:64], pb[:, 0:128].rearrange("p (k c) -> p k c", k=2))
                rpi = 0
                for ci in range(3):
                    qk = 0 if ci < 2 else 1
                    for (a0, n) in BLOCKS:
                        cosT_b = cosb[rpi % 2]; sinT_b = sinb[rpi % 2]
                        K.dma(cosT_b[:, 0:n], ropec_d[:, a0:a0 + n], lane="cos%d" % (rpi % 2))
                        K.dma(sinT_b[:, 0:n], ropes_d[:, a0:a0 + n], lane="sin%d" % (rpi % 2))
                        rpi += 1
                        pb = bank(2 + a0 // 512 % 2)
                        for k in range(8):
                            K.mm(pb[:, 0:n], wqk[:, k, ci, :], xnT[:, k, a0:a0 + n], start=(k == 0), stop=(k == 7))
                        K.copy("act", R_(qraw[:, 0:n]), pb[:, 0:n])
                        K.act(asq[:, 0:n], pb[:, 0:n], AF.Square)
                        pb2 = bank(4 + a0 // 512 % 2)
                        K.mm(pb2[:, 0:n], blkb, asq[:, 0:n])
                        K.act(ars[:, 0:n], pb2[:, 0:n], AF.Sqrt, bias=eps_norm, scale=1.0 / 64)
                        K.recip(arstd[:, 0:n], ars[:, 0:n])
                        pb3 = bank(6 + a0 // 512 % 2)
                        K.mm(pb3[:, 0:n], R_(Rg[:, qk * 128:(qk + 1) * 128]), R_(qraw[:, 0:n]))
                        K.stt("dve", t1[:, 0:n], qraw[:, 0:n], gcol[:, qk:qk + 1], cosT_b[:, 0:n], ALU.mult, ALU.mult)
                        K.tt("dve", t2[:, 0:n], pb3[:, 0:n], sinT_b[:, 0:n], ALU.mult)
                        K.tt("dve", t1[:, 0:n], t1[:, 0:n], t2[:, 0:n], ALU.add)
                        dst = qTa[:, ci, a0:a0 + n] if ci < 2 else kTa[:, a0:a0 + n]
                        K.tt("dve", dst, t1[:, 0:n], arstd[:, 0:n], ALU.mult)
                if l == 0:
                    dump("qTa%d" % grp, qTa); dump("kTa%d" % grp, kTa)
                K.dma(wo4[0:64], wout_d[l][wrows:wrows + 256, :].rearrange("(h p) n -> p h n", p=64), lane="wo4", eng="pool")
                K.dma(wmk, wmask_d, lane="wmk", eng="pool")
                pti = 0
                for h in range(4):
                    ci = h % 2; kv = h // 2; rows = slice(64 * kv, 64 * kv + 64)
                    qblocks = [(256 + 512 * i, 512) for i in range(4)] + ([] if last else [(0, 256)])
                    for (a0, n) in qblocks:
                        isctx = a0 < CL
                        if isctx:
                            kts = [(0, None), (1, None)]
                        elif grp == 0:
                            kts = [(t, None) for t in range(NTILE)]
                        else:
                            qb = (a0 - 256) // 512
                            kts = [(0, None), (1, None)]
                            for rel in range(-1, 5):
                                lt = 4 * qb + rel
                                if 0 <= lt < 16:
                                    kts.append((lt + 2, rel + 1))
                        ob = bank(4 + (pti % 2))
                        for i_, (kt, mi) in enumerate(kts):
                            sb_ = bank(pti % 3)
                            PTt = PTs[pti % 3]; pti += 1
                            K.mm(sb_[:, 0:n], kTa[rows, kt * 128:(kt + 1) * 128], qTa[rows, ci, a0:a0 + n])
                            K.act(PTt[:, 0:n], sb_[:, 0:n], AF.Exp, scale=0.125)
                            if mi is not None:
                                K.tt("dve", PTt[:, 0:n], PTt[:, 0:n], wmk[:, mi * 512:mi * 512 + n], ALU.mult)
                            K.mm(ob[0:65, 0:n], vtm[:, kt, kv, 0:65], PTt[:, 0:n], start=(i_ == 0), stop=(i_ == len(kts) - 1))
                        if grp == 1:
                            K.ts("dve", rden[64:65, 0:n], ob[64:65, 0:n], esk[64:65, h:h + 1], ALU.add)
                            K.recip(rden[64:65, 0:n], rden[64:65, 0:n])
                        else:
                            K.recip(rden[64:65, 0:n], ob[64:65, 0:n])
                        rb_ = bank(6 + (pti % 2))
                        K.mm(rb_[0:64, 0:n], cst[64:65, C_ONES:C_ONES + 64], rden[64:65, 0:n])
                        K.copy("dve", rdr[0:64, 0:n], rb_[0:64, 0:n])
                        K.tt("dve", oTa[0:64, h, a0:a0 + n], ob[0:64, 0:n], rdr[0:64, 0:n], ALU.mult)
                if l == 0:
                    dump("oTa%d" % grp, oTa[0:64, :, :])
                for (a0, n) in oblocks:
                    for cch in range(8):
                        pb = bank(cch % 4)
                        for h in range(4):
                            K.mm(pb[:, 0:n], wo4[0:64, h, cch * 128:(cch + 1) * 128], oTa[0:64, h, a0:a0 + n],
                                 start=(h == 0), stop=(h == 3))
                        resid_add(l, 0, pb, cch, (a0, n))
            AR.release(m_att)
            if l == 0:
                dump("xT_attn", xT[:, :, :])
            if stop == "attn":
                break

            m_mlp = AR.mark()
            norm_blocks(l, 1, oblocks)
            h1 = AR.bf16(32 * 512).rearrange("p (f n) -> p f n", f=32)
            hr = [AR.bf16(512) for _ in range(2)]
            w1b = [AR.bf16(8 * 512).rearrange("p (k n) -> p k n", k=8) for _ in range(2)]
            w2b = [AR.bf16(32 * 128).rearrange("p (f n) -> p f n", f=32) for _ in range(2)]
            w1i = 0; w2i = 0
            for (a0, n) in oblocks:
                for fb in range(8):
                    w1 = w1b[w1i % 2]
                    load_cols(w1, w1_d, l, fb * 512, 512, "w1b%d" % (w1i % 2)); w1i += 1
                    for f4 in range(4):
                        f = fb * 4 + f4
                        pb = bank(f % 4)
                        for k in range(8):
                            K.mm(pb[:, 0:n], w1[:, k, f4 * 128:(f4 + 1) * 128], xnT[:, k, a0:a0 + n], start=(k == 0), stop=(k == 7))
                        hb = hr[f % 2]
                        K.act(hb[:, 0:n], pb[:, 0:n], AF.Relu)
                        K.tt("dve", h1[:, f, 0:n], hb[:, 0:n], hb[:, 0:n], ALU.mult)
                for cch in range(8):
                    w2 = w2b[w2i % 2]
                    K.dma(w2, w2_d[l].rearrange("(f p) n -> p f n", p=128)[:, :, cch * 128:(cch + 1) * 128],
                          lane="w2b%d" % (w2i % 2), eng="pool"); w2i += 1
                    pb = bank(4 + cch % 4)
                    for f in range(32):
                        K.mm(pb[:, 0:n], w2[:, f, :], h1[:, f, 0:n], start=(f == 0), stop=(f == 31))
                    resid_add(l, 1, pb, cch, (a0, n))
            AR.release(m_mlp)
            AR.release(m_layer)
            if l == 0:
                dump("xT_l0", xT[:, :, :])

        m0 = AR.mark()
        xo = [AR.f32(1024) for _ in range(2)]
        outs = []
        for t in range(16):
            a0 = CL + t * 128
            pb = ps[:, (t % 2) * 1024:(t % 2) * 1024 + 1024]
            for cch in range(8):
                K.tr(pb[:, cch * 128:(cch + 1) * 128], xT[:, cch, a0:a0 + 128], ident)
            K.copy("act" if t % 2 else "dve", xo[t % 2], pb)
            outs.append(K.dma(out_d[t * 128:(t + 1) * 128, :], xo[t % 2], lane="xo%d" % (t % 2), track_out=False))
        dbg_ops = [P.lane_last[k] for k in P.lane_last if str(k).startswith("dbg_")]
        P.add("sp", None, extra=outs + dbg_ops)
        print("ops:", len(P.ops), {e: len(v) for e, v in P.eng_ops.items()})
        P.finalize(st)
        print("waits:", P.n_wait)
    return nc


def _prep_common(inputs):
    f = lambda a: np.ascontiguousarray(np.asarray(a, dtype=np.float32))
    com = {
        "c_ctx": f(inputs["c_ctx"]), "w_mod": f(inputs["w_mod"]), "b_mod": f(inputs["b_mod"]),
        "g_attn": f(inputs["g_attn"]), "w_in": f(inputs["w_in"]), "gdn_conv_w": f(inputs["gdn_conv_w"]),
        "gdn_a_log": f(inputs["gdn_a_log"]).reshape(L, 16), "gdn_dt_bias": f(inputs["gdn_dt_bias"]).reshape(L, 16),
        "gdn_norm_g": f(inputs["gdn_norm_g"]), "ga_q_norm_g": f(inputs["ga_q_norm_g"]),
        "ga_k_norm_g": f(inputs["ga_k_norm_g"]), "wa_q_norm_g": f(inputs["wa_q_norm_g"]),
        "wa_k_norm_g": f(inputs["wa_k_norm_g"]), "wa_sink": f(inputs["wa_sink"]), "w_out": f(inputs["w_out"]),
        "g_mlp": f(inputs["g_mlp"]), "w_mlp_in": f(inputs["w_mlp_in"]), "w_mlp_out": f(inputs["w_mlp_out"]),
        "consts": make_consts(),
    }
    cosT, sinT = make_rope()
    com["ropec"] = cosT; com["ropes"] = sinT
    com["wmask"] = np.ascontiguousarray(make_wmask().transpose(1, 0, 2).reshape(128, 6 * 512))
    return com


def kernel(**inputs):
    nc = build(bass.Bass("TRN2", target_bir_lowering=False))
    com = _prep_common(inputs)
    x = np.asarray(inputs["x"], np.float32); c = np.asarray(inputs["c"], np.float32)
    ctx = np.asarray(inputs["ctx"], np.float32)
    in_maps = []
    for b in range(8):
        m = dict(com)
        m["x"] = np.ascontiguousarray(x[b]); m["ctx"] = np.ascontiguousarray(ctx[b]); m["c"] = np.ascontiguousarray(c[b])
        in_maps.append(m)
    res = run_bass_kernel_spmd(nc, in_maps, core_ids=list(range(8)))
    return np.stack([np.asarray(res.results[b]["out"], np.float32) for b in range(8)], axis=0)
```

```python
from concourse.bass_utils import run_bass_kernel_spmd
import numpy as np
import concourse.bass as bass
import concourse.mybir as mybir

F32 = mybir.dt.float32
BF16 = mybir.dt.bfloat16
F32R = mybir.dt.float32r
ALU = mybir.AluOpType
AF = mybir.ActivationFunctionType
AX = mybir.AxisListType

_ESZ = {}


def _esize(dt):
    if dt not in _ESZ:
        _ESZ[dt] = mybir.dt.size(dt) if hasattr(mybir.dt, "size") else None
    return _ESZ[dt]


def esize(dt):
    s = str(dt)
    if "float32" in s or "int32" in s:
        return 4
    if "bfloat16" in s or "float16" in s or "int16" in s:
        return 2
    if "int8" in s or "float8" in s:
        return 1
    raise ValueError(s)


def region(ap):
    pat = ap.ap
    es = esize(ap.dtype)
    pstep, pcnt = pat[0]
    off = ap.offset
    if pstep == 0:
        p0 = 0
        f0 = off
    else:
        p0 = off // pstep
        f0 = off % pstep
    ext = 1
    for st, cn in pat[1:]:
        ext += (cn - 1) * abs(st)
    b0, b1 = f0 * es, (f0 + ext) * es
    if ap.tensor.name == "ps":
        b0 = (b0 // 2048) * 2048
        b1 = ((b1 + 2047) // 2048) * 2048
        return (ap.tensor.name, 0, 128, b0, b1)
    return (ap.tensor.name, p0, p0 + pcnt, b0, b1)


def _overlap(a, b):
    return a[1] < b[2] and b[1] < a[2] and a[3] < b[4] and b[3] < a[4]


def _covers(a, b):
    return a[1] <= b[1] and a[2] >= b[2] and a[3] <= b[3] and a[4] >= b[4]


STRICT_SAME_ENGINE = False


class Prog:
    ENGS = ("pe", "act", "dve", "pool", "sp")

    def __init__(self, nc):
        self.nc = nc
        self.ops = []
        self.eng_ops = {e: [] for e in self.ENGS}
        self.track = {}
        self.lane_last = {}
        self.lane_cnt = {}

    def add(self, eng, fn, reads=(), writes=(), lane=None, extra=()):
        idx = len(self.ops)
        deps = set((d, 'raw') for d in extra)
        rregs = [region(a) for a in reads]
        wregs = [region(a) for a in writes]
        for r in rregs:
            st = self.track.setdefault(r[0], {"w": [], "r": []})
            for (wr, wop) in st["w"]:
                if _overlap(r, wr):
                    deps.add((wop, "raw"))
        for w in wregs:
            st = self.track.setdefault(w[0], {"w": [], "r": []})
            for (wr, wop) in st["w"]:
                if _overlap(w, wr):
                    deps.add((wop, "waw"))
            for (rr, rop) in st["r"]:
                if _overlap(w, rr):
                    deps.add((rop, "war"))
        for w in wregs:
            st = self.track[w[0]]
            st["w"] = [(wr, wop) for (wr, wop) in st["w"] if not _covers(w, wr)]
            st["r"] = [(rr, rop) for (rr, rop) in st["r"] if not _covers(w, rr)]
            st["w"].append((w, idx))
        for r in rregs:
            st = self.track[r[0]]
            st["r"] = [(rr, rop) for (rr, rop) in st["r"]
                       if rop == idx or not (self.ops[rop]["eng"] == eng and self.ops[rop]["lane"] is None
                               and lane is None and _covers(r, rr))]
            st["r"].append((r, idx))
        if lane is not None:
            if lane in self.lane_last:
                deps.add((self.lane_last[lane], "lane"))
            self.lane_last[lane] = idx
            self.lane_cnt[lane] = self.lane_cnt.get(lane, 0) + 1
        op = dict(eng=eng, fn=fn, deps=deps, lane=lane, idx=idx,
                  pos=len(self.eng_ops[eng]) + 1,
                  lpos=self.lane_cnt.get(lane, 0) if lane is not None else 0)
        self.ops.append(op)
        self.eng_ops[eng].append(idx)
        return idx

    def finalize(self, stack):
        nc = self.nc
        ops = self.ops
        def dom(o):
            return ("L", o["lane"]) if o["lane"] is not None else ("E", o["eng"])

        def dpos(o):
            return o["lpos"] if o["lane"] is not None else o["pos"]

        clk = {e: {} for e in self.ENGS}
        vcs = [None] * len(ops)
        waits = [None] * len(ops)
        signal = [False] * len(ops)
        for o in ops:
            e = o["eng"]
            c = clk[e]
            ws = []
            for (d, kind) in sorted(o["deps"], key=lambda t: -t[0]):
                od = ops[d]
                if od["lane"] is None and od["eng"] == e:
                    if e == "pe" or e == "sp" or (kind != "raw" and not STRICT_SAME_ENGINE):
                        continue
                dd, dp = dom(od), dpos(od)
                if c.get(dd, 0) >= dp:
                    continue
                ws.append(d)
                for k, v in vcs[d].items():
                    if c.get(k, 0) < v:
                        c[k] = v
            waits[o["idx"]] = ws
            for d in ws:
                signal[d] = True
            vc = dict(c)
            vc[dom(o)] = dpos(o)
            vcs[o["idx"]] = vc
        import os as _os
        LIM = int(_os.environ.get("SEMLIM", 1000))
        LIML = 60
        sem = {}
        cnt = {}
        skey = [None] * len(ops)
        sval = [0] * len(ops)

        def getsem(key):
            if key not in sem:
                sem[key] = stack.enter_context(nc.semaphore("s%d" % len(sem)))
            return sem[key]

        for o in ops:
            d = dom(o)
            if o["lane"] is not None:
                cnt[d] = cnt.get(d, 0) + 1
                ep = (cnt[d] - 1) // LIML
                skey[o["idx"]] = (d, ep)
                sval[o["idx"]] = ((cnt[d] - 1) % LIML + 1) * 16
                getsem((d, ep))
            elif signal[o["idx"]]:
                cnt[d] = cnt.get(d, 0) + 1
                ep = (cnt[d] - 1) // LIM
                skey[o["idx"]] = (d, ep)
                sval[o["idx"]] = (cnt[d] - 1) % LIM + 1
                getsem((d, ep))
        self.n_sem = len(sem)
        self.n_wait = sum(len(w) for w in waits)
        block = stack.enter_context(nc.Block())
        engs = {"pe": block.tensor, "act": block.scalar, "dve": block.vector,
                "pool": block.gpsimd, "sp": block.sync}

        def make(e):
            def body(eng):
                for i in self.eng_ops[e]:
                    o = ops[i]
                    for d in waits[i]:
                        eng.wait_ge(sem[skey[d]], sval[d])
                    if o["fn"] is None:
                        continue
                    ins = o["fn"](eng)
                    if o["lane"] is not None:
                        ins.then_inc(sem[skey[i]], 16)
                    elif signal[i]:
                        ins.then_inc(sem[skey[i]], 1)
            return body

        for e in self.ENGS:
            if self.eng_ops[e]:
                engs[e](make(e))
from contextlib import ExitStack

D = 1024; T = 2048; CL = 256; NT = 2304; NTILE = 18; DFF = 4096; DIN = 3104; L = 2
BLOCKS = [(0, 256)] + [(256 + 512 * i, 512) for i in range(4)]
BASE_B = 2080; BASE_C = 2592
ORDER = [list(range(18)), [1, 0] + list(range(17, 1, -1))]


def make_consts():
    k = np.arange(128)[:, None]; j = np.arange(128)[None, :]
    ident = (k == j)
    A_f = (k > j); A_b = (k < j); B_f = (k <= j); B_b = (k >= j)
    Minc_f = (j >= k); Minc_b = (j <= k); Mstr_f = (j > k); Mstr_b = (j < k)
    blk = (k // 64 == j // 64)
    ones = np.ones((128, 128))
    R = np.zeros((128, 128))
    for d in range(128):
        loc = d % 32
        if loc < 16:
            R[d + 16, d] = -1.0
        else:
            R[d - 16, d] = 1.0
    b32 = (k // 32 == j // 32); b64 = (k // 64 == j // 64)
    cols = [ident, A_f, A_b, B_f, B_b, Minc_f, Minc_f, Minc_b, Minc_b, Mstr_f, Mstr_f, Mstr_b, Mstr_b,
            blk, ones, R, ident, ident, ident, ident, b32, b64 & ~b32, ~b64]
    return np.concatenate([np.asarray(c, np.float32) for c in cols], axis=1)


C_ID = 0; C_AF = 128; C_AB = 256; C_BF = 384; C_BB = 512; C_MINC = 640; C_MSTR = 1152
C_BLK = 1664; C_ONES = 1792; C_R = 1920; C_ID4 = 2048; C_B32 = 2560; C_B64 = 2688; C_BOFF = 2816; NCONST = 2944


def make_rope():
    half = 16
    inv = 10000.0 ** (-np.arange(half, dtype=np.float64) / half)
    t = np.arange(T)
    row = (t // 64).astype(np.float64); col = (t % 64).astype(np.float64)
    cosT = np.ones((128, NT), np.float32); sinT = np.zeros((128, NT), np.float32)
    for p in range(128):
        d = p % 64; grp = d // 32; f = d % 16
        pos = row if grp == 0 else col
        ang = (pos.astype(np.float32) * np.float32(inv[f]).astype(np.float32)).astype(np.float32)
        cosT[p, CL:] = np.cos(ang); sinT[p, CL:] = np.sin(ang)
    return cosT, sinT


def make_wmask():
    m = np.zeros((6, 128, 512), np.float32)
    jj = np.arange(128)[:, None]; ii = np.arange(512)[None, :]
    for r in range(6):
        rel = r - 1
        m[r] = (np.abs(ii - jj - 128 * rel) <= 128)
    return m


class KB:
    def __init__(self, nc, stack, dbg=None, stop=None):
        self.nc = nc; self.st = stack; self.P = Prog(nc); self.dbg = dbg or {}
        self.stop = stop
        self.lane_rr = 0

    def mm(self, out, lhsT, rhs, start=True, stop=True):
        self.P.add("pe", lambda e: e.matmul(out, lhsT, rhs, start=start, stop=stop), reads=[lhsT, rhs], writes=[out])

    def tr(self, out, in_, ident):
        self.P.add("pe", lambda e: e.transpose(out, in_, ident), reads=[in_, ident], writes=[out])

    def act(self, out, in_, func, bias=None, scale=None):
        kw = {}
        rd = [in_]
        if bias is not None:
            kw["bias"] = bias
            if not isinstance(bias, (int, float)):
                rd.append(bias)
        if scale is not None:
            kw["scale"] = scale
            if not isinstance(scale, (int, float)):
                rd.append(scale)
        self.P.add("act", lambda e: e.activation(out, in_, func, **kw), reads=rd, writes=[out])

    def tt(self, eng, out, in0, in1, op):
        self.P.add(eng, lambda e: e.tensor_tensor(out, in0, in1, op), reads=[in0, in1], writes=[out])

    def ts(self, eng, out, in0, s1, op0, s2=None, op1=None):
        rd = [in0] + [s for s in (s1, s2) if s is not None and not isinstance(s, (int, float))]
        if op1 is None:
            self.P.add(eng, lambda e: e.tensor_scalar(out, in0, s1, None, op0), reads=rd, writes=[out])
        else:
            self.P.add(eng, lambda e: e.tensor_scalar(out, in0, s1, s2, op0, op1), reads=rd, writes=[out])

    def stt(self, eng, out, in0, scalar, in1, op0, op1):
        rd = [in0, in1] + ([] if isinstance(scalar, (int, float)) else [scalar])
        self.P.add(eng, lambda e: e.scalar_tensor_tensor(out, in0, scalar, in1, op0, op1), reads=rd, writes=[out])

    def copy(self, eng, out, in_):
        if eng == "act":
            self.P.add("act", lambda e: e.activation(out, in_, AF.Copy), reads=[in_], writes=[out])
        else:
            self.P.add(eng, lambda e: e.tensor_copy(out, in_), reads=[in_], writes=[out])

    def memset(self, eng, out, val):
        self.P.add(eng, lambda e: e.memset(out, val), writes=[out])

    def recip(self, out, in_):
        self.P.add("dve", lambda e: e.reciprocal(out, in_), reads=[in_], writes=[out])

    def dma(self, out, in_, lane, eng="sp", slow=False, track_out=True):
        kw = {"allow_slow_non_contiguous": True} if slow else {}
        wr = [out] if track_out else []
        rd = [] if track_out else [in_]
        return self.P.add(eng, lambda e: e.dma_start(out=out, in_=in_, **kw), reads=rd, writes=wr, lane=lane)

    def barrier(self, engs=("pe", "act", "dve", "pool")):
        lasts = [self.P.eng_ops[e][-1] for e in engs if self.P.eng_ops[e]]
        for e in engs:
            if self.P.eng_ops[e]:
                self.P.add(e, None, extra=lasts)

    def sb(self, name, shape, dt):
        return self.st.enter_context(self.nc.sbuf_tensor(name, shape, dt))

class Arena:
    def __init__(self, t, words):
        self.t = t; self.words = words; self.off = 0

    def f32(self, n):
        a = self.t[:, self.off:self.off + n]; self.off += n
        assert self.off <= self.words, ("arena overflow", self.off, self.words)
        return a

    def bf16(self, n):
        w = (n + 1) // 2
        a = self.t[:, self.off:self.off + w].bitcast(BF16); self.off += w
        assert self.off <= self.words, ("arena overflow", self.off, self.words)
        return a[:, 0:n]

    def mark(self):
        return self.off

    def release(self, m):
        self.off = m


def R_(ap):
    return ap


def RR(ap):
    import os
    return ap.bitcast(F32R) if os.environ.get("USE_F32R") else ap


def build(nc, dbg=(), stop=None, nlayers=L):
    def dram(name, shape, kind="ExternalInput"):
        return nc.dram_tensor(name, list(shape), F32, kind=kind).ap()
    x_d = dram("x", [T, D]); ctx_d = dram("ctx", [CL, D]); c_d = dram("c", [D]); cc_d = dram("c_ctx", [D])
    wmod_d = dram("w_mod", [L, D, 6 * D]); bmod_d = dram("b_mod", [L, 6 * D]); gattn_d = dram("g_attn", [L, D])
    win_d = dram("w_in", [L, D, DIN]); conv_d = dram("gdn_conv_w", [L, 5, 1536])
    alog_d = dram("gdn_a_log", [L, 16]); dtb_d = dram("gdn_dt_bias", [L, 16]); gng_d = dram("gdn_norm_g", [L, 64])
    gaq_d = dram("ga_q_norm_g", [L, 64]); gak_d = dram("ga_k_norm_g", [L, 64])
    waq_d = dram("wa_q_norm_g", [L, 64]); wak_d = dram("wa_k_norm_g", [L, 64]); sink_d = dram("wa_sink", [L, 4])
    wout_d = dram("w_out", [L, D, D]); gmlp_d = dram("g_mlp", [L, D])
    w1_d = dram("w_mlp_in", [L, D, DFF]); w2_d = dram("w_mlp_out", [L, DFF, D])
    cst_d = dram("consts", [128, NCONST]); ropec_d = dram("ropec", [128, NT]); ropes_d = dram("ropes", [128, NT])
    wmask_d = dram("wmask", [128, 6 * 512])
    out_d = dram("out", [T, D], kind="ExternalOutput")
    dbg_d = {}
    for (nm, shape) in dbg:
        dbg_d[nm] = dram("dbg_" + nm, shape, kind="ExternalOutput")

    st = ExitStack()
    with st:
        K = KB(nc, st, stop=stop)
        P = K.P
        xT = K.sb("xT", [128, 8, NT], F32)
        xnT = K.sb("xnT", [128, 8, NT], BF16)
        cst = K.sb("cst", [128, NCONST], F32)
        cstR = K.sb("cstR", [128, 256], F32)
        dbl = K.sb("dbl", [128, 6 * 512], F32)
        cb = K.sb("cb", [128, 512], BF16)
        modT = [K.sb("modT%d" % l, [128, 48, 2], F32) for l in range(L)]
        gsA = K.sb("gsA", [128, 8, 2], F32); gsM = K.sb("gsM", [128, 8, 2], F32)
        small = K.sb("small", [128, 64], F32)
        AW = (nc.sbuf_bytes_remaining - 2048) // 4
        arena_t = K.sb("arena", [128, AW], F32)
        AR = Arena(arena_t, AW)
        ps = st.enter_context(nc.psum_tensor("ps", [128, 4096], F32))

        def bank(i):
            return ps[:, i * 512:(i + 1) * 512]

        ident = cst[:, C_ID:C_ID + 128]
        identb = cb[:, 0:128]; onesb = cb[:, 128:256]; blkb = cb[:, 256:384]
        eps_norm = small[:, 0:1]
        one_c = small[:, 1:2]

        def dump(name, ap_src, dst=None):
            if name in dbg_d:
                K.dma(dbg_d[name] if dst is None else dst, ap_src, lane="dbg_" + name, track_out=False, eng="pool")

        K.dma(cst[:], cst_d, lane="cst")
        K.copy("dve", R_(cstR[:]), cst[:, C_AF:C_AF + 256])
        K.copy("dve", identb, cst[:, C_ID:C_ID + 128])
        K.copy("dve", onesb, cst[:, C_ONES:C_ONES + 128])
        K.copy("dve", blkb, cst[:, C_BLK:C_BLK + 128])
        K.memset("dve", eps_norm, 1e-6)
        K.memset("dve", one_c, 1.0)

        m0 = AR.mark()
        xin = [AR.f32(1024) for _ in range(2)]
        for t in range(NTILE):
            src = ctx_d[t * 128:(t + 1) * 128, :] if t < 2 else x_d[(t - 2) * 128:(t - 1) * 128, :]
            xi = xin[t % 2]
            K.dma(xi, src, lane="xin%d" % (t % 2))
            pb = ps[:, (t % 2) * 1024:(t % 2) * 1024 + 1024]
            for cch in range(8):
                K.tr(pb[:, cch * 128:(cch + 1) * 128], xi[:, cch * 128:(cch + 1) * 128], ident)
            K.copy("act" if t % 2 else "dve", xT[:, :, t * 128:(t + 1) * 128],
                   pb.rearrange("p (c n) -> p c n", c=8))
        AR.release(m0)

        m0 = AR.mark()
        craw = AR.f32(16).rearrange("p (k r) -> p k r", r=2)
        K.dma(craw[:, :, 0], c_d.rearrange("(k p) -> p k", p=128), lane="c0", slow=True)
        K.dma(craw[:, :, 1], cc_d.rearrange("(k p) -> p k", p=128), lane="c1", slow=True)
        scb = AR.bf16(16).rearrange("p (k r) -> p k r", r=2)
        K.act(scb, craw, AF.Silu)
        modrow = AR.f32(6144)
        brow = AR.f32(6144)
        wmb = [AR.bf16(8 * 512).rearrange("p (k n) -> p k n", k=8) for _ in range(2)]
        for l in range(nlayers):
            K.dma(brow[0:2, :], bmod_d[l:l + 1, :].partition_broadcast(2).rearrange("p o n -> p (o n)"), lane="brow")
            for j in range(12):
                wb = wmb[j % 2]
                K.dma(wb, wmod_d[l].rearrange("(k p) n -> p k n", p=128)[:, :, j * 512:(j + 1) * 512],
                      lane="wmb%d" % (j % 2), eng="pool")
                pb = bank(j % 2)
                for k in range(8):
                    K.mm(pb[0:2, :], scb[:, k, :], wb[:, k, :], start=(k == 0), stop=(k == 7))
                K.tt("dve", modrow[0:2, j * 512:(j + 1) * 512], pb[0:2, :], brow[0:2, j * 512:(j + 1) * 512], ALU.add)
            pb = bank(2)
            for ch in range(48):
                K.tr(pb[:, ch * 2:ch * 2 + 2], modrow[0:2, ch * 128:(ch + 1) * 128], ident[0:2, 0:2])
            K.copy("dve", modT[l][:], pb[:, 0:96].rearrange("p (c r) -> p c r", r=2))
            dump("modT%d" % l, modT[l][:])
        AR.release(m0)

        def norm_blocks(l, which, blocks):
            gs = gsA if which == 0 else gsM
            shoff = 0 if which == 0 else 24
            m1 = AR.mark()
            sqb = AR.bf16(8 * 512).rearrange("p (c n) -> p c n", c=8)
            rs = AR.f32(512); rstd = AR.f32(512)
            tmp = [AR.f32(512) for _ in range(2)]
            for (a0, n) in blocks:
                r = 1 if a0 < CL else 0
                K.act(sqb[:, :, 0:n], xT[:, :, a0:a0 + n], AF.Square)
                pb = bank(7)
                for cch in range(8):
                    K.mm(pb[:, 0:n], onesb, sqb[:, cch, 0:n], start=(cch == 0), stop=(cch == 7))
                K.act(rs[:, 0:n], pb[:, 0:n], AF.Sqrt, bias=eps_norm, scale=1.0 / D)
                K.recip(rstd[:, 0:n], rs[:, 0:n])
                for cch in range(8):
                    tm = tmp[cch % 2]
                    K.stt("dve", tm[:, 0:n], xT[:, cch, a0:a0 + n], gs[:, cch, r:r + 1], rstd[:, 0:n], ALU.mult, ALU.mult)
                    K.act(xnT[:, cch, a0:a0 + n], tm[:, 0:n], AF.Identity, bias=modT[l][:, shoff + cch, r:r + 1])
            AR.release(m1)

        def load_cols(dst, wd, l, col0, ncols, lane, nk=8):
            K.dma(dst, wd[l].rearrange("(k p) n -> p k n", p=128)[:, :, col0:col0 + ncols], lane=lane, eng="pool")

        def proj_fm(pb, w, blk):
            a0, n = blk
            for k in range(8):
                K.mm(pb[:, 0:n], w[:, k, :], xnT[:, k, a0:a0 + n], start=(k == 0), stop=(k == 7))

        def resid_add(l, which, pb, cch, blk):
            a0, n = blk
            r = 1 if a0 < CL else 0
            goff = 16 if which == 0 else 40
            K.stt("dve", xT[:, cch, a0:a0 + n], pb[:, 0:n], modT[l][:, goff + cch, r:r + 1], xT[:, cch, a0:a0 + n],
                  ALU.mult, ALU.add)

        for l in range(nlayers):
            last = (l == L - 1)
            oblocks = BLOCKS[1:] if last else BLOCKS
            m_layer = AR.mark()
            gT = AR.f32(16).rearrange("p (k r) -> p k r", r=2)
            K.dma(gT[:, :, 0], gattn_d[l].rearrange("(k p) -> p k", p=128), lane="gT0", slow=True)
            K.dma(gT[:, :, 1], gmlp_d[l].rearrange("(k p) -> p k", p=128), lane="gT1", slow=True)
            for r in range(2):
                K.stt("dve", gsA[:, :, r], modT[l][:, 8:16, r], 1.0, gT[:, :, 0], ALU.add, ALU.mult)
                K.stt("dve", gsM[:, :, r], modT[l][:, 32:40, r], 1.0, gT[:, :, 1], ALU.add, ALU.mult)
            norm_blocks(l, 0, BLOCKS)
            if l == 0:
                dump("xnT", xnT[:, :, :])
            if stop == "norm":
                break

            m_gdn = AR.mark()
            gates = AR.f32(18 * 16 * 6)
            m1 = AR.mark()
            gates2 = AR.f32(18 * 16 * 2)
            gv = lambda i: (gates[:, i * 288:(i + 1) * 288] if i < 6 else gates2[:, (i - 6) * 288:(i - 5) * 288]).rearrange("p (t c) -> p t c", c=16)
            beta_tm, nbeta_tm, g_tm, eG_tm, kes_tm, eGl_rep, Gam_tm, Gl_rep = [gv(i) for i in range(8)]
            wg = AR.bf16(8 * 32).rearrange("p (k n) -> p k n", k=8)
            load_cols(wg, win_d, l, 2048, 32, "wg")
            graw = AR.f32(18 * 32).rearrange("p (t c) -> p t c", c=32)
            pb = bank(0)
            for t in range(NTILE):
                for k in range(8):
                    K.mm(pb[:, t * 32:(t + 1) * 32] if t < 16 else bank(1)[:, (t - 16) * 32:(t - 15) * 32],
                         xnT[:, k, t * 128:(t + 1) * 128], wg[:, k, :], start=(k == 0), stop=(k == 7))
            K.copy("dve", graw[:, 0:16, :], pb.rearrange("p (t c) -> p t c", c=32))
            K.copy("dve", graw[:, 16:18, :], bank(1)[:, 0:64].rearrange("p (t c) -> p t c", c=32))
            rep = AR.f32(32)
            K.dma(rep[:, 0:16], dtb_d[l:l + 1, :].partition_broadcast(128).rearrange("p o n -> p (o n)"), lane="rep0")
            K.dma(rep[:, 16:32], alog_d[l:l + 1, :].partition_broadcast(128).rearrange("p o n -> p (o n)"), lane="rep1")
            negA = AR.f32(16)
            K.act(negA, rep[:, 16:32], AF.Exp)
            K.ts("dve", negA, negA, -1.0, ALU.mult)
            K.act(beta_tm, graw[:, :, 0:16], AF.Sigmoid)
            K.ts("dve", nbeta_tm, beta_tm, -1.0, ALU.mult)
            xa = AR.f32(288).rearrange("p (t c) -> p t c", c=16)
            ab = AR.f32(288).rearrange("p (t c) -> p t c", c=16)
            K.tt("dve", xa, graw[:, :, 16:32], rep[:, 0:16].unsqueeze(1).to_broadcast([128, 18, 16]), ALU.add)
            K.ts("dve", ab, xa, -1.0, ALU.mult)
            K.tt("dve", ab, ab, xa, ALU.max)
            K.act(ab, ab, AF.Exp, scale=-1.0)
            K.act(ab, ab, AF.Ln, bias=one_c)
            K.ts("dve", xa, xa, 0.0, ALU.max)
            K.tt("dve", xa, xa, ab, ALU.add)
            K.tt("dve", g_tm, xa, negA.unsqueeze(1).to_broadcast([128, 18, 16]), ALU.mult)
            pb = bank(2)
            for n_ in range(NTILE):
                for d_ in range(2):
                    Bm = cst[:, C_BF:C_BF + 128] if d_ == 0 else cst[:, C_BB:C_BB + 128]
                    K.mm(pb[:, n_ * 16 + d_ * 8:n_ * 16 + d_ * 8 + 8], Bm, g_tm[:, n_, d_ * 8:d_ * 8 + 8])
            K.copy("dve", Gam_tm, pb[:, 0:288].rearrange("p (t c) -> p t c", c=16))
            pb = bank(3)
            K.mm(pb[:, 0:288], cst[:, C_ONES:C_ONES + 128], gates[:, 2 * 288:3 * 288])
            K.copy("dve", Gl_rep, pb[:, 0:288].rearrange("p (t c) -> p t c", c=16))
            K.tt("dve", kes_tm, Gl_rep, Gam_tm, ALU.subtract)
            K.ts("dve", kes_tm, kes_tm, -40.0, ALU.max)
            K.act(kes_tm, kes_tm, AF.Exp)
            K.ts("dve", eG_tm, Gam_tm, -40.0, ALU.max)
            K.act(eG_tm, eG_tm, AF.Exp)
            K.ts("dve", eGl_rep, Gl_rep, -40.0, ALU.max)
            K.act(eGl_rep, eGl_rep, AF.Exp)
            if l == 0:
                dump("g_tm", g_tm); dump("beta_tm", beta_tm)
            AR.release(m1)
            convT = AR.f32(60).rearrange("p (q j) -> p q j", j=5)
            for j in range(5):
                K.dma(convT[:, :, j], conv_d[l][j].rearrange("(q p) -> p q", p=128), lane="convT%d" % j, slow=True)
            gng = AR.f32(64)
            K.dma(gng, gng_d[l:l + 1, :].partition_broadcast(128).rearrange("p o n -> p (o n)"), lane="gng")
            S_all = AR.f32(128)
            pgt = AR.f32(6 * 72)
            pgv = lambda i: pgt[:, i * 72:(i + 1) * 72].rearrange("p (t c) -> p t c", c=4)
            p_beta, p_nbeta, p_g, p_eG, p_kes, p_eGl = [pgv(i) for i in range(6)]
            qT = AR.f32(NT); kT = AR.f32(NT); vT = AR.f32(NT)
            o_acc = AR.f32(NT)
            o_acc3 = o_acc.rearrange("p (t c) -> p t c", c=128)
            wq4 = AR.bf16(8 * 3 * 128).rearrange("p (k t n) -> p k t n", k=8, t=3)
            m_ov = AR.mark()
            dg = AR.bf16(5 * 128).rearrange("p (j n) -> p j n", j=5)
            raw = AR.bf16(2312)
            nsq = AR.bf16(512); nrs = AR.f32(512); nrstd = AR.f32(512)
            AR.release(m_ov)
            zsT = AR.bf16(NT); oTp = AR.bf16(NT); wo = AR.bf16(1024); ors = AR.f32(36); orstd = AR.f32(36)
            AR.release(m_ov)
            def f512():
                return AR.f32(512)
            gB = f512(); iT = f512()
            PTb = [dbl[:, 0:512], dbl[:, 512:1024]]; Pb = [dbl[:, 1024:1536], dbl[:, 1536:2048]]
            Yb = [dbl[:, 2048:2560], dbl[:, 2560:3072]]
            dI = gB; Eb = f512(); dS = f512()
            kE = AR.f32(256); kend = AR.f32(256); vt = AR.f32(256); ub = AR.f32(256); vn = AR.f32(256)
            wT = f512()
            Q64 = f512(); Q128 = f512()
            print("arena used at GDN:", AR.off, "of", AR.words)

            def rawpos(a0):
                return a0 + 2 if a0 < CL else a0 + 6

            import os
            _pairs = [int(v) for v in os.environ.get('PAIRS', '0,1,2,3').split(',') if v != 'none']
            for pr in _pairs:
                for src_, dst_ in ((beta_tm, p_beta), (nbeta_tm, p_nbeta), (g_tm, p_g), (eG_tm, p_eG), (kes_tm, p_kes), (eGl_rep, p_eGl)):
                    for d_ in range(2):
                        K.copy("dve", dst_[:, :, d_ * 2:d_ * 2 + 2], src_[:, :, d_ * 8 + 2 * pr:d_ * 8 + 2 * pr + 2])
                K.memset("pool", raw, 0.0)
                for ty in range(3):
                    K.dma(wq4[:, :, ty, :], win_d[l].rearrange("(k p) n -> p k n", p=128)[:, :, ty * 512 + pr * 128: ty * 512 + pr * 128 + 128],
                          lane="wq4_%d" % ty, eng="pool")
                for ty in range(3):
                    for (a0, n) in BLOCKS:
                        pb = bank(a0 // 512 % 2)
                        for k in range(8):
                            K.mm(pb[:, 0:n], wq4[:, k, ty, :], xnT[:, k, a0:a0 + n], start=(k == 0), stop=(k == 7))
                        K.copy("act", raw[:, rawpos(a0):rawpos(a0) + n], pb[:, 0:n])
                    for j in range(5):
                        K.ts("dve", dg[:, j, :], identb, convT[:, ty * 4 + pr, j:j + 1], ALU.mult)
                    dst = (qT, kT, vT)[ty]
                    for (a0, n) in BLOCKS:
                        pb = bank(2 + a0 // 512 % 2)
                        for j in range(5):
                            K.mm(pb[:, 0:n], dg[:, j, :], raw[:, rawpos(a0) + j - 2:rawpos(a0) + j - 2 + n],
                                 start=(j == 0), stop=(j == 4))
                        if ty == 2:
                            K.act(R_(vT[:, a0:a0 + n]), pb[:, 0:n], AF.Silu)
                        else:
                            K.act(o_acc[:, a0:a0 + n], pb[:, 0:n], AF.Silu)
                            K.act(nsq[:, 0:n], o_acc[:, a0:a0 + n], AF.Square)
                            pb2 = bank(4 + a0 // 512 % 2)
                            K.mm(pb2[:, 0:n], blkb, nsq[:, 0:n])
                            K.act(nrs[:, 0:n], pb2[:, 0:n], AF.Sqrt, bias=eps_norm, scale=1.0)
                            K.recip(nrstd[:, 0:n], nrs[:, 0:n])
                            K.stt("dve", R_(dst[:, a0:a0 + n]), o_acc[:, a0:a0 + n], 0.125 if ty == 0 else 1.0,
                                  nrstd[:, 0:n], ALU.mult, ALU.mult)
                if l == 0 and _pairs and pr == _pairs[0]:
                    dump("gq0", qT); dump("gk0", kT); dump("gv0", vT)
                if stop == "gdnproj":
                    break
                K.memset("dve", R_(S_all), 0.0)
                K.memset("pool", o_acc, 0.0)
                import os
                for s in range(int(os.environ.get('GDN_STEPS', NTILE))):
                    info = []
                    for b in range(4):
                        d_, m_ = b // 2, b % 2
                        n_ = ORDER[d_][s]
                        info.append((d_, m_, n_, d_ * 2 + m_))
                    bs = lambda b: slice(b * 128, (b + 1) * 128)
                    hs = lambda b: slice(b * 64, (b + 1) * 64)
                    for b, (d_, m_, n_, col) in enumerate(info):
                        tok = slice(n_ * 128, (n_ + 1) * 128); rows = slice(64 * m_, 64 * m_ + 64)
                        K.mm(bank(m_)[:, d_ * 128:(d_ + 1) * 128], R_(kT[rows, tok]), R_(kT[rows, tok]))
                        K.mm(bank(m_)[:, 256 + d_ * 128:256 + (d_ + 1) * 128], R_(kT[rows, tok]), R_(qT[rows, tok]))
                    for b, (d_, m_, n_, col) in enumerate(info):
                        Bm = cst[:, C_BF:C_BF + 128] if d_ == 0 else cst[:, C_BB:C_BB + 128]
                        K.ts("dve", R_(gB[:, bs(b)]), Bm, p_g[:, n_, col:col + 1], ALU.mult)
                        Am = cstR[:, 0:128] if d_ == 0 else cstR[:, 128:256]
                        K.mm(bank(2)[:, bs(b)], R_(Am), R_(gB[:, bs(b)]))
                    if (os.environ.get('CUTALL') or s == int(os.environ.get('CUTSTEP', 0))) and int(os.environ.get('CUT', 99)) == 1:
                        continue
                    if os.environ.get('PHASEBAR'):
                        K.barrier()
                    K.ts("dve", Eb, bank(2), -40.0, ALU.max)
                    K.act(Eb, Eb, AF.Exp)
                    K.tt("dve", dI, Eb, cst[:, C_MINC:C_MINC + 512], ALU.mult)
                    K.tt("dve", dS, Eb, cst[:, C_MSTR:C_MSTR + 512], ALU.mult)
                    PT, Pm, Y = PTb[0], Pb[0], Yb[0]
                    for b, (d_, m_, n_, col) in enumerate(info):
                        K.stt("dve", RR(PT[:, bs(b)]), bank(m_)[:, d_ * 128:(d_ + 1) * 128], p_nbeta[:, n_, col:col + 1], dS[:, bs(b)],
                              ALU.mult, ALU.mult)
                    for b, (d_, m_, n_, col) in enumerate(info):
                        K.tt("dve", R_(iT[:, bs(b)]), bank(m_)[:, 256 + d_ * 128:256 + (d_ + 1) * 128], dI[:, bs(b)], ALU.mult)
                    if (os.environ.get('CUTALL') or s == int(os.environ.get('CUTSTEP', 0))) and int(os.environ.get('CUT', 99)) == 2:
                        continue
                    if os.environ.get('PHASEBAR'):
                        K.barrier()
                    for b in range(4):
                        K.tr(bank(3)[:, bs(b)], PT[:, bs(b)], ident)
                    K.copy("dve", RR(Pm), bank(3))
                    v4 = lambda t_: t_.rearrange("p (b n) -> p b n", b=4)
                    mk = lambda c0: cst[:, c0:c0 + 128].unsqueeze(1).to_broadcast([128, 4, 128])
                    K.tt("dve", v4(Q64), v4(Pm), mk(C_B64), ALU.mult)
                    K.tt("dve", v4(Q128), v4(Pm), mk(C_BOFF), ALU.mult)
                    K.tt("dve", v4(RR(PT)), v4(PT), mk(C_B32), ALU.mult)
                    K.tt("dve", v4(RR(Pm)), v4(Pm), mk(C_B32), ALU.mult)
                    K.tt("dve", RR(Y), PT, cst[:, C_ID4:C_ID4 + 512], ALU.add)
                    if (os.environ.get('CUTALL') or s == int(os.environ.get('CUTSTEP', 0))) and int(os.environ.get('CUT', 99)) == 3:
                        continue
                    if os.environ.get('PHASEBAR'):
                        K.barrier()
                    for d_ in range(2):
                        n_ = ORDER[d_][s]
                        tok = slice(n_ * 128, (n_ + 1) * 128)
                        K.tr(bank(6)[:, d_ * 128:(d_ + 1) * 128], kT[:, tok], ident)
                        K.tr(bank(6)[:, 256 + d_ * 128:256 + (d_ + 1) * 128], vT[:, tok], ident)
                    K.copy("dve", R_(vt), bank(6)[:, 256:512])
                    for b, (d_, m_, n_, col) in enumerate(info):
                        if os.environ.get("P3") == "noke":
                            continue
                        if os.environ.get("P3") == "const":
                            K.ts("dve", R_(kE[:, hs(b)]), bank(6)[:, hs(b)], 0.5, ALU.mult)
                            K.ts("dve", R_(kend[:, hs(b)]), bank(6)[:, hs(b)], 0.5, ALU.mult)
                            continue
                        K.ts("dve", R_(kE[:, hs(b)]), bank(6)[:, hs(b)], p_eG[:, n_, col:col + 1], ALU.mult)
                        K.ts("dve", R_(kend[:, hs(b)]), bank(6)[:, hs(b)], p_kes[:, n_, col:col + 1], ALU.mult)
                    if (os.environ.get('CUTALL') or s == int(os.environ.get('CUTSTEP', 0))) and int(os.environ.get('CUT', 99)) == 4:
                        continue
                    if os.environ.get('PHASEBAR'):
                        K.barrier()
                    cur = 0
                    for kk in range(1, 5):
                        PTn, Pn, Yn = PTb[1 - cur], Pb[1 - cur], Yb[1 - cur]
                        PTc, Pc, Yc = PTb[cur], Pb[cur], Yb[cur]
                        for b in range(4):
                            K.mm(bank(4)[:, bs(b)], RR(Pc[:, bs(b)]), RR(PTc[:, bs(b)]))
                        for b in range(4):
                            K.mm(bank(5)[:, bs(b)], RR(PTc[:, bs(b)]), RR(Pc[:, bs(b)]))
                        K.copy("dve", RR(PTn), bank(4))
                        K.copy("dve", RR(Pn), bank(5))
                        for b in range(4):
                            K.mm(bank(7)[:, bs(b)], RR(Pn[:, bs(b)]), RR(Yc[:, bs(b)]))
                        K.tt("dve", RR(Yn), bank(7), Yc, ALU.add)
                        if os.environ.get('PHASEBAR'):
                            K.barrier()
                        cur = 1 - cur
                    for Qm in (Q64, Q128):
                        Yc = Yb[cur]; Yn = Yb[1 - cur]
                        for b in range(4):
                            K.tr(bank(4)[:, bs(b)], Yc[:, bs(b)], ident)
                        K.copy("act", dS, bank(4))
                        for b in range(4):
                            K.mm(bank(5)[:, bs(b)], Qm[:, bs(b)], Yc[:, bs(b)])
                        K.copy("dve", Eb, bank(5))
                        for b in range(4):
                            K.mm(bank(7)[:, bs(b)], dS[:, bs(b)], Eb[:, bs(b)])
                        K.tt("dve", RR(Yn), bank(7), Yc, ALU.add)
                        cur = 1 - cur
                    Y = Yb[cur]
                    if (os.environ.get('CUTALL') or s == int(os.environ.get('CUTSTEP', 0))) and int(os.environ.get('CUT', 99)) == 5:
                        continue
                    if os.environ.get('PHASEBAR'):
                        K.barrier()
                    for b, (d_, m_, n_, col) in enumerate(info):
                        K.mm(bank(0)[:, hs(b)], R_(Y[:, bs(b)]), R_(vt[:, hs(b)]))
                        K.mm(bank(1)[:, bs(b)], R_(kE[:, d_ * 128:(d_ + 1) * 128]), R_(Y[:, bs(b)]))
                    if (os.environ.get('CUTALL') or s == int(os.environ.get('CUTSTEP', 0))) and int(os.environ.get('CUT', 99)) == 51:
                        continue
                    if os.environ.get('PHASEBAR'):
                        K.barrier()
                    for b, (d_, m_, n_, col) in enumerate(info):
                        K.ts("dve", ub[:, hs(b)], bank(0)[:, hs(b)], p_beta[:, n_, col:col + 1], ALU.mult)
                    if (os.environ.get('CUTALL') or s == int(os.environ.get('CUTSTEP', 0))) and int(os.environ.get('CUT', 99)) == 52:
                        continue
                    if os.environ.get('PHASEBAR'):
                        K.barrier()
                    for b, (d_, m_, n_, col) in enumerate(info):
                        rows = slice(64 * m_, 64 * m_ + 64)
                        cc_ = int(os.environ.get('CUT2', 0))
                        if (cc_ == 1 and m_ == 1) or (cc_ == 2 and m_ == 0):
                            continue
                        if m_ == 0 and os.environ.get('ACTCOPY'):
                            K.act(wT[rows, bs(b)], bank(1)[rows, bs(b)], AF.Copy)
                        else:
                            K.copy("dve", wT[rows, bs(b)], bank(1)[rows, bs(b)])
                    if (os.environ.get('CUTALL') or s == int(os.environ.get('CUTSTEP', 0))) and int(os.environ.get('CUT', 99)) == 6:
                        continue
                    if os.environ.get('PHASEBAR'):
                        K.barrier()
                    if os.environ.get('PHASEBAR'):
                        K.barrier()
                    for b, (d_, m_, n_, col) in enumerate(info):
                        rows = slice(64 * m_, 64 * m_ + 64)
                        Sb = S_all[rows, d_ * 64:(d_ + 1) * 64]
                        K.mm(bank(2 + 2 * m_)[:, hs(b)], R_(wT[rows, bs(b)]), R_(Sb))
                    if os.environ.get('PHASEBAR'):
                        K.barrier()
                    if os.environ.get('CUTALL') and int(os.environ.get('CUT', 99)) == 8:
                        continue
                    for b, (d_, m_, n_, col) in enumerate(info):
                        K.stt("dve", R_(vn[:, hs(b)]), bank(2 + 2 * m_)[:, hs(b)], p_nbeta[:, n_, col:col + 1], ub[:, hs(b)],
                              ALU.mult, ALU.add)
                    if os.environ.get('PHASEBAR'):
                        K.barrier()
                    if os.environ.get('CUTALL') and int(os.environ.get('CUT', 99)) == 9:
                        continue
                    for b, (d_, m_, n_, col) in enumerate(info):
                        tok = slice(n_ * 128, (n_ + 1) * 128); rows = slice(64 * m_, 64 * m_ + 64)
                        Sb = S_all[rows, d_ * 64:(d_ + 1) * 64]
                        if not (last and n_ < 2):
                            K.mm(bank(3 + 2 * m_)[:, hs(b)], R_(qT[rows, tok]), R_(Sb))
                    for b, (d_, m_, n_, col) in enumerate(info):
                        if not (last and n_ < 2):
                            K.mm(bank(7)[:, hs(b)], R_(iT[:, bs(b)]), R_(vn[:, hs(b)]))
                    for b, (d_, m_, n_, col) in enumerate(info):
                        K.mm(bank(6)[:, hs(b)], R_(kend[:, d_ * 128:(d_ + 1) * 128]), R_(vn[:, hs(b)]))
                    if os.environ.get('PHASEBAR'):
                        K.barrier()
                    if os.environ.get('CUTALL') and int(os.environ.get('CUT', 99)) == 10:
                        continue
                    for b, (d_, m_, n_, col) in enumerate(info):
                        rows = slice(64 * m_, 64 * m_ + 64)
                        Sb = S_all[rows, d_ * 64:(d_ + 1) * 64]
                        if not (last and n_ < 2):
                            oa = o_acc3[:, n_, 64 * m_:64 * m_ + 64]
                            K.tt("dve", oa, bank(7)[:, hs(b)], oa, ALU.add)
                            K.stt("dve", oa, bank(3 + 2 * m_)[:, hs(b)], p_eG[:, n_, col:col + 1], oa, ALU.mult, ALU.add)
                        K.stt("dve", R_(Sb), Sb, p_eGl[rows, n_, col:col + 1], bank(6)[rows, hs(b)], ALU.mult, ALU.add)
                    if os.environ.get("STEPBAR"):
                        K.barrier()
                if l == 0 and _pairs and pr == _pairs[0]:
                    dump("oacc0", o_acc)
                if stop == "gdnscan":
                    break
                sqf = qT
                K.act(sqf, o_acc, AF.Square)
                if int(os.environ.get('CUT3', 99)) == 1:
                    break
                ss = ors; rr = orstd
                P.add("dve", lambda e, ss=ss, sqf=sqf: e.reduce_sum(ss, sqf.rearrange("p (g c) -> p g c", c=64), AX.X),
                      reads=[sqf], writes=[ss])
                if int(os.environ.get('CUT3', 99)) == 2:
                    break
                K.act(ss, ss, AF.Sqrt, bias=eps_norm, scale=1.0 / 64)
                K.recip(rr, ss)
                if int(os.environ.get('CUT3', 99)) == 3:
                    break
                og = o_acc.rearrange("p (g c) -> p g c", c=64)
                K.tt("dve", og, og, rr.unsqueeze(2).to_broadcast([128, 36, 64]), ALU.mult)
                if int(os.environ.get('CUT3', 99)) == 4:
                    break
                K.tt("dve", og, og, gng.unsqueeze(1).to_broadcast([128, 36, 64]), ALU.mult)
                if int(os.environ.get('CUT3', 99)) == 5:
                    break
                K.dma(wq4[:, :, 0, :], win_d[l].rearrange("(k p) n -> p k n", p=128)[:, :, 1536 + pr * 128: 1536 + pr * 128 + 128],
                      lane="wq4_0", eng="pool")
                for (a0, n) in BLOCKS:
                    pb = bank(a0 // 512 % 2)
                    for k in range(8):
                        K.mm(pb[:, 0:n], wq4[:, k, 0, :], xnT[:, k, a0:a0 + n], start=(k == 0), stop=(k == 7))
                    K.act(zsT[:, a0:a0 + n], pb[:, 0:n], AF.Silu)
                if int(os.environ.get('CUT3', 99)) == 6:
                    break
                K.dma(wo, wout_d[l][pr * 128:(pr + 1) * 128, :], lane="wo", eng="pool")
                for (a0, n) in BLOCKS:
                    pb = bank(2 + a0 // 512 % 2)
                    for i_ in range(n // 128):
                        t_ = a0 // 128 + i_
                        K.tr(pb[:, i_ * 128:(i_ + 1) * 128], o_acc3[:, t_, :], ident)
                    K.tt("dve", oTp[:, a0:a0 + n], pb[:, 0:n], zsT[:, a0:a0 + n], ALU.mult)
                if l == 0 and _pairs and pr == _pairs[0]:
                    dump("oTp0", oTp)
                if int(os.environ.get('CUT3', 99)) == 7:
                    break
                for (a0, n) in oblocks:
                    for cch in range(8):
                        pb = bank(4 + cch % 4)
                        K.mm(pb[:, 0:n], wo[:, cch * 128:(cch + 1) * 128], oTp[:, a0:a0 + n])
                        resid_add(l, 0, pb, cch, (a0, n))
            AR.release(m_gdn)
            if stop in ("gdnproj", "gdnscan", "gdn"):
                break

            m_att = AR.mark()
            qTa = AR.bf16(2 * NT).rearrange("p (c n) -> p c n", c=2)
            kTa = AR.bf16(NT)
            vtm = AR.bf16(18 * 2 * 66).rearrange("p (t k c) -> p t k c", t=18, k=2)
            oTa = AR.bf16(4 * NT).rearrange("p (h n) -> p h n", h=4)
            gcol = AR.f32(2); esk = AR.f32(4)
            m_aov = AR.mark()
            cosb = [AR.f32(512) for _ in range(2)]; sinb = [AR.f32(512) for _ in range(2)]
            wqk = AR.bf16(8 * 3 * 128).rearrange("p (k c n) -> p k c n", k=8, c=3)
            wv = AR.bf16(8 * 128).rearrange("p (k n) -> p k n", k=8)
            Rg = AR.f32(2 * 128)
            qraw = AR.f32(512); t1 = AR.f32(512); t2 = AR.f32(512); ars = AR.f32(512); arstd = AR.f32(512)
            asq = AR.bf16(512)
            AR.release(m_aov)
            wmk = AR.bf16(6 * 512)
            wo4 = AR.bf16(4 * 1024).rearrange("p (h n) -> p h n", h=4)
            PTs = [AR.bf16(512) for _ in range(3)]
            rden = AR.f32(512); rdr = AR.f32(512)
            print("arena used at ATT:", AR.off, "of", AR.words)
            for grp in range(2):
                base = BASE_B if grp == 0 else BASE_C
                gq_d, gk_d = (gaq_d, gak_d) if grp == 0 else (waq_d, wak_d)
                wrows = 512 if grp == 0 else 768
                wv_in = win_d[l].rearrange("(k p) n -> p k n", p=128)
                for ci, heads in enumerate(((0, 2), (1, 3))):
                    for hi, h in enumerate(heads):
                        K.dma(wqk[:, :, ci, hi * 64:(hi + 1) * 64], wv_in[:, :, base + h * 64: base + h * 64 + 64],
                              lane="wqk%d%d" % (ci, hi), eng="pool")
                K.dma(wqk[:, :, 2, :], wv_in[:, :, base + 256: base + 384], lane="wqk2", eng="pool")
                K.dma(wv, wv_in[:, :, base + 384: base + 512], lane="wv", eng="pool")
                for hh in range(2):
                    K.dma(gcol[hh * 64:(hh + 1) * 64, 0:1], gq_d[l].rearrange("(p o) -> p o", o=1), lane="gq%d" % hh, slow=True)
                    K.dma(gcol[hh * 64:(hh + 1) * 64, 1:2], gk_d[l].rearrange("(p o) -> p o", o=1), lane="gk%d" % hh, slow=True)
                for qk in range(2):
                    K.ts("dve", R_(Rg[:, qk * 128:(qk + 1) * 128]), cst[:, C_R:C_R + 128], gcol[:, qk:qk + 1], ALU.mult)
                if grp == 1:
                    K.dma(esk[64:65, 0:4], sink_d[l:l + 1, :], lane="sink")
                    K.act(esk[64:65, 0:4], esk[64:65, 0:4], AF.Exp)
                K.memset("pool", vtm[:, :, :, 64:65], 1.0)
                for t in range(NTILE):
                    pb = bank(t % 2)
                    for k in range(8):
                        K.mm(pb[:, 0:128], xnT[:, k, t * 128:(t + 1) * 128], wv[:, k, :], start=(k == 0), stop=(k == 7))
                    K.copy("act", vtm[:, t, :, 0:64], pb[:, 0:128].rearrange("p (k c) -> p k c", k=2))
                rpi = 0
                for ci in range(3):
                    qk = 0 if ci < 2 else 1
                    for (a0, n) in BLOCKS:
                        cosT_b = cosb[rpi % 2]; sinT_b = sinb[rpi % 2]
                        K.dma(cosT_b[:, 0:n], ropec_d[:, a0:a0 + n], lane="cos%d" % (rpi % 2))
                        K.dma(sinT_b[:, 0:n], ropes_d[:, a0:a0 + n], lane="sin%d" % (rpi % 2))
                        rpi += 1
                        pb = bank(2 + a0 // 512 % 2)
                        for k in range(8):
                            K.mm(pb[:, 0:n], wqk[:, k, ci, :], xnT[:, k, a0:a0 + n], start=(k == 0), stop=(k == 7))
                        K.copy("act", R_(qraw[:, 0:n]), pb[:, 0:n])
                        K.act(asq[:, 0:n], pb[:, 0:n], AF.Square)
                        pb2 = bank(4 + a0 // 512 % 2)
                        K.mm(pb2[:, 0:n], blkb, asq[:, 0:n])
                        K.act(ars[:, 0:n], pb2[:, 0:n], AF.Sqrt, bias=eps_norm, scale=1.0 / 64)
                        K.recip(arstd[:, 0:n], ars[:, 0:n])
                        pb3 = bank(6 + a0 // 512 % 2)
                        K.mm(pb3[:, 0:n], R_(Rg[:, qk * 128:(qk + 1) * 128]), R_(qraw[:, 0:n]))
                        K.stt("dve", t1[:, 0:n], qraw[:, 0:n], gcol[:, qk:qk + 1], cosT_b[:, 0:n], ALU.mult, ALU.mult)
                        K.tt("dve", t2[:, 0:n], pb3[:, 0:n], sinT_b[:, 0:n], ALU.mult)
                        K.tt("dve", t1[:, 0:n], t1[:, 0:n], t2[:, 0:n], ALU.add)
                        dst = qTa[:, ci, a0:a0 + n] if ci < 2 else kTa[:, a0:a0 + n]
                        K.tt("dve", dst, t1[:, 0:n], arstd[:, 0:n], ALU.mult)
                if l == 0:
                    dump("qTa%d" % grp, qTa); dump("kTa%d" % grp, kTa)
                K.dma(wo4[0:64], wout_d[l][wrows:wrows + 256, :].rearrange("(h p) n -> p h n", p=64), lane="wo4", eng="pool")
                K.dma(wmk, wmask_d, lane="wmk", eng="pool")
                pti = 0
                for h in range(4):
                    ci = h % 2; kv = h // 2; rows = slice(64 * kv, 64 * kv + 64)
                    qblocks = [(256 + 512 * i, 512) for i in range(4)] + ([] if last else [(0, 256)])
                    for (a0, n) in qblocks:
                        isctx = a0 < CL
                        if isctx:
                            kts = [(0, None), (1, None)]
                        elif grp == 0:
                            kts = [(t, None) for t in range(NTILE)]
                        else:
                            qb = (a0 - 256) // 512
                            kts = [(0, None), (1, None)]
                            for rel in range(-1, 5):
                                lt = 4 * qb + rel
                                if 0 <= lt < 16:
                                    kts.append((lt + 2, rel + 1))
                        ob = bank(4 + (pti % 2))
                        for g0 in range(0, len(kts), 3):
                            grp_ = list(enumerate(kts))[g0:g0 + 3]
                            slots = []
                            for i_, (kt, mi) in grp_:
                                sb_ = bank(pti % 3); PTt = PTs[pti % 3]; pti += 1
                                slots.append((sb_, PTt))
                                K.mm(sb_[:, 0:n], kTa[rows, kt * 128:(kt + 1) * 128], qTa[rows, ci, a0:a0 + n])
                            for (i_, (kt, mi)), (sb_, PTt) in zip(grp_, slots):
                                K.act(PTt[:, 0:n], sb_[:, 0:n], AF.Exp, scale=0.125)
                                if mi is not None:
                                    K.tt("dve", PTt[:, 0:n], PTt[:, 0:n], wmk[:, mi * 512:mi * 512 + n], ALU.mult)
                            for (i_, (kt, mi)), (sb_, PTt) in zip(grp_, slots):
                                K.mm(ob[0:65, 0:n], vtm[:, kt, kv, 0:65], PTt[:, 0:n], start=(i_ == 0), stop=(i_ == len(kts) - 1))
                        if grp == 1:
                            K.ts("dve", rden[64:65, 0:n], ob[64:65, 0:n], esk[64:65, h:h + 1], ALU.add)
                            K.recip(rden[64:65, 0:n], rden[64:65, 0:n])
                        else:
                            K.recip(rden[64:65, 0:n], ob[64:65, 0:n])
                        rb_ = bank(6 + (pti % 2))
                        K.mm(rb_[0:64, 0:n], cst[64:65, C_ONES:C_ONES + 64], rden[64:65, 0:n])
                        K.copy("dve", rdr[0:64, 0:n], rb_[0:64, 0:n])
                        K.tt("dve", oTa[0:64, h, a0:a0 + n], ob[0:64, 0:n], rdr[0:64, 0:n], ALU.mult)
                if l == 0:
                    dump("oTa%d" % grp, oTa[0:64, :, :])
                for (a0, n) in oblocks:
                    for cch in range(8):
                        pb = bank(cch % 4)
                        for h in range(4):
                            K.mm(pb[:, 0:n], wo4[0:64, h, cch * 128:(cch + 1) * 128], oTa[0:64, h, a0:a0 + n],
                                 start=(h == 0), stop=(h == 3))
                        resid_add(l, 0, pb, cch, (a0, n))
            AR.release(m_att)
            if l == 0:
                dump("xT_attn", xT[:, :, :])
            if stop == "attn":
                break

            m_mlp = AR.mark()
            norm_blocks(l, 1, oblocks)
            h1 = AR.bf16(32 * 512).rearrange("p (f n) -> p f n", f=32)
            hr = [AR.bf16(512) for _ in range(2)]
            w1b = [AR.bf16(8 * 512).rearrange("p (k n) -> p k n", k=8) for _ in range(2)]
            w2b = [AR.bf16(32 * 128).rearrange("p (f n) -> p f n", f=32) for _ in range(2)]
            w1i = 0; w2i = 0
            for (a0, n) in oblocks:
                for fb in range(8):
                    w1 = w1b[w1i % 2]
                    load_cols(w1, w1_d, l, fb * 512, 512, "w1b%d" % (w1i % 2)); w1i += 1
                    for f4 in range(4):
                        f = fb * 4 + f4
                        pb = bank(f % 4)
                        for k in range(8):
                            K.mm(pb[:, 0:n], w1[:, k, f4 * 128:(f4 + 1) * 128], xnT[:, k, a0:a0 + n], start=(k == 0), stop=(k == 7))
                        hb = hr[f % 2]
                        K.act(hb[:, 0:n], pb[:, 0:n], AF.Relu)
                        K.tt("dve", h1[:, f, 0:n], hb[:, 0:n], hb[:, 0:n], ALU.mult)
                for cch in range(8):
                    w2 = w2b[w2i % 2]
                    K.dma(w2, w2_d[l].rearrange("(f p) n -> p f n", p=128)[:, :, cch * 128:(cch + 1) * 128],
                          lane="w2b%d" % (w2i % 2), eng="pool"); w2i += 1
                    pb = bank(4 + cch % 4)
                    for f in range(32):
                        K.mm(pb[:, 0:n], w2[:, f, :], h1[:, f, 0:n], start=(f == 0), stop=(f == 31))
                    resid_add(l, 1, pb, cch, (a0, n))
            AR.release(m_mlp)
            AR.release(m_layer)
            if l == 0:
                dump("xT_l0", xT[:, :, :])

        m0 = AR.mark()
        xo = [AR.f32(1024) for _ in range(2)]
        outs = []
        for t in range(16):
            a0 = CL + t * 128
            pb = ps[:, (t % 2) * 1024:(t % 2) * 1024 + 1024]
            for cch in range(8):
                K.tr(pb[:, cch * 128:(cch + 1) * 128], xT[:, cch, a0:a0 + 128], ident)
            K.copy("act" if t % 2 else "dve", xo[t % 2], pb)
            outs.append(K.dma(out_d[t * 128:(t + 1) * 128, :], xo[t % 2], lane="xo%d" % (t % 2), track_out=False))
        dbg_ops = [P.lane_last[k] for k in P.lane_last if str(k).startswith("dbg_")]
        P.add("sp", None, extra=outs + dbg_ops)
        print("ops:", len(P.ops), {e: len(v) for e, v in P.eng_ops.items()})
        P.finalize(st)
        print("waits:", P.n_wait)
    return nc


def _prep_common(inputs):
    f = lambda a: np.ascontiguousarray(np.asarray(a, dtype=np.float32))
    com = {
        "c_ctx": f(inputs["c_ctx"]), "w_mod": f(inputs["w_mod"]), "b_mod": f(inputs["b_mod"]),
        "g_attn": f(inputs["g_attn"]), "w_in": f(inputs["w_in"]), "gdn_conv_w": f(inputs["gdn_conv_w"]),
        "gdn_a_log": f(inputs["gdn_a_log"]).reshape(L, 16), "gdn_dt_bias": f(inputs["gdn_dt_bias"]).reshape(L, 16),
        "gdn_norm_g": f(inputs["gdn_norm_g"]), "ga_q_norm_g": f(inputs["ga_q_norm_g"]),
        "ga_k_norm_g": f(inputs["ga_k_norm_g"]), "wa_q_norm_g": f(inputs["wa_q_norm_g"]),
        "wa_k_norm_g": f(inputs["wa_k_norm_g"]), "wa_sink": f(inputs["wa_sink"]), "w_out": f(inputs["w_out"]),
        "g_mlp": f(inputs["g_mlp"]), "w_mlp_in": f(inputs["w_mlp_in"]), "w_mlp_out": f(inputs["w_mlp_out"]),
        "consts": make_consts(),
    }
    cosT, sinT = make_rope()
    com["ropec"] = cosT; com["ropes"] = sinT
    com["wmask"] = np.ascontiguousarray(make_wmask().transpose(1, 0, 2).reshape(128, 6 * 512))
    return com


def kernel(**inputs):
    nc = build(bass.Bass("TRN2", target_bir_lowering=False))
    com = _prep_common(inputs)
    x = np.asarray(inputs["x"], np.float32); c = np.asarray(inputs["c"], np.float32)
    ctx = np.asarray(inputs["ctx"], np.float32)
    in_maps = []
    for b in range(8):
        m = dict(com)
        m["x"] = np.ascontiguousarray(x[b]); m["ctx"] = np.ascontiguousarray(ctx[b]); m["c"] = np.ascontiguousarray(c[b])
        in_maps.append(m)
    res = run_bass_kernel_spmd(nc, in_maps, core_ids=list(range(8)))
    return np.stack([np.asarray(res.results[b]["out"], np.float32) for b in range(8)], axis=0)
```
